# Optimizing a Trainium2 kernel written in Bass

```python
import math
import jax, jax.numpy as jnp
from jax import lax
import numpy as np

D_MODEL = 1024
BATCH = 8
SEQ = 2048
DEPTH = 1
DEC_BATCH = 32
DEC_SEQ = 16
PAST_LEN = 1024

CHUNK = 64
HEAD_DIM = 64
N_Q_HEADS = (D_MODEL // 2) // HEAD_DIM
N_KV_HEADS = 2
GQA_GROUP = N_Q_HEADS // N_KV_HEADS
WINDOW = 128
WIN_CHUNKS = WINDOW // CHUNK
ATTN_WIDTH = N_Q_HEADS * HEAD_DIM
KV_WIDTH = N_KV_HEADS * HEAD_DIM
POOL_WINDOWS = (2, 4, 8, 16)
N_POOL_GROUPS = len(POOL_WINDOWS)
POOL_WIDTH = D_MODEL // 2
POOL_GROUP_WIDTH = POOL_WIDTH // N_POOL_GROUPS
POOL_PAD = max(POOL_WINDOWS) - 1
MIX_WIDTH = ATTN_WIDTH + POOL_WIDTH
IN_WIDTH = ATTN_WIDTH + 2 * KV_WIDTH + POOL_WIDTH
D_FF = ((8 * D_MODEL + 3 * 256 - 1) // (3 * 256)) * 256
PLE_DIM = 256
N_BUCKETS = 32
MAX_DISTANCE = 128
RMS_EPS = 1e-6
MASK_VALUE = -1e30

kernel_name = "hybrid_swa_sink_pool_stream_step"


def _rmsnorm(x, g):
    xf = x.astype(jnp.float32)
    y = xf * lax.rsqrt(jnp.mean(xf * xf, axis=-1, keepdims=True) + RMS_EPS)
    return (y * g.astype(jnp.float32)).astype(x.dtype)


def _t5_bucket(rel):
    half = N_BUCKETS // 2
    max_exact = half // 2
    ret = jnp.where(rel > 0, half, 0)
    n = jnp.abs(rel)
    nf = jnp.maximum(n, 1).astype(jnp.float32)
    large = max_exact + (jnp.log(nf / max_exact) / math.log(MAX_DISTANCE / max_exact)
                         * (half - max_exact)).astype(jnp.int32)
    large = jnp.minimum(large, half - 1)
    return ret + jnp.where(n < max_exact, n, large)


def _rel_bias(q_pos, k_pos, table):
    b = table[_t5_bucket(k_pos[None, :] - q_pos[:, None])].astype(jnp.float32)
    b = jnp.transpose(b, (2, 0, 1))
    return b.reshape(N_KV_HEADS, GQA_GROUP, q_pos.shape[0], k_pos.shape[0])


def _sink_attention(q, k, v, bias, valid, sinks):
    s = jnp.einsum('bnqkgd,bnskd->bnkgqs', q.astype(jnp.float32), k.astype(jnp.float32))
    s = s * (HEAD_DIM ** -0.5) + bias[None, None]
    s = jnp.where(valid[None, :, None, None, None, :], s, MASK_VALUE)
    sk = sinks.astype(jnp.float32).reshape(N_KV_HEADS, GQA_GROUP)[None, None, :, :, None, None]
    m = jnp.maximum(jnp.max(s, axis=-1, keepdims=True), sk)
    p = jnp.exp(s - m)
    w = p / (jnp.sum(p, axis=-1, keepdims=True) + jnp.exp(sk - m))
    return jnp.einsum('bnkgqs,bnskd->bnqkgd', w.astype(v.dtype), v)


def _prompt_attention(q, k, v, table, sinks):
    B, S, _ = q.shape
    nc = S // CHUNK
    nk = (WIN_CHUNKS + 1) * CHUNK
    qb = q.reshape(B, nc, CHUNK, N_KV_HEADS, GQA_GROUP, HEAD_DIM)
    pad = ((0, 0), (WIN_CHUNKS, 0), (0, 0), (0, 0), (0, 0))
    kp = jnp.pad(k.reshape(B, nc, CHUNK, N_KV_HEADS, HEAD_DIM), pad)
    vp = jnp.pad(v.reshape(B, nc, CHUNK, N_KV_HEADS, HEAD_DIM), pad)
    kb = jnp.concatenate([kp[:, w:w + nc] for w in range(WIN_CHUNKS + 1)], axis=2)
    vb = jnp.concatenate([vp[:, w:w + nc] for w in range(WIN_CHUNKS + 1)], axis=2)
    q_pos = jnp.arange(CHUNK, dtype=jnp.int32)
    k_pos = jnp.arange(nk, dtype=jnp.int32) - WIN_CHUNKS * CHUNK
    bias = _rel_bias(q_pos, k_pos, table)
    valid = (jnp.arange(nc, dtype=jnp.int32)[:, None] * CHUNK + k_pos[None, :]) >= 0
    o = _sink_attention(qb, kb, vb, bias, valid, sinks)
    return o.reshape(B, S, ATTN_WIDTH)


def _sample_attention(q, k, v, cache_k, cache_v, table, sinks):
    B, L, _ = q.shape
    n_past = cache_k.shape[1]
    qb = q.reshape(B, 1, L, N_KV_HEADS, GQA_GROUP, HEAD_DIM)
    kb = jnp.concatenate([cache_k, k.reshape(B, L, N_KV_HEADS, HEAD_DIM)], axis=1)[:, None]
    vb = jnp.concatenate([cache_v, v.reshape(B, L, N_KV_HEADS, HEAD_DIM)], axis=1)[:, None]
    q_pos = PAST_LEN + jnp.arange(L, dtype=jnp.int32)
    k_pos = PAST_LEN - n_past + jnp.arange(n_past + L, dtype=jnp.int32)
    bias = _rel_bias(q_pos, k_pos, table)
    valid = (k_pos >= 0)[None, :]
    o = _sink_attention(qb, kb, vb, bias, valid, sinks)
    return o.reshape(B, L, ATTN_WIDTH)


def _multiscale_pool(ext, pos, w_pool, scale):
    L = pos.shape[0]
    cs = jnp.pad(jnp.cumsum(ext.astype(jnp.float32), axis=1), ((0, 0), (1, 0), (0, 0)))
    u = ext[:, POOL_PAD:].astype(jnp.float32)
    outs = []
    for g, w in enumerate(POOL_WINDOWS):
        sl = slice(g * POOL_GROUP_WIDTH, (g + 1) * POOL_GROUP_WIDTH)
        total = cs[:, POOL_PAD + 1:POOL_PAD + 1 + L, sl] - cs[:, POOL_PAD + 1 - w:POOL_PAD + 1 - w + L, sl]
        count = jnp.minimum(pos + 1, w).astype(jnp.float32)[None, :, None]
        diff = (total / count - u[..., sl]).astype(ext.dtype)
        outs.append(jnp.einsum('bld,de->ble', diff, w_pool[g]))
    return jnp.concatenate(outs, axis=-1) * scale


def _split_proj(x, g_mix_pre, w_in):
    z = _rmsnorm(x, g_mix_pre) @ w_in
    q, k, v, u = jnp.split(z, [ATTN_WIDTH, ATTN_WIDTH + KV_WIDTH, ATTN_WIDTH + 2 * KV_WIDTH], axis=-1)
    return q, k, v, u


def _finish_layer(x, attn, pool, pe, w_out, g_mix_post, g_ffn_pre, w_ffn_gate, w_ffn_up,
                  w_ffn_down, g_ffn_post, w_ple, w_ple_gate):
    mix = jnp.concatenate([attn, pool.astype(attn.dtype)], axis=-1) @ w_out
    x = x + _rmsnorm(mix, g_mix_post)
    h = _rmsnorm(x, g_ffn_pre)
    f = (jax.nn.silu(h @ w_ffn_gate) * (h @ w_ffn_up)) @ w_ffn_down
    x = x + _rmsnorm(f, g_ffn_post)
    return x + jax.nn.sigmoid(x @ w_ple_gate) * (pe @ w_ple)


def setup_inputs(seed: int = 0) -> dict:
    key = jax.random.key(seed)
    ks = jax.random.split(key, 24)
    nrm = lambda k, shape, s=1.0: jax.random.normal(k, shape, jnp.float32) * s
    win_rows = min(WINDOW, PAST_LEN)
    return {
        "x_prompt": nrm(ks[0], (BATCH, SEQ, D_MODEL)),
        "x_sample": nrm(ks[1], (DEC_BATCH, DEC_SEQ, D_MODEL)),
        "cache_k": nrm(ks[2], (DEPTH, DEC_BATCH, win_rows, N_KV_HEADS, HEAD_DIM)),
        "cache_v": nrm(ks[3], (DEPTH, DEC_BATCH, win_rows, N_KV_HEADS, HEAD_DIM)),
        "state_pool": nrm(ks[4], (DEPTH, DEC_BATCH, POOL_PAD, POOL_WIDTH)),
        "p_prompt": nrm(ks[5], (DEPTH, BATCH, SEQ, PLE_DIM)),
        "p_sample": nrm(ks[6], (DEPTH, DEC_BATCH, DEC_SEQ, PLE_DIM)),
        "rel_bias_table": nrm(ks[7], (N_BUCKETS, N_Q_HEADS), 0.5),
        "g_mix_pre": 1.0 + nrm(ks[8], (DEPTH, D_MODEL), 0.05),
        "w_in": nrm(ks[9], (DEPTH, D_MODEL, IN_WIDTH), D_MODEL ** -0.5),
        "attn_sinks": nrm(ks[10], (DEPTH, N_Q_HEADS), 0.5),
        "w_pool": nrm(ks[11], (DEPTH, N_POOL_GROUPS, POOL_GROUP_WIDTH, POOL_GROUP_WIDTH), POOL_GROUP_WIDTH ** -0.5),
        "pool_scale": 1.0 + nrm(ks[12], (DEPTH, POOL_WIDTH), 0.05),
        "w_out": nrm(ks[13], (DEPTH, MIX_WIDTH, D_MODEL), MIX_WIDTH ** -0.5),
        "g_mix_post": 1.0 + nrm(ks[14], (DEPTH, D_MODEL), 0.05),
        "g_ffn_pre": 1.0 + nrm(ks[15], (DEPTH, D_MODEL), 0.05),
        "w_ffn_gate": nrm(ks[16], (DEPTH, D_MODEL, D_FF), D_MODEL ** -0.5),
        "w_ffn_up": nrm(ks[17], (DEPTH, D_MODEL, D_FF), D_MODEL ** -0.5),
        "w_ffn_down": nrm(ks[18], (DEPTH, D_FF, D_MODEL), D_FF ** -0.5),
        "g_ffn_post": 1.0 + nrm(ks[19], (DEPTH, D_MODEL), 0.05),
        "w_ple": nrm(ks[20], (DEPTH, PLE_DIM, D_MODEL), PLE_DIM ** -0.5),
        "w_ple_gate": nrm(ks[21], (DEPTH, D_MODEL, D_MODEL), D_MODEL ** -0.5),
    }


def reference(x_prompt, x_sample, cache_k, cache_v, state_pool, p_prompt, p_sample, rel_bias_table,
              g_mix_pre, w_in, attn_sinks, w_pool, pool_scale, w_out, g_mix_post, g_ffn_pre,
              w_ffn_gate, w_ffn_up, w_ffn_down, g_ffn_post, w_ple, w_ple_gate):
    xp, xs = x_prompt, x_sample
    S = xp.shape[1]
    L = xs.shape[1]
    win_prompt = min(WINDOW, S)
    pos_prompt = jnp.arange(S, dtype=jnp.int32)
    pos_sample = PAST_LEN + jnp.arange(L, dtype=jnp.int32)
    kp_list, vp_list, up_list, ks_list, vs_list, us_list = [], [], [], [], [], []
    for i in range(DEPTH):
        q, k, v, u = _split_proj(xp, g_mix_pre[i], w_in[i])
        a = _prompt_attention(q, k, v, rel_bias_table, attn_sinks[i])
        ext = jnp.pad(u, ((0, 0), (POOL_PAD, 0), (0, 0)))
        pl = _multiscale_pool(ext, pos_prompt, w_pool[i], pool_scale[i])
        kp_list.append(k.reshape(k.shape[0], S, N_KV_HEADS, HEAD_DIM)[:, S - win_prompt:])
        vp_list.append(v.reshape(v.shape[0], S, N_KV_HEADS, HEAD_DIM)[:, S - win_prompt:])
        up_list.append(u[:, S - POOL_PAD:])
        xp = _finish_layer(xp, a, pl, p_prompt[i], w_out[i], g_mix_post[i], g_ffn_pre[i],
                           w_ffn_gate[i], w_ffn_up[i], w_ffn_down[i], g_ffn_post[i], w_ple[i], w_ple_gate[i])
        q, k, v, u = _split_proj(xs, g_mix_pre[i], w_in[i])
        a = _sample_attention(q, k, v, cache_k[i], cache_v[i], rel_bias_table, attn_sinks[i])
        ext = jnp.concatenate([state_pool[i].astype(u.dtype), u], axis=1)
        pl = _multiscale_pool(ext, pos_sample, w_pool[i], pool_scale[i])
        ks_list.append(k.reshape(k.shape[0], L, N_KV_HEADS, HEAD_DIM))
        vs_list.append(v.reshape(v.shape[0], L, N_KV_HEADS, HEAD_DIM))
        us_list.append(ext[:, ext.shape[1] - POOL_PAD:])
        xs = _finish_layer(xs, a, pl, p_sample[i], w_out[i], g_mix_post[i], g_ffn_pre[i],
                           w_ffn_gate[i], w_ffn_up[i], w_ffn_down[i], g_ffn_post[i], w_ple[i], w_ple_gate[i])
    new_k_prompt = jnp.stack(kp_list)
    new_v_prompt = jnp.stack(vp_list)
    new_pool_prompt = jnp.stack(up_list)
    new_k_sample = jnp.stack(ks_list)
    new_v_sample = jnp.stack(vs_list)
    new_pool_sample = jnp.stack(us_list)
    return (xp, xs, new_k_prompt, new_v_prompt, new_pool_prompt, new_k_sample, new_v_sample, new_pool_sample)
```

```python
import math
import numpy as np
import concourse.bass as bass
import concourse.mybir as mybir
from concourse.bass_utils import run_bass_kernel_spmd

F32 = mybir.dt.float32
BF16 = mybir.dt.bfloat16
AF = mybir.ActivationFunctionType
ALU = mybir.AluOpType
AX = mybir.AxisListType

D = 1024
S_LEN = 2048
NQH = 8
HD = 64
DFF = 2816
NFC = DFF // 128
PLE = 256
NG = 4
GT = 512
NS = 64
EPS = 1e-6
POOL_W = (2, 4, 8, 16)
N_BUCKETS = 32
MAX_DISTANCE = 128


class Buf:
    __slots__ = ("name", "w", "r")

    def __init__(self, name):
        self.name = name
        self.w = None
        self.r = []


class Op:
    __slots__ = ("eng", "fns", "deps", "ords", "sig", "val", "dma", "idx", "cost", "marker", "fin", "pos", "prio")

    def __init__(self, eng, dma):
        self.eng = eng
        self.fns = []
        self.deps = {}
        self.ords = set()
        self.sig = False
        self.val = 0
        self.dma = dma
        self.idx = 0
        self.cost = 0.0
        self.marker = False
        self.fin = 0.0
        self.pos = 0
        self.prio = None


class Sched:
    ENGS = ("pe", "act", "dve", "pool", "sp")
    LAT = 150.0

    def __init__(self):
        self.ops = []
        self.dma_last = {}
        self.out_keys = set()
        self.cur_bar = {e: None for e in self.ENGS}
        self.since_bar = []
        self.grp = None
        self.q = None

    @staticmethod
    def _cost(eng, n, dma):
        if dma is not None:
            return 2200.0 + n / 200.0
        if eng == "pe":
            return 64.0 + max(n, 32) / 2.4
        if eng == "act":
            return 230.0 + n * 0.84
        if eng == "dve":
            return 130.0 + n * 1.05
        if eng == "pool":
            return 250.0 + n * 2.2
        return 100.0

    def begin(self, eng):
        op = Op(eng, None)
        op.idx = len(self.ops)
        self.ops.append(op)
        self.since_bar.append(op)
        if self.cur_bar[eng] is not None:
            op.ords.add(self.cur_bar[eng])
        self.grp = op

    def end(self):
        self.grp = None

    def add(self, eng, fn, r=(), w=(), dma=None, out=False, n=128, serial=False):
        if self.grp is not None:
            op = self.grp
            assert op.eng == eng and dma is None
        else:
            op = Op(eng, dma)
            op.idx = len(self.ops)
            self.ops.append(op)
            self.since_bar.append(op)
            if self.cur_bar[eng] is not None:
                op.ords.add(self.cur_bar[eng])
        op.fns.append(fn)
        op.cost += self._cost(eng, n, dma)
        for b in r:
            if b.w is not None and b.w is not op:
                op.deps[b.w] = "RAW"
        for b in w:
            if b.w is not None and b.w is not op and b.w not in op.deps:
                op.deps[b.w] = "WAW"
            for o in b.r:
                if o is not op and o not in op.deps:
                    op.deps[o] = "WAR"
        if dma is not None:
            prev = self.dma_last.get(dma)
            if prev is not None:
                if serial and prev not in op.deps:
                    op.deps[prev] = "RAW"
                op.ords.add(prev)
            self.dma_last[dma] = op
            if out:
                self.out_keys.add(dma)
        for b in r:
            if not b.r or b.r[-1] is not op:
                b.r.append(op)
        for b in w:
            b.w = op
            b.r = []
        return op

    def snapshot(self):
        return list(self.since_bar)

    def barrier(self, prior=None):
        if prior is None:
            prior = list(self.since_bar)
            self.since_bar = []
        else:
            ps = set(prior)
            self.since_bar = [o for o in self.since_bar if o not in ps]
        for e in self.ENGS:
            op = Op(e, None)
            op.marker = True
            op.idx = len(self.ops)
            self.ops.append(op)
            for o in prior:
                op.deps[o] = "BAR"
            if self.cur_bar[e] is not None:
                op.ords.add(self.cur_bar[e])
            self.cur_bar[e] = op

    def _keep(self, op, dep, kind):
        if dep.marker:
            return False
        if dep.dma is not None or op.dma is not None:
            return True
        if dep.eng != op.eng:
            return True
        if op.marker:
            return False
        if op.eng == "pe":
            return False
        return True

    def schedule(self):
        import heapq
        ops = self.ops
        succ = {}
        indeg = {}
        for op in ops:
            alld = set(op.deps.keys()) | op.ords
            indeg[op] = len(alld)
            for d in alld:
                succ.setdefault(d, []).append(op)
        free_at = {e: 0.0 for e in self.ENGS}
        pending = {e: [] for e in self.ENGS}
        avail = {e: [] for e in self.ENGS}
        q = {e: [] for e in self.ENGS}

        def ready_time(op):
            t = 0.0
            for d in list(op.deps.keys()) + list(op.ords):
                lat = 0.0 if (d.eng == op.eng and d.dma is None and op.eng == "pe") else self.LAT
                if d.marker:
                    lat = 0.0
                t = max(t, d.fin + lat)
            return t

        cp = {}
        outdeg = {op: len(succ.get(op, ())) for op in ops}
        preds = {op: list(set(op.deps.keys()) | op.ords) for op in ops}
        stack = [op for op in ops if outdeg[op] == 0]
        rtopo = []
        while stack:
            o_ = stack.pop()
            rtopo.append(o_)
            for p_ in preds[o_]:
                outdeg[p_] -= 1
                if outdeg[p_] == 0:
                    stack.append(p_)
        assert len(rtopo) == len(ops)
        for op in rtopo:
            m = 0.0
            for s_ in succ.get(op, ()):
                v = cp[s_] + (0.0 if (s_.marker or op.marker) else 0.5 * self.LAT)
                if v > m:
                    m = v
            cp[op] = (0.0 if op.marker else op.cost) + m
        for op in ops:
            if op.prio is not None and op.prio >= 3000.0:
                op.prio = 1e12 + op.prio
            else:
                op.prio = -cp[op]
        for op in ops:
            if indeg[op] == 0:
                heapq.heappush(pending[op.eng], (0.0, op.prio, op.idx, op))
        nleft = len(ops)
        while nleft:
            best = None
            for e in self.ENGS:
                pe_, av = pending[e], avail[e]
                while pe_ and pe_[0][0] <= free_at[e]:
                    rt, pr, idx, op = heapq.heappop(pe_)
                    heapq.heappush(av, (pr, idx, rt, op))
                if av:
                    pr, idx, rt, op = av[0]
                    start = max(free_at[e], rt)
                    cand = (start, pr, idx, e, True)
                elif pe_:
                    rt, pr, idx, op = pe_[0]
                    cand = (max(free_at[e], rt), pr, idx, e, False)
                else:
                    continue
                if best is None or cand < best:
                    best = cand
            assert best is not None, "scheduler deadlock (dependency cycle?)"
            start, pr, idx, e, from_av = best
            if from_av:
                pr, idx, rt, op = heapq.heappop(avail[e])
            else:
                rt, pr, idx, op = heapq.heappop(pending[e])
            if op.marker:
                op.fin = start
                free_at[e] = start
            elif op.dma is not None:
                op.fin = start + op.cost
                free_at[e] = start + (900.0 if e == "pool" else 80.0)
            else:
                op.fin = start + op.cost
                free_at[e] = op.fin
            op.pos = len(q[e])
            q[e].append(op)
            nleft -= 1
            for s_ in succ.get(op, ()):
                indeg[s_] -= 1
                if indeg[s_] == 0:
                    heapq.heappush(pending[s_.eng], (ready_time(s_), s_.prio, s_.idx, s_))
        self.q = q
        self.est_ns = max(free_at.values())

    def emit(self, nc, do_schedule=True):
        if do_schedule:
            self.schedule()
        else:
            self.q = {e: [o for o in self.ops if o.eng == e] for e in self.ENGS}
            for e in self.ENGS:
                for i, o in enumerate(self.q[e]):
                    o.pos = i
        for op in self.ops:
            if op.marker:
                last = {}
                for d in op.deps:
                    if d.marker:
                        continue
                    key = ("d", d.dma) if d.dma is not None else ("e", d.eng)
                    if key not in last or d.pos > last[key].pos:
                        last[key] = d
                op.deps = {d: "BAR" for d in last.values() if not (d.dma is None and d.eng == op.eng)}
        for op in self.ops:
            for dep, kind in op.deps.items():
                if self._keep(op, dep, kind):
                    dep.sig = True
        cnt = {e: 0 for e in self.ENGS}
        dcnt = {}
        for e in self.ENGS:
            for op in self.q[e]:
                if op.marker:
                    continue
                if op.dma is not None:
                    dcnt[op.dma] = dcnt.get(op.dma, 0) + 16
                    op.val = dcnt[op.dma]
                elif op.sig:
                    cnt[e] += 1
                    op.val = cnt[e]
        import contextlib
        with contextlib.ExitStack() as st:
            esem = {e: st.enter_context(nc.semaphore("sem_" + e)) for e in self.ENGS}
            dsem = {k: st.enter_context(nc.semaphore("dsem_" + str(k))) for k in dcnt}
            block = st.enter_context(nc.Block())

            def run(ename, eng):
                waited = {}
                for op in self.q[ename]:
                    need = {}
                    for dep, kind in op.deps.items():
                        if not self._keep(op, dep, kind):
                            continue
                        sem = dsem[dep.dma] if dep.dma is not None else esem[dep.eng]
                        if need.get(sem.name, (None, 0))[1] < dep.val:
                            need[sem.name] = (sem, dep.val)
                    for sname, (sem, val) in need.items():
                        if waited.get(sname, 0) >= val:
                            continue
                        eng.wait_ge(sem, val)
                        waited[sname] = val
                    ins = None
                    for fn in op.fns:
                        ins = fn(eng)
                    if ins is None:
                        continue
                    if op.dma is not None:
                        ins.then_inc(dsem[op.dma], 16)
                    elif op.sig:
                        ins.then_inc(esem[ename], 1)
                if ename == "sp":
                    for k in sorted(self.out_keys, key=str):
                        eng.wait_ge(dsem[k], dcnt[k])

            block.tensor(lambda e: run("pe", e))
            block.scalar(lambda e: run("act", e))
            block.vector(lambda e: run("dve", e))
            block.gpsimd(lambda e: run("pool", e))
            block.sync(lambda e: run("sp", e))


def _t5_bucket_np(rel):
    half = N_BUCKETS // 2
    max_exact = half // 2
    ret = np.where(rel > 0, half, 0)
    n = np.abs(rel)
    nf = np.maximum(n, 1).astype(np.float32)
    large = max_exact + (np.log(nf / np.float32(max_exact)) / np.float32(math.log(MAX_DISTANCE / max_exact))
                         * np.float32(half - max_exact)).astype(np.int32)
    large = np.minimum(large, half - 1)
    return ret + np.where(n < max_exact, n, large)


def _consts():
    rel = np.arange(256) - 191
    bk = _t5_bucket_np(rel)
    onehot = np.zeros((32, 256), np.float32)
    onehot[bk, np.arange(256)] = 1.0
    identf = np.eye(128, dtype=np.float32)
    j2 = np.zeros((128, 128), np.float32)
    for p in range(64):
        j2[p, 63 - p] = 1.0
        j2[64 + p, 127 - p] = 1.0
    invc = np.zeros((128, 4, 16), np.float32)
    for g, w in enumerate(POOL_W):
        for pos in range(16):
            invc[:, g, pos] = 1.0 / min(pos + 1, w)
    return {"c_onehot": onehot, "c_ident": identf, "c_j2": j2, "c_invc": invc.reshape(128, 64)}


def build_program(stage=3):
    nc = bass.Bass("TRN2", target_bir_lowering=False)
    S = Sched()

    def din(name, shape):
        return nc.dram_tensor(name, list(shape), F32, kind="ExternalInput").ap()

    def dout(name, shape):
        return nc.dram_tensor(name, list(shape), F32, kind="ExternalOutput").ap()

    x_p = din("x_p", (S_LEN, D)); x_s = din("x_s", (NS, D))
    ck = din("ck", (4, 128, 128)); cv = din("cv", (4, 128, 128)); spool = din("spool", (4, 15, 512))
    p_p = din("p_p", (S_LEN, PLE)); p_s = din("p_s", (NS, PLE))
    table = din("table", (32, 8))
    g_mix_pre = din("g_mix_pre", (1, D)); g_mix_post = din("g_mix_post", (1, D))
    g_ffn_pre = din("g_ffn_pre", (1, D)); g_ffn_post = din("g_ffn_post", (1, D))
    w_in = din("w_in", (D, 1280)); sinks = din("sinks", (1, 8))
    w_pool = din("w_pool", (4, 128, 128)); pool_scale = din("pool_scale", (4, 128))
    w_out = din("w_out", (D, D)); w_gate = din("w_gate", (D, DFF)); w_up = din("w_up", (D, DFF))
    w_down = din("w_down", (DFF, D)); w_ple = din("w_ple", (PLE, D)); w_pg = din("w_pg", (D, D))
    c_onehot = din("c_onehot", (32, 256)); c_ident = din("c_ident", (128, 128))
    c_j2 = din("c_j2", (128, 128)); c_invc = din("c_invc", (128, 64))

    y_p = dout("y_p", (S_LEN, D)); y_s = dout("y_s", (NS, D))
    nk_p = dout("nk_p", (128, 128)); nv_p = dout("nv_p", (128, 128)); npool_p = dout("npool_p", (15, 512))
    nk_s = dout("nk_s", (NS, 128)); nv_s = dout("nv_s", (NS, 128)); npool_s = dout("npool_s", (4, 15, 512))
    rb_dram = nc.dram_tensor("rb_scratch", [8, 256], F32).ap()
    sc_g = nc.dram_tensor("sc_gate", [11, 128, 2048], BF16).ap()
    sc_u = nc.dram_tensor("sc_up", [11, 128, 2048], BF16).ap()
    sc_d = nc.dram_tensor("sc_down", [NFC, 128, 1024], BF16).ap()

    sb = nc.alloc_sbuf_tensor
    wA = sb("wA", [128, 8, 1408], BF16)
    wo = sb("wo", [128, 8, 1024], BF16)
    wpl = sb("wpl", [128, 4, 128], BF16)
    wpg = sb("wpg", [128, 8, 1024], BF16)
    wpe = sb("wpe", [128, 2, 1024], BF16)
    gb = {i: sb("gb%d" % i, [128, 1024], F32) for i in (1, 3)}
    gcol = {0: sb("gcol0", [128, 8], F32), 2: sb("gcol2", [128, 8], F32)}
    xs = [sb("xs%d" % i, [128, 1024], F32) for i in range(7)]
    hT = sb("hT", [128, 8, 576], BF16)
    actT = sb("actT", [128, NFC, 576], BF16)
    wd = sb("wd", [128, NFC, 1024], BF16)
    ring = [(sb("rg%d" % i, [128, 8, 256], BF16), sb("ru%d" % i, [128, 8, 256], BF16)) for i in range(2)]
    kT = sb("kT", [128, 128 + 512], BF16)
    VdA = sb("VdA", [128, 5, 256], BF16)
    VdB = sb("VdB", [128, 5, 256], BF16)
    bias = sb("bias", [128, 4, 192], F32)
    ident = sb("ident", [128, 128], BF16)
    identf = sb("identf", [128, 128], F32)
    sinkc = sb("sinkc", [128, 4], F32)
    pscol = sb("pscol", [128, 4], F32)
    invc = sb("invc", [128, 64], F32)
    ucarry = sb("ucarry", [128, 4, 16], F32)
    stats = sb("stats", [128, 512], F32)
    sg = [sb("sg%d" % i, [128, 512], F32) for i in range(2)]
    pbf_ = [sb("pbf%d" % i, [128, 256], BF16) for i in range(2)]
    pT_ = [sb("pT%d" % i, [128, 256], BF16) for i in range(2)]

    regions = [wd[:].rearrange("p a b -> p (a b)"), actT[:].rearrange("p a b -> p (a b)"),
               ring[0][0][:].rearrange("p a b -> p (a b)"), ring[0][1][:].rearrange("p a b -> p (a b)"),
               ring[1][0][:].rearrange("p a b -> p (a b)"), ring[1][1][:].rearrange("p a b -> p (a b)")]
    rsize = [NFC * 1024, NFC * 576, 2048, 2048, 2048, 2048]
    roff = [0] * len(regions)

    def new_pass():
        for i in range(len(roff)):
            roff[i] = 0

    def carve(nbytes, dtype):
        n16 = (nbytes + 63) // 64 * 32
        for ri in range(len(regions)):
            if roff[ri] + n16 <= rsize[ri]:
                a = roff[ri]
                roff[ri] += n16
                v = regions[ri][:, a:a + nbytes // 2]
                if dtype == F32:
                    v = v.bitcast(F32)
                return v
        raise AssertionError("carve: out of transient space")

    roff[0] = rsize[0]; roff[1] = rsize[1]
    Hk = carve(192 * 4, F32)
    rsb = carve(256 * 4, F32)
    tb32 = carve(8 * 4, F32)
    oh = carve(256 * 4, F32)
    j2 = carve(128 * 4, F32)
    gtmp = carve(128 * 4, F32)
    gtmp2 = carve(128 * 4, F32)
    stqa = carve(8 * 256 * 2, BF16)
    stqb = carve(8 * 256 * 2, BF16)
    stv = carve(8 * 128 * 2, BF16)
    new_pass()
    for _ri in (0, 1):
        roff[_ri] = rsize[_ri]
    tmpfb = [carve(4096, F32) for _ in range(2)]
    junkb = carve(2048, BF16)
    new_pass()
    for _ri in (0, 1):
        roff[_ri] = rsize[_ri]
    tmpf3_ = [carve(4096, F32) for _ in range(2)]
    x2b_ = [carve(2048, BF16) for _ in range(2)]
    x2T_ = [carve(2048, BF16) for _ in range(2)]
    new_pass()
    for _ri in (2, 3, 4, 5):
        roff[_ri] = rsize[_ri]
    hb = [carve(2048, BF16) for _ in range(2)]
    junks = [carve(2048, BF16) for _ in range(2)]
    junk = junks[0]
    qT2 = carve(12 * 4 * 64 * 2, BF16)
    uT = carve(4 * 528 * 4, F32)
    tmpf = [uT[:, 0:1024], uT[:, 1024:2048]]
    uTs = carve(4 * 4 * 32 * 4, F32)
    pa = carve(528 * 4, F32); pb_ = carve(528 * 4, F32)
    dT = carve(4 * 576 * 2, BF16)
    aT = carve(4 * 576 * 2, BF16)
    plT = carve(4 * 576 * 2, BF16)
    Sb = [carve(2 * 193 * 4, F32) for _ in range(2)]
    Pf = [carve(2 * 193 * 4, F32) for _ in range(2)]
    Pn = [carve(2 * 192 * 2, BF16) for _ in range(2)]
    PnT = [carve(512 * 2, BF16) for _ in range(2)]
    ksT = carve(4 * 144 * 2, BF16)
    ckb = carve(4 * 128 * 2, BF16)
    Vsc = carve(4 * 256 * 2, BF16)
    Vsn = carve(4 * 256 * 2, BF16)
    stg = carve(896 * 4, F32)
    spx = carve(512 * 4, F32)

    PSB = [nc.alloc_psum_tensor("PS%d" % i, [128, 1024], BF16) for i in range(8)]
    PF = {i: PSB[i][:].bitcast(F32) for i in range(8)}
    PB0 = PSB[0]
    PB5 = PSB[7]

    B = {}

    def buf(name):
        if name not in B:
            B[name] = Buf(name)
        return B[name]

    bPF = {i: buf("PS%d" % i) for i in range(8)}
    bPB0, bPB5 = bPF[0], bPF[7]
    stat_col = [0]
    NSG = 64
    stat_bufs = [Buf("st%d" % i) for i in range(NSG)]

    def new_stat_group():
        gi_ = stat_col[0] % NSG
        stat_col[0] += 1
        return gi_ * 8, stat_bufs[gi_]

    def rstd_from_ms(c, rows, bst):
        ms = stats[:, c:c + 1]; t = stats[:, c + 2:c + 3]; l = stats[:, c + 3:c + 4]; rstd = stats[:, c + 4:c + 5]
        S.add("dve", lambda e: e.tensor_scalar(out=t[0:rows, :], in0=ms[0:rows, :], scalar1=EPS, scalar2=None, op0=ALU.add),
              r=[bst], w=[bst], n=1)
        S.add("act", lambda e: e.activation(out=l[0:rows, :], in_=t[0:rows, :], func=AF.Ln), r=[bst], w=[bst], n=1)
        S.add("act", lambda e: e.activation(out=rstd[0:rows, :], in_=l[0:rows, :], func=AF.Exp, scale=-0.5), r=[bst], w=[bst], n=1)
        return rstd, bst

    S.add("pool", lambda e: e.memset(ucarry[:].rearrange("p a b -> p (a b)"), 0.0), w=[buf("ucarry")])
    S.add("pool", lambda e: e.memset(VdA[:, 0, :], 0.0), w=[buf("VdA0")])
    S.add("pool", lambda e: e.memset(kT[:, 0:128], 0.0), w=[buf("kTc")])
    S.add("sp", lambda e: e.dma_start(out=identf[:], in_=c_ident[:, :]), w=[buf("identf")], dma="identf")
    S.add("dve", lambda e: e.tensor_copy(out=ident[:], in_=identf[:]), r=[buf("identf")], w=[buf("ident")])
    S.add("sp", lambda e: e.dma_start(out=invc[:], in_=c_invc[:, :]), w=[buf("invc")], dma="invc")
    for i, gsrc in ((1, g_mix_post), (3, g_ffn_post)):
        S.add("sp", lambda e, i=i, gsrc=gsrc: e.dma_start(out=gb[i][:], in_=bass.AP(tensor=gsrc.tensor, offset=0, ap=[[0, 128], [1, 1024]])),
              w=[buf("gb%d" % i)], dma="gb%d" % i)
    for i, gsrc in ((0, g_mix_pre), (2, g_ffn_pre)):
        S.add("sp", lambda e, gsrc=gsrc: e.dma_start(out=gtmp2[0:8, 0:128], in_=bass.AP(tensor=gsrc.tensor, offset=0, ap=[[128, 8], [1, 128]])),
              w=[buf("gtmp2")], dma="gtmp2")
        S.add("pe", lambda e: e.matmul(PF[4][:, 0:8], gtmp2[0:8, 0:128], identf[0:8, 0:8], start=True, stop=True),
              r=[buf("gtmp2"), buf("identf")], w=[bPF[4]])
        S.add("dve", lambda e, i=i: e.tensor_copy(out=gcol[i][:], in_=PF[4][:, 0:8]), w=[bPF[4], buf("gcol%d" % i)])
    for pi in range(4):
        for hh in range(2):
            h = 2 * pi + hh
            S.add("sp", lambda e, pi=pi, hh=hh, h=h: e.dma_start(
                out=sinkc[hh * 64:(hh + 1) * 64, pi:pi + 1], in_=bass.AP(tensor=sinks.tensor, offset=h, ap=[[0, 64], [1, 1]])),
                w=[buf("sinkc")], dma="sinkc")
    S.add("sp", lambda e: e.dma_start(out=gtmp[0:4, 0:128], in_=pool_scale[:, :]), w=[buf("gtmp")], dma="gtmp")
    S.add("pe", lambda e: e.matmul(PF[1][:, 0:4], gtmp[0:4, 0:128], identf[0:4, 0:4], start=True, stop=True),
          r=[buf("gtmp"), buf("identf")], w=[bPF[1]])
    S.add("dve", lambda e: e.tensor_copy(out=pscol[:], in_=PF[1][:, 0:4]), w=[bPF[1], buf("pscol")])

    def wload(dst_ap, src_ap, key, bname):
        S.add("pool", lambda e: e.dma_start(out=dst_ap, in_=src_ap), w=[buf(bname)], dma=key)

    w_in_v = w_in.rearrange("(k p) c -> p k c", p=128)
    stqa3 = stqa.rearrange("p (k c) -> p k c", k=8)
    stqb3 = stqb.rearrange("p (k c) -> p k c", k=8)
    stv3 = stv.rearrange("p (k c) -> p k c", k=8)
    wload(stqa3, w_in_v[:, :, 0:256], "stqa", "stqa")
    wload(stqb3, w_in_v[:, :, 256:512], "stqb", "stqb")
    wload(wA[:, :, 512:640], w_in_v[:, :, 512:640], "wAk", "wAk")
    wload(stv3, w_in_v[:, :, 640:768], "stv", "stv")
    wload(wA[:, :, 896:1408], w_in_v[:, :, 768:1280], "wAu", "wAu")
    S.add("dve", lambda e: e.tensor_copy(out=wA[:, :, 0:512].rearrange("p k (j c) -> p k j c", j=4)[:, :, :, 0:64],
                                         in_=stqa3.rearrange("p k (j c) -> p k j c", j=4)), r=[buf("stqa")], w=[buf("wAq")], n=2048)
    S.add("pool", lambda e: e.tensor_copy(out=wA[:, :, 0:512].rearrange("p k (j c) -> p k j c", j=4)[:, :, :, 64:128],
                                          in_=stqb3.rearrange("p k (j c) -> p k j c", j=4)), r=[buf("stqb")], w=[buf("wAq")], n=2048)
    for kv in range(2):
        for dup in range(2):
            c0 = 640 + kv * 128 + dup * 64
            eng = "dve" if dup == 0 else "pool"
            S.add(eng, lambda e, kv=kv, c0=c0: e.tensor_copy(out=wA[:, :, c0:c0 + 64], in_=stv3[:, :, kv * 64:(kv + 1) * 64]),
                  r=[buf("stv")], w=[buf("wAv")], n=512)
    wload(wpl[:], w_pool.rearrange("g d e -> d g e"), "wpl", "wpl")
    wload(wo[:], w_out.rearrange("(k p) c -> p k c", p=128), "wo", "wo")
    wgv = w_gate.rearrange("(k p) c -> p k c", p=128)
    wuv = w_up.rearrange("(k p) c -> p k c", p=128)
    conv_ops = []
    for blk in range(11):
        conv_ops.append(S.add("pool", lambda e, blk=blk: e.dma_start(out=sc_g[blk].rearrange("p (k c) -> p k c", k=8),
                                                                     in_=wgv[:, :, blk * 256:(blk + 1) * 256]),
                              w=[buf("scg")], dma="cvg", n=1048576))
        conv_ops.append(S.add("pool", lambda e, blk=blk: e.dma_start(out=sc_u[blk].rearrange("p (k c) -> p k c", k=8),
                                                                     in_=wuv[:, :, blk * 256:(blk + 1) * 256]),
                              w=[buf("scu")], dma="cvu", n=1048576))
    for q4 in range(2):
        conv_ops.append(S.add("pool", lambda e, q4=q4: e.dma_start(out=sc_d[q4 * 11:(q4 + 1) * 11].rearrange("c p n -> (c p) n"),
                                                                   in_=w_down[q4 * 1408:(q4 + 1) * 1408, :]),
                              w=[buf("scd")], dma="cvd", n=5767168))

    S.add("sp", lambda e: e.dma_start(out=tb32[0:32, 0:8], in_=table[:, :]), w=[buf("tb32")], dma="tb32")
    S.add("sp", lambda e: e.dma_start(out=oh[0:32, 0:256], in_=c_onehot[:, :]), w=[buf("oh")], dma="oh")
    S.add("sp", lambda e: e.dma_start(out=j2[:], in_=c_j2[:, :]), w=[buf("j2")], dma="j2")
    S.add("pe", lambda e: e.matmul(PF[2][0:8, 0:256], tb32[0:32, 0:8], oh[0:32, 0:256], start=True, stop=True),
          r=[buf("tb32"), buf("oh")], w=[bPF[2]])
    S.add("dve", lambda e: e.tensor_copy(out=rsb[0:8, 0:256], in_=PF[2][0:8, 0:256]), w=[bPF[2], buf("rsb")])
    S.add("sp", lambda e: e.dma_start(out=rb_dram[:, :], in_=rsb[0:8, 0:256]), r=[buf("rsb")], w=[buf("rb_dram")], dma="rbw")
    Hks = [sg[0][:, 0:192], sg[0][:, 192:384], sg[1][:, 0:192], sg[1][:, 192:384]]
    for pi in range(4):
        for hh in range(2):
            h = 2 * pi + hh
            src = bass.AP(tensor=rb_dram.tensor, offset=h * 256, ap=[[1, 64], [1, 192]])
            S.add("sp", lambda e, hh=hh, src=src, pi=pi: e.dma_start(out=Hks[pi][hh * 64:(hh + 1) * 64, :], in_=src),
                  r=[buf("rb_dram")], w=[buf("Hk%d" % pi)], dma="Hk%d" % pi)
    for pi in range(4):
        pbk = 3 + (pi % 2)
        S.add("pe", lambda e, pi=pi, pbk=pbk: e.matmul(PF[pbk][:, 0:192], j2[:], Hks[pi], start=True, stop=True),
              r=[buf("j2"), buf("Hk%d" % pi)], w=[bPF[pbk]])
        S.add("dve", lambda e, pi=pi, pbk=pbk: e.tensor_copy(out=bias[:, pi, :], in_=PF[pbk][:, 0:192]),
              w=[bPF[pbk], buf("bias")])

    for ci, o in enumerate(conv_ops):
        o.prio = 3000.0 + ci
        for bn in ("bias", "wo", "wAu", "wAk", "stqa", "stqb", "stv", "wpl", "xs0", "xs1", "xs2", "xs3"):
            w_ = B[bn].w if bn in B else None
            if w_ is not None and w_ is not o:
                o.deps[w_] = "RAW"
    jsel = [0]

    def rms_stats(src_ap, rows, bsrc, eng_sq="act"):
        c, bst = new_stat_group()
        ms = stats[:, c:c + 1]
        jsel[0] ^= 1
        jk = junks[jsel[0]]; bjk = buf("junk%d" % jsel[0])
        S.add("act", lambda e: e.activation(out=jk[0:rows, :], in_=src_ap, func=AF.Square, scale=1.0 / 32.0,
                                            accum_out=ms[0:rows, :]),
              r=bsrc, w=[bjk, bst], n=1024)
        return rstd_from_ms(c, rows, bst)

    def transposes(src_tile, rows, bsrc, dst_fn, bdst, nk, psum=PB0, bps=None):
        bps = bps or bPB0
        S.begin("pe")
        for k in range(nk):
            S.add("pe", lambda e, k=k: e.transpose(psum[:, k * 128:k * 128 + rows], src_tile[0:rows, k * 128:(k + 1) * 128],
                                                   ident[0:rows, 0:rows]),
                  r=bsrc + [buf("ident")], w=[bps], n=rows)
        S.end()
        return bps

    def issue_x_loads(g, only=None):
        tl = [(i, 128) for i in range(4)] + ([(4, NS)] if g == NG - 1 else [])
        if only is not None:
            tl = [t_ for t_ in tl if t_[0] in only]
        for (i, rows) in tl:
            src = x_p[(4 * g + i) * 128:(4 * g + i + 1) * 128, :] if i < 4 else x_s[:, :]
            si = (4 * g + i) % 7
            S.add("sp", lambda e, si=si, rows=rows, src=src: e.dma_start(out=xs[si][0:rows, :], in_=src),
                  w=[buf("xs%d" % si)], dma="xs%d" % si, n=rows * 4096)

    hw = 0
    for g in range(NG):
        has_s = (g == NG - 1)
        xsl = (lambda i, g=g: (4 * g + i) % 7)
        T = GT + (NS if has_s else 0)
        halves = [(0, 320), (320, 576)] if has_s else [(0, 256), (256, 512)]
        tiles = [(i, 128) for i in range(4)] + ([(4, NS)] if has_s else [])
        gtile0 = 4 * g

        if has_s:
            S.add("pool", lambda e: e.memset(qT2[:, 8 * 256:12 * 256], 0.0), w=[buf("qT2s")])
        if g == 0:
            issue_x_loads(0)
        wgv = w_gate.rearrange("(k p) c -> p k c", p=128)
        wuv = w_up.rearrange("(k p) c -> p k c", p=128)

        def ring_load(blk):
            rg, ru = ring[blk % 2]
            rgf = rg[:].rearrange("p k c -> p (k c)")
            ruf = ru[:].rearrange("p k c -> p (k c)")
            bg, bu = buf("rg%d" % (blk % 2)), buf("ru%d" % (blk % 2))
            S.add("sp", lambda e, blk=blk, rgf=rgf: e.dma_start(out=rgf, in_=sc_g[blk, :, :]),
                  r=[buf("scg")], w=[bg], dma="rgh%d" % (blk % 2), n=524288)
            S.add("sp", lambda e, blk=blk, ruf=ruf: e.dma_start(out=ruf, in_=sc_u[blk, :, :]),
                  r=[buf("scu")], w=[bu], dma="ruh%d" % (blk % 2), n=524288)

        def wd_load(c):
            S.add("sp", lambda e, c=c: e.dma_start(out=wd[:, c, :], in_=sc_d[c, :, :]),
                  r=[buf("scd")], w=[buf("wd%d" % c)], dma="wdh%d" % (c % 4), n=262144, serial=True)

        if g > 0 and stage >= 2:
            ring_load(0)
            ring_load(1)
        for (i, rows) in tiles:
            bx = buf("xs%d" % xsl(i))
            rstd, brs = rms_stats(xs[xsl(i)][0:rows, :], rows, [bx])
            hbuf = hb[hw % 2]; bh = buf("hb%d" % (hw % 2)); hw += 1
            S.add("dve", lambda e, i=i, rows=rows, rstd=rstd, hbuf=hbuf, xi=xsl(i): e.tensor_scalar(
                out=hbuf[0:rows, :], in0=xs[xi][0:rows, :], scalar1=rstd[0:rows, :], scalar2=None, op0=ALU.mult),
                r=[bx, brs], w=[bh], n=512)
            tb = (0, 7)[i % 2]
            transposes(hbuf, rows, [bh], None, None, 8, psum=PSB[tb], bps=bPF[tb])
            c0 = i * 128
            S.add("dve", lambda e, c0=c0, rows=rows, tb=tb: e.tensor_tensor(
                out=hT[:, :, c0:c0 + rows], in0=PSB[tb][:].rearrange("p (k t) -> p k t", k=8)[:, :, 0:rows],
                in1=gcol[0][:].unsqueeze(2).to_broadcast([128, 8, rows]), op=ALU.mult),
                r=[buf("gcol0")], w=[bPF[tb], buf("hT%d" % i)], n=8 * rows)
        bhT_all = [buf("hT%d" % i) for (i, _) in tiles]

        qv = qT2.rearrange("p (c j q) -> p c j q", j=4, q=64)
        uv = uT.rearrange("p (g t) -> p g t", g=4)
        usv = uTs.rearrange("p (g b t) -> p g b t", g=4, b=4)
        ksv = ksT.rearrange("p (b t) -> p b t", b=4)
        S.add("dve", lambda e: e.tensor_copy(out=uv[:, :, 0:16], in_=ucarry[:]), r=[buf("ucarry")], w=[buf("uT")], n=64)
        def v_for_tile(i):
            if i < 4:
                S.begin("pe")
                for k in range(8):
                    S.add("pe", lambda e, i=i, k=k: e.matmul(PF[6][:, 0:256], hT[:, k, i * 128:(i + 1) * 128],
                                                             wA[:, k, 640:896], start=(k == 0), stop=(k == 7)),
                          r=[buf("wAv"), buf("hT%d" % i)], w=[bPF[6]], n=256)
                S.end()
                S.add("act", lambda e, i=i: e.activation(out=VdA[:, i + 1, :], in_=PF[6][:, 0:256], func=AF.Copy),
                      w=[bPF[6], buf("VdA%d" % (i + 1))], n=256)
                if i == 0:
                    S.add("sp", lambda e: e.dma_start(out=VdB[0:64, 0, :], in_=VdA[64:128, 0, :]),
                          r=[buf("VdA0")], w=[buf("VdBlo0")], dma="VdBlo0")
                t = i + 1
                S.add("sp", lambda e, t=t: e.dma_start(out=VdB[0:64, t, :], in_=VdA[64:128, t, :]),
                      r=[buf("VdA%d" % t)], w=[buf("VdBlo%d" % t)], dma="VdBlo%d" % t)
                S.add("sp", lambda e, t=t: e.dma_start(out=VdB[64:128, t - 1, :], in_=VdA[0:64, t, :]),
                      r=[buf("VdA%d" % t)], w=[buf("VdBhi%d" % (t - 1))], dma="VdBhi%d" % (t - 1))
            else:
                for b in range(4):
                    S.begin("pe")
                    for k in range(8):
                        S.add("pe", lambda e, b=b, k=k: e.matmul(PF[6][0:16, 0:256], hT[:, k, 512 + 16 * b:512 + 16 * b + 16],
                                                                 wA[:, k, 640:896], start=(k == 0), stop=(k == 7)),
                              r=[buf("wAv"), buf("hT4")], w=[bPF[6]], n=64)
                    S.end()
                    S.add("act", lambda e, b=b: e.activation(out=Vsn[0:16, b * 256:(b + 1) * 256], in_=PF[6][0:16, 0:256],
                                                             func=AF.Copy), w=[bPF[6], buf("Vsn")], n=256)

        half_first_idx = []
        for hi, (a, b_) in enumerate(halves):
            half_first_idx.append(len(S.ops))
            hT_need = [buf("hT%d" % t_) for t_ in range(a // 128, (b_ - 1) // 128 + 1)]
            for oc in range(9):
                pbank = PF[3 + (oc % 2)]; bp = bPF[3 + (oc % 2)]
                n = b_ - a
                S.begin("pe")
                for k in range(8):
                    S.add("pe", lambda e, oc=oc, k=k, a=a, b_=b_, n=n, pbank=pbank: e.matmul(
                        pbank[:, 0:n], wA[:, k, (oc if oc < 5 else oc + 2) * 128:((oc if oc < 5 else oc + 2) + 1) * 128],
                        hT[:, k, a:b_], start=(k == 0), stop=(k == 7)),
                        r=[buf("wAq"), buf("wAk"), buf("wAu")] + hT_need, w=[bp], n=n)
                S.end()
                npr = min(b_, GT) - a
                ncp = npr // 64
                cbase = a // 64
                if oc < 4:
                    S.add("act", lambda e, oc=oc, pbank=pbank, npr=npr, ncp=ncp, cbase=cbase: e.activation(
                        out=qv[:, cbase:cbase + ncp, oc, :], in_=pbank[:, 0:npr].rearrange("p (c q) -> p c q", q=64),
                        func=AF.Copy, scale=0.125), w=[bp, buf("qT2h%d" % hi)])
                    if has_s and hi == 1:
                        S.add("act", lambda e, oc=oc, pbank=pbank, npr=npr: e.activation(
                            out=qv[:, 8:12, oc, 0:16], in_=pbank[:, npr:npr + 64].rearrange("p (c q) -> p c q", q=16),
                            func=AF.Copy, scale=0.125), w=[bp, buf("qT2s")])
                elif oc == 4:
                    S.add("act", lambda e, pbank=pbank, npr=npr, a=a: e.activation(
                        out=kT[:, 128 + a:128 + a + npr], in_=pbank[:, 0:npr], func=AF.Copy), w=[bp, buf("kT%d" % hi)])
                    if has_s and hi == 1:
                        S.add("act", lambda e, pbank=pbank, npr=npr: e.activation(
                            out=ksv[:, :, 128:144], in_=pbank[:, npr:npr + 64].rearrange("p (b t) -> p b t", t=16),
                            func=AF.Copy), w=[bp, buf("ksT")])
                else:
                    gi = oc - 5
                    S.add("dve", lambda e, gi=gi, pbank=pbank, npr=npr, a=a: e.tensor_copy(
                        out=uv[:, gi, 16 + a:16 + a + npr], in_=pbank[:, 0:npr]), w=[bp, buf("uT")])
                    if has_s and hi == 1:
                        S.add("dve", lambda e, gi=gi, pbank=pbank, npr=npr: e.tensor_copy(
                            out=usv[:, gi, :, 16:32], in_=pbank[:, npr:npr + 64].rearrange("p (b t) -> p b t", t=16)),
                            w=[bp, buf("uTs")])
            for (i_, rows_) in tiles:
                if a <= i_ * 128 < b_:
                    v_for_tile(i_)


        out_tiles = []
        if g == NG - 1:
            out_tiles = [(3, 128, 3 * 128), (4, NS, 512)]
        for (i, rows, c0) in out_tiles:
            for (pa_, pb2, bank) in ((0, 512, 4), (512, 896, 6)):
                for k in range(8):
                    S.add("pe", lambda e, k=k, c0=c0, rows=rows, pa_=pa_, pb2=pb2, bank=bank: e.matmul(
                        PF[bank][0:rows, 0:pb2 - pa_], hT[:, k, c0:c0 + rows], wA[:, k, 512 + pa_:512 + pb2],
                        start=(k == 0), stop=(k == 7)), r=[buf("wAq"), buf("wAk"), buf("wAv"), buf("wAu"), buf("hT%d" % i)], w=[bPF[bank]])
                S.add("dve", lambda e, rows=rows, pa_=pa_, pb2=pb2, bank=bank: e.tensor_copy(
                    out=stg[0:rows, pa_:pb2], in_=PF[bank][0:rows, 0:pb2 - pa_]), w=[bPF[bank], buf("stg")])
            bs = buf("stg")
            if i == 3:
                S.add("sp", lambda e: e.dma_start(out=nk_p[:, :], in_=stg[:, 0:128]), r=[bs], dma="o_nkp", out=True)
                S.add("sp", lambda e: e.dma_start(out=nv_p[:, 0:64], in_=stg[:, 128:192]), r=[bs], dma="o_nvp", out=True)
                S.add("sp", lambda e: e.dma_start(out=nv_p[:, 64:128], in_=stg[:, 256:320]), r=[bs], dma="o_nvp", out=True)
                S.add("sp", lambda e: e.dma_start(out=npool_p[:, :], in_=stg[113:128, 384:896]), r=[bs], dma="o_npp", out=True)
            else:
                S.add("sp", lambda e: e.dma_start(out=nk_s[:, :], in_=stg[0:NS, 0:128]), r=[bs], dma="o_nks", out=True)
                S.add("sp", lambda e: e.dma_start(out=nv_s[:, 0:64], in_=stg[0:NS, 128:192]), r=[bs], dma="o_nvs", out=True)
                S.add("sp", lambda e: e.dma_start(out=nv_s[:, 64:128], in_=stg[0:NS, 256:320]), r=[bs], dma="o_nvs", out=True)
                for b in range(4):
                    S.add("sp", lambda e, b=b: e.dma_start(out=npool_s[b, :, :], in_=stg[16 * b + 1:16 * b + 16, 384:896]),
                          r=[bs], dma="o_nps", out=True)

        if has_s:
            for b in range(4):
                S.add("pool", lambda e, b=b: e.dma_start(out=ckb[:, b * 128:(b + 1) * 128], in_=ck[b, :, :]),
                      w=[buf("ckb")], dma="ckb")
                for kv in range(2):
                    for dup in range(2):
                        c0 = b * 256 + kv * 128 + dup * 64
                        S.add("pool", lambda e, b=b, kv=kv, c0=c0: e.dma_start(
                            out=Vsc[:, c0:c0 + 64], in_=cv[b, :, kv * 64:(kv + 1) * 64]), w=[buf("Vsc")], dma="Vsc")
            for b in range(4):
                S.add("pe", lambda e, b=b: e.transpose(PB5[:, b * 128:(b + 1) * 128], ckb[:, b * 128:(b + 1) * 128], ident[:]),
                      r=[buf("ckb"), buf("ident")], w=[bPB5])
            S.add("act", lambda e: e.activation(out=ksv[:, :, 0:128], in_=PB5[:, 0:512].rearrange("p (b t) -> p b t", b=4),
                                                func=AF.Copy), w=[bPB5, buf("ksT")])
            for b in range(4):
                S.add("sp", lambda e, b=b: e.dma_start(out=spx[0:15, :], in_=spool[b, :, :]), w=[buf("spx")], dma="spx")
                for gi in range(4):
                    S.add("pe", lambda e, gi=gi: e.matmul(PF[4][:, gi * 16:gi * 16 + 15], spx[0:15, gi * 128:(gi + 1) * 128],
                                                          identf[0:15, 0:15], start=True, stop=True),
                          r=[buf("spx"), buf("identf")], w=[bPF[4]])
                S.add("dve", lambda e, b=b: e.tensor_copy(
                    out=usv[:, :, b, 1:16], in_=PF[4][:, 0:64].rearrange("p (g t) -> p g t", t=16)[:, :, 0:15]),
                    w=[bPF[4], buf("uTs")])

        def pool_windows(src3, L, nb, gi, w, dst3, first16):
            pav = pa[:, 0:nb * (16 + L)].rearrange("p (b t) -> p b t", b=nb)
            pbv = pb_[:, 0:nb * (16 + L)].rearrange("p (b t) -> p b t", b=nb)
            cur = src3
            step = 1
            tmpsel = [pav, pbv]
            ti = 0
            rb_ = [buf("uT"), buf("uTs")]
            while step < w:
                nxt = tmpsel[ti]; ti ^= 1
                lo = 2 * step
                S.add("pool", lambda e, cur=cur, nxt=nxt, lo=lo, step=step, L=L: e.tensor_tensor(
                    out=nxt[:, :, lo:16 + L], in0=cur[:, :, lo:16 + L], in1=cur[:, :, lo - step:16 + L - step], op=ALU.add),
                    r=rb_ + [buf("ptmp")], w=[buf("ptmp")], n=nb * (16 + L) // 2)
                cur = nxt
                step *= 2
            tq = pbv if cur is pav else pav
            if first16:
                S.add("dve", lambda e, cur=cur, gi=gi, tq=tq: e.tensor_tensor(
                    out=tq[:, 0, 0:16], in0=cur[:, 0, 16:32], in1=invc[:, gi * 16:(gi + 1) * 16], op=ALU.mult),
                    r=[buf("ptmp"), buf("invc")], w=[buf("ptmp")], n=16)
            S.add("dve", lambda e, cur=cur, w=w, L=L: e.scalar_tensor_tensor(
                out=dst3[:, :, 0:L], in0=cur[:, :, 16:16 + L], scalar=1.0 / w, in1=src3[:, :, 16:16 + L],
                op0=ALU.mult, op1=ALU.subtract), r=rb_ + [buf("ptmp")], w=[buf("dT")], n=nb * L)
            if first16:
                S.add("dve", lambda e, tq=tq: e.tensor_tensor(
                    out=dst3[:, 0, 0:16], in0=tq[:, 0, 0:16], in1=src3[:, 0, 16:32], op=ALU.subtract),
                    r=[buf("ptmp"), buf("uT")], w=[buf("dT")], n=16)

        dv = dT.rearrange("p (g t) -> p g t", g=4)
        pool_first_idx = len(S.ops)
        for gi, w in enumerate(POOL_W):
            pool_windows(uv[:, gi:gi + 1, :], 512, 1, gi, w, dv[:, gi:gi + 1, 0:512], first16=(g == 0))
            if has_s:
                pool_windows(usv[:, gi, :, :], 16, 4, gi, w,
                             dv[:, gi, 512:576].rearrange("p (b t) -> p b t", b=4), first16=False)
        S.add("dve", lambda e: e.tensor_copy(out=ucarry[:], in_=uv[:, :, 512:528]), r=[buf("uT")], w=[buf("ucarry")], n=64)
        plv = plT.rearrange("p (g t) -> p g t", g=4)
        for gi in range(4):
            for hi, (a, b_) in enumerate(halves):
                n = b_ - a
                S.add("pe", lambda e, gi=gi, a=a, b_=b_, n=n, hi=hi: e.matmul(
                    PF[5 + hi][:, 0:n], wpl[:, gi, :], dv[:, gi, a:b_], start=True, stop=True),
                    r=[buf("wpl"), buf("dT")], w=[bPF[5 + hi]], n=n)
                S.add("act", lambda e, gi=gi, a=a, b_=b_, n=n, hi=hi: e.activation(
                    out=plv[:, gi, a:b_], in_=PF[5 + hi][:, 0:n], func=AF.Copy, scale=pscol[:, gi:gi + 1]),
                    r=[buf("pscol")], w=[bPF[5 + hi], buf("plT")], n=n)

        pool_last_idx = len(S.ops)
        av = aT.rearrange("p (j t) -> p j t", j=4)
        Sbv = [x.rearrange("p (a t) -> p a t", a=2) for x in Sb]
        Pfv = [x.rearrange("p (a t) -> p a t", a=2) for x in Pf]
        Pnv = [x.rearrange("p (a t) -> p a t", a=2) for x in Pn]
        PnTv = [x.rearrange("p (b a t) -> p b a t", b=2, a=2) for x in PnT]
        for kv in range(2):
            S.add("dve", lambda e, kv=kv: e.tensor_copy(out=Sbv[kv][:, :, 0:1],
                                                        in_=sinkc[:, 2 * kv:2 * kv + 2].rearrange("p (a o) -> p a o", o=1)),
                  r=[buf("sinkc")], w=[buf("Sb%d" % kv)], n=2)
        st_i = 0

        def s_pair(lhs_fn, rhs_ap, nkeys, bias_lo, kv, blocks, out_cols, nq, rbufs):
            nonlocal st_i
            sl = st_i % 2; st_i += 1
            sbank = 1 + sl; tbank = (0, 7)[sl]
            psS = PF[sbank][:, 0:384].rearrange("p (a t) -> p a t", a=2); bpsS = bPF[sbank]
            psT = PSB[tbank][:, 0:512].rearrange("p (b a t) -> p b a t", b=2, a=2); bpsT = bPF[tbank]
            psO = PF[tbank][:, 256:512].rearrange("p (a t) -> p a t", a=2); bpsO = bPF[tbank]
            kvp = slice(kv * 64, kv * 64 + 64)
            S.begin("pe")
            for pp in range(2):
                S.add("pe", lambda e, pp=pp: e.matmul(psS[:, pp, 0:nkeys], lhs_fn(kvp, pp), rhs_ap(kvp), start=True, stop=True),
                      r=rbufs, w=[bpsS], n=nkeys)
            S.end()
            bSb = buf("Sb%d" % kv); bPf = buf("Pf%d" % sl); bPn = buf("Pn%d" % sl); bPnT = buf("PnT%d" % sl)
            S.add("dve", lambda e: e.tensor_tensor(out=Sbv[kv][:, :, 1:1 + nkeys], in0=psS[:, :, 0:nkeys],
                                                   in1=bias[:, 2 * kv:2 * kv + 2, bias_lo:bias_lo + nkeys], op=ALU.add),
                  r=[buf("bias")], w=[bpsS, bSb], n=2 * nkeys)
            c0, bstp = new_stat_group()
            negm = stats[:, c0:c0 + 2]; rsum = stats[:, c0 + 2:c0 + 4]; rr = stats[:, c0 + 4:c0 + 6]
            bng, brsum, brr = bstp, bstp, bstp
            S.add("dve", lambda e: e.reduce_max(out=negm, in_=Sbv[kv][:, :, 0:1 + nkeys], axis=AX.X, negate=True),
                  r=[bSb], w=[bng], n=2 * nkeys)
            for pp in range(2):
                S.add("act", lambda e, pp=pp: e.activation(out=Pfv[sl][:, pp, 0:1 + nkeys], in_=Sbv[kv][:, pp, 0:1 + nkeys],
                                                           func=AF.Exp, bias=negm[:, pp:pp + 1], scale=1.0,
                                                           accum_out=rsum[:, pp:pp + 1]),
                      r=[bSb, bng], w=[bPf, brsum], n=nkeys)
            S.add("dve", lambda e: e.reciprocal(out=rr, in_=rsum), r=[brsum], w=[brr], n=2)
            S.add("pool", lambda e: e.tensor_tensor(out=Pnv[sl][:, :, 0:nkeys], in0=Pfv[sl][:, :, 1:1 + nkeys],
                                                    in1=rr.unsqueeze(2).to_broadcast([128, 2, nkeys]), op=ALU.mult),
                  r=[bPf, brr], w=[bPn], n=nkeys)
            S.begin("pe")
            off = 0
            for bi, (nk_, v_ap, vb) in enumerate(blocks):
                for pp in range(2):
                    S.add("pe", lambda e, bi=bi, pp=pp, off=off, nk_=nk_: e.transpose(
                        psT[0:nk_, bi, pp, :], Pnv[sl][:, pp, off:off + nk_], ident[:]),
                        r=[bPn, buf("ident")], w=[bpsT], n=128)
                off += nk_
            S.end()
            for bi, (nk_, v_ap, vb) in enumerate(blocks):
                if bi == 0:
                    S.add("act", lambda e, bi=bi, nk_=nk_: e.activation(out=PnTv[sl][0:nk_, bi, :, :], in_=psT[0:nk_, bi, :, :],
                                                                        func=AF.Copy), w=[bpsT, bPnT], n=256)
                else:
                    S.add("dve", lambda e, bi=bi, nk_=nk_: e.tensor_copy(out=PnTv[sl][0:nk_, bi, :, :], in_=psT[0:nk_, bi, :, :]),
                          w=[bpsT, bPnT], n=200)
            S.begin("pe")
            for pp in range(2):
                for bi, (nk_, v_ap, vb) in enumerate(blocks):
                    S.add("pe", lambda e, bi=bi, pp=pp, nk_=nk_, v_ap=v_ap: e.matmul(
                        psO[:, pp, :], v_ap, PnTv[sl][0:nk_, bi, pp, :],
                        start=(bi == 0), stop=(bi == len(blocks) - 1)), r=[bPnT] + vb, w=[bpsO], n=128)
            S.end()
            for hh in range(2):
                eng = "act" if hh == 0 else "dve"
                if eng == "act":
                    S.add("act", lambda e, hh=hh: e.activation(
                        out=av[hh * 64:(hh + 1) * 64, 2 * kv:2 * kv + 2, out_cols:out_cols + nq],
                        in_=psO[hh * 64:(hh + 1) * 64, :, hh * 64:hh * 64 + nq], func=AF.Copy), w=[bpsO, buf("aT%d" % (out_cols // 128))], n=2 * nq)
                else:
                    S.add("dve", lambda e, hh=hh: e.tensor_copy(
                        out=av[hh * 64:(hh + 1) * 64, 2 * kv:2 * kv + 2, out_cols:out_cols + nq],
                        in_=psO[hh * 64:(hh + 1) * 64, :, hh * 64:hh * 64 + nq]), w=[bpsO, buf("aT%d" % (out_cols // 128))], n=2 * nq)

        early_att_first = len(S.ops)
        early_att_last = early_att_first
        for c in range(8):
            gc = 8 * g + c
            t = c // 2
            if c * 64 == halves[0][1] or (c == 4 and halves[0][1] > 256):
                pass
            if (c + 1) * 64 <= 256:
                early_att_last = None
            elif early_att_last is None:
                early_att_last = len(S.ops)
            for kv in range(2):
                vs = slice(kv * 128, (kv + 1) * 128)
                if gc == 0:
                    kc0, nkeys, blo = 128, 64, 128
                    blocks = [(64, VdA[0:64, 1, vs], [buf("VdA1")])]
                elif gc == 1:
                    kc0, nkeys, blo = 128, 128, 64
                    blocks = [(128, VdA[:, 1, vs], [buf("VdA1")])]
                else:
                    kc0, nkeys, blo = 128 + (c - 2) * 64, 192, 0
                    if c % 2 == 0:
                        blocks = [(128, VdA[:, t, vs], [buf("VdA%d" % t)]),
                                  (64, VdA[0:64, t + 1, vs], [buf("VdA%d" % (t + 1))])]
                    else:
                        blocks = [(128, VdB[:, t, vs], [buf("VdBlo%d" % t), buf("VdBhi%d" % t)]),
                                  (64, VdB[0:64, t + 1, vs], [buf("VdBlo%d" % (t + 1))])]
                lhs_fn = (lambda kvp, pp, c=c: qT2[kvp, (c * 4 + 2 * pp) * 64:(c * 4 + 2 * pp + 2) * 64])
                rhs_fn = (lambda kvp, kc0=kc0, nkeys=nkeys: kT[kvp, kc0:kc0 + nkeys])
                hq = 0 if c * 64 < halves[0][1] else 1
                kb = set()
                for cc_ in range(max(c - 2, -2), c + 1):
                    kb.add("kTc" if cc_ < 0 else ("kT0" if cc_ * 64 < halves[0][1] else "kT1"))
                s_pair(lhs_fn, rhs_fn, nkeys, blo, kv, blocks, c * 64, 64, [buf("qT2h%d" % hq)] + [buf(x) for x in sorted(kb)])
        if has_s:
            for b in range(4):
                for kv in range(2):
                    vs = slice(b * 256 + kv * 128, b * 256 + (kv + 1) * 128)
                    blocks = [(128, Vsc[:, vs], [buf("Vsc")]), (16, Vsn[0:16, vs], [buf("Vsn")])]
                    lhs_fn = (lambda kvp, pp, b=b: qT2[kvp, ((8 + b) * 4 + 2 * pp) * 64:((8 + b) * 4 + 2 * pp + 2) * 64])
                    rhs_fn = (lambda kvp, b=b: ksT[kvp, b * 144:(b + 1) * 144])
                    s_pair(lhs_fn, rhs_fn, 144, 0, kv, blocks, 512 + 16 * b, 16, [buf("qT2s"), buf("ksT")])

        if early_att_last is not None and len(half_first_idx) > 1:
            n_e = max(early_att_last - early_att_first, 1)
            for j, o in enumerate(S.ops[early_att_first:early_att_last]):
                o.prio = half_first_idx[1] - 0.5 + 0.4 * j / n_e
        att_last_idx = len(S.ops)
        pool_ops = S.ops[pool_first_idx:pool_last_idx]
        for j, o in enumerate(pool_ops):
            o.prio = pool_last_idx + (j + 1) * (att_last_idx - pool_last_idx) * 0.4 / (len(pool_ops) + 1)
        if g < NG - 1:
            S.add("act", lambda e: e.activation(out=kT[:, 0:128], in_=kT[:, 512:640], func=AF.Copy),
                  r=[buf("kT1")], w=[buf("kTc")])
            S.add("act", lambda e: e.activation(out=VdA[:, 0, :], in_=VdA[:, 4, :], func=AF.Copy),
                  r=[buf("VdA4")], w=[buf("VdA0")])

        def post_norm_residual(i, rows, banks, gidx, tf, btf, jk, bjk):
            c, bms = new_stat_group()
            ms = stats[:, c:c + 1]; ms2 = stats[:, c + 1:c + 2]
            S.add("act", lambda e: e.activation(out=jk[0:rows, 0:512], in_=PF[banks[0]][0:rows, :], func=AF.Square,
                                                scale=1.0 / 32.0, accum_out=ms[0:rows, :]), w=[bPF[banks[0]], bjk, bms], n=512)
            S.add("act", lambda e: e.activation(out=jk[0:rows, 512:1024], in_=PF[banks[1]][0:rows, :], func=AF.Square,
                                                scale=1.0 / 32.0, accum_out=ms2[0:rows, :]), w=[bPF[banks[1]], bjk, bms], n=512)
            S.add("dve", lambda e: e.tensor_tensor(out=ms[0:rows, :], in0=ms[0:rows, :], in1=ms2[0:rows, :], op=ALU.add),
                  r=[bms], w=[bms], n=1)
            rstd, brs = rstd_from_ms(c, rows, bms)
            for hf in range(2):
                S.add("dve", lambda e, hf=hf: e.scalar_tensor_tensor(
                    out=tf[0:rows, hf * 512:(hf + 1) * 512], in0=PF[banks[hf]][0:rows, :], scalar=rstd[0:rows, :],
                    in1=gb[gidx][0:rows, hf * 512:(hf + 1) * 512], op0=ALU.mult, op1=ALU.mult),
                    r=[brs, buf("gb%d" % gidx)], w=[bPF[banks[hf]], btf], n=512)
            S.add("pool", lambda e, xi=xsl(i): e.tensor_tensor(out=xs[xi][0:rows, :], in0=xs[xi][0:rows, :], in1=tf[0:rows, :], op=ALU.add),
                  r=[btf], w=[buf("xs%d" % xsl(i))], n=1024)

        for ti, (i, rows) in enumerate(tiles):
            c0 = i * 128
            mb = (5, 6) if ti % 2 == 0 else (3, 4)
            S.begin("pe")
            for hf in range(2):
                for j in range(8):
                    src = av if j < 4 else plv
                    S.add("pe", lambda e, j=j, hf=hf, c0=c0, rows=rows, src=src, mb=mb: e.matmul(
                        PF[mb[hf]][0:rows, :], src[:, j % 4, c0:c0 + rows], wo[:, j, hf * 512:(hf + 1) * 512],
                        start=(j == 0), stop=(j == 7)), r=[buf("wo"), buf("aT%d" % i), buf("plT")], w=[bPF[mb[hf]]], n=512)
            S.end()
            post_norm_residual(i, rows, mb, 1, tmpf[ti % 2], buf("tmpf%d" % (ti % 2)), junks[ti % 2], buf("junk%d" % (ti % 2)))

        for (i, rows) in tiles:
            bx = buf("xs%d" % xsl(i))
            rstd, brs = rms_stats(xs[xsl(i)][0:rows, :], rows, [bx])
            hbuf = hb[hw % 2]; bh = buf("hb%d" % (hw % 2)); hw += 1
            S.add("dve", lambda e, i=i, rows=rows, rstd=rstd, hbuf=hbuf, xi=xsl(i): e.tensor_scalar(
                out=hbuf[0:rows, :], in0=xs[xi][0:rows, :], scalar1=rstd[0:rows, :], scalar2=None, op0=ALU.mult),
                r=[bx, brs], w=[bh], n=512)
            tb = (0, 7)[i % 2]
            transposes(hbuf, rows, [bh], None, None, 8, psum=PSB[tb], bps=bPF[tb])
            c0 = i * 128
            S.add("dve", lambda e, c0=c0, rows=rows, tb=tb: e.tensor_tensor(
                out=hT[:, :, c0:c0 + rows], in0=PSB[tb][:].rearrange("p (k t) -> p k t", k=8)[:, :, 0:rows],
                in1=gcol[2][:].unsqueeze(2).to_broadcast([128, 8, rows]), op=ALU.mult),
                r=[buf("gcol2")], w=[bPF[tb], buf("hT%d" % i)], n=8 * rows)

        if g == 0:
            wload(wpg[:], w_pg.rearrange("(k p) c -> p k c", p=128), "wpg", "wpg")
            wload(wpe[:], w_ple.rearrange("(k p) c -> p k c", p=128), "wpe", "wpe")
        if g + 1 < NG:
            issue_x_loads(g + 1, only=(0, 1, 2))
        def p2a_front(c):
            blk = c // 2
            cc = c % 2
            rg, ru = ring[blk % 2]
            for hi, (a, b_) in enumerate([(0, 512)] + ([(512, 576)] if has_s else [])):
                n = b_ - a
                gbank = (1, 2)[hi] if c % 2 == 0 else (5, 6)[hi]
                ubank = (3, 4)[hi] if c % 2 == 0 else (0, 7)[hi]
                if hi == 0 and c < 4:
                    cblocks = [(0, 384), (384, 512)]
                else:
                    cblocks = [(a, b_)]
                for (wt, wbuf, bank) in ((rg, "rg%d" % (blk % 2), gbank), (ru, "ru%d" % (blk % 2), ubank)):
                    for (ca, cb) in cblocks:
                        need = [buf("hT%d" % t_) for t_ in range(ca // 128, min((cb - 1) // 128, 4) + 1)]
                        S.begin("pe")
                        for k in range(8):
                            S.add("pe", lambda e, k=k, ca=ca, cb=cb, a=a, wt=wt, cc=cc, bank=bank: e.matmul(
                                PF[bank][:, ca - a:cb - a], wt[:, k, cc * 128:(cc + 1) * 128], hT[:, k, ca:cb],
                                start=(k == 0), stop=(k == 7)), r=[buf(wbuf)] + need, w=[bPF[bank]], n=cb - ca)
                        S.end()
                sgi = hi
                S.add("act", lambda e, n=n, gbank=gbank, sgi=sgi: e.activation(out=sg[sgi][:, 0:n], in_=PF[gbank][:, 0:n],
                                                                               func=AF.Silu), w=[bPF[gbank], buf("sg%d" % sgi)], n=n)

        def p2a_back(c):
            for hi, (a, b_) in enumerate([(0, 512)] + ([(512, 576)] if has_s else [])):
                n = b_ - a
                ubank = (3, 4)[hi] if c % 2 == 0 else (0, 7)[hi]
                sgi = hi
                S.add("dve", lambda e, c=c, a=a, b_=b_, n=n, ubank=ubank, sgi=sgi: e.tensor_tensor(
                    out=actT[:, c, a:b_], in0=PF[ubank][:, 0:n], in1=sg[sgi][:, 0:n], op=ALU.mult),
                    r=[buf("sg%d" % sgi)], w=[bPF[ubank], buf("actT%d" % c)], n=n)

        pre = 0
        if stage >= 2 and g > 0:
            for c in range(1):
                p2a_front(c)
            pre = 1
        S.barrier()

        if stage >= 2:
            if g == 0:
                ring_load(0)
                ring_load(1)
            for c in range(pre):
                p2a_back(c)
            for c in range(pre, NFC):
                blk = c // 2
                cc = c % 2
                p2a_front(c)
                p2a_back(c)
                if cc == 1:
                    if blk + 2 < NFC // 2:
                        ring_load(blk + 2)
                    wd_load(2 * blk)
                    wd_load(2 * blk + 1)

            def p_load(i, rows):
                pr = i % 2
                psrc = p_p[(gtile0 + i) * 128:(gtile0 + i + 1) * 128, :] if i < 4 else p_s[:, :]
                S.add("pool", lambda e, rows=rows, psrc=psrc, pr=pr: e.dma_start(out=pbf_[pr][0:rows, :], in_=psrc),
                      w=[buf("pbf%d" % pr)], dma="pbf%d" % pr, n=rows * 1024)

            if stage >= 3:
                p_load(0, 128)
                p_load(1, 128)
            for ti, (i, rows) in enumerate(tiles):
                c0 = i * 128
                banks = (1, 2) if ti % 2 == 0 else (5, 6)
                S.begin("pe")
                for hf in range(2):
                    for c in range(NFC):
                        S.add("pe", lambda e, c=c, hf=hf, c0=c0, rows=rows, banks=banks: e.matmul(
                            PF[banks[hf]][0:rows, :], actT[:, c, c0:c0 + rows], wd[:, c, hf * 512:(hf + 1) * 512],
                            start=(c == 0), stop=(c == NFC - 1)), r=[buf("actT%d" % c), buf("wd%d" % c)], w=[bPF[banks[hf]]], n=512)
                S.end()
                post_norm_residual(i, rows, banks, 3, tmpfb[ti % 2], buf(("rg0", "ru0")[ti % 2]), junkb, buf("rg1"))

        p2b_done = S.snapshot()

        for (i, rows) in tiles:
            bx = buf("xs%d" % xsl(i))
            if stage >= 3:
                pr = i % 2
                x2b, x2T, pbf, pT, tf = x2b_[pr], x2T_[pr], pbf_[pr], pT_[pr], tmpf3_[pr]
                bx2b, bx2T, bpbf, bpT, btf = (buf("rg1"), buf("ru1"), buf("pbf%d" % pr), buf("pT%d" % pr),
                                              buf(("rg0", "ru0")[pr]))
                tbx, tbp = ((0, 7) if pr == 0 else (7, 0))
                if i >= 2:
                    p_load(i, rows)
                S.add("act", lambda e, i=i, rows=rows, x2b=x2b, xi=xsl(i): e.activation(out=x2b[0:rows, :], in_=xs[xi][0:rows, :], func=AF.Copy),
                      r=[bx], w=[bx2b], n=1024)
                transposes(x2b, rows, [bx2b], None, None, 8, psum=PSB[tbx], bps=bPF[tbx])
                S.add("act", lambda e, rows=rows, x2T=x2T, tbx=tbx: e.activation(
                    out=x2T.rearrange("p (k t) -> p k t", k=8)[:, :, 0:rows],
                    in_=PSB[tbx][:].rearrange("p (k t) -> p k t", k=8)[:, :, 0:rows], func=AF.Copy), w=[bPF[tbx], bx2T], n=8 * rows)
                transposes(pbf, rows, [bpbf], None, None, 2, psum=PSB[tbp], bps=bPF[tbp])
                S.add("dve", lambda e, rows=rows, pT=pT, tbp=tbp: e.tensor_copy(
                    out=pT.rearrange("p (k t) -> p k t", k=2)[:, :, 0:rows],
                    in_=PSB[tbp][:, 0:256].rearrange("p (k t) -> p k t", k=2)[:, :, 0:rows]), w=[bPF[tbp], bpT], n=2 * rows)
                x2Tv = x2T.rearrange("p (k t) -> p k t", k=8)
                pTv = pT.rearrange("p (k t) -> p k t", k=2)
                gbk = (1, 2) if pr == 0 else (5, 6)
                for hf in range(2):
                    S.begin("pe")
                    for k in range(8):
                        S.add("pe", lambda e, k=k, hf=hf, rows=rows, gbk=gbk, x2Tv=x2Tv: e.matmul(
                            PF[gbk[hf]][0:rows, :], x2Tv[:, k, 0:rows], wpg[:, k, hf * 512:(hf + 1) * 512],
                            start=(k == 0), stop=(k == 7)), r=[bx2T, buf("wpg")], w=[bPF[gbk[hf]]], n=512)
                    S.end()
                    S.begin("pe")
                    for k in range(2):
                        S.add("pe", lambda e, k=k, hf=hf, rows=rows, pTv=pTv: e.matmul(
                            PF[3 + hf][0:rows, :], pTv[:, k, 0:rows], wpe[:, k, hf * 512:(hf + 1) * 512],
                            start=(k == 0), stop=(k == 1)), r=[bpT, buf("wpe")], w=[bPF[3 + hf]], n=512)
                    S.end()
                for hf in range(2):
                    S.add("act", lambda e, hf=hf, rows=rows, gbk=gbk, tf=tf: e.activation(
                        out=tf[0:rows, hf * 512:(hf + 1) * 512], in_=PF[gbk[hf]][0:rows, :], func=AF.Sigmoid),
                        w=[bPF[gbk[hf]], btf], n=512)
                    S.add("dve", lambda e, hf=hf, rows=rows, tf=tf: e.tensor_tensor(
                        out=tf[0:rows, hf * 512:(hf + 1) * 512], in0=PF[3 + hf][0:rows, :], in1=tf[0:rows, hf * 512:(hf + 1) * 512],
                        op=ALU.mult), r=[btf], w=[bPF[3 + hf], btf], n=512)
                S.add("pool", lambda e, i=i, rows=rows, tf=tf, xi=xsl(i): e.tensor_tensor(out=xs[xi][0:rows, :], in0=xs[xi][0:rows, :],
                                                                               in1=tf[0:rows, :], op=ALU.add),
                      r=[btf], w=[bx], n=1024)
            dst = y_p[(gtile0 + i) * 128:(gtile0 + i + 1) * 128, :] if i < 4 else y_s[:, :]
            S.add("sp", lambda e, i=i, rows=rows, dst=dst, xi=xsl(i): e.dma_start(out=dst, in_=xs[xi][0:rows, :]),
                  r=[bx], dma="o_xs%d" % xsl(i), out=True)

        S.barrier(prior=p2b_done)
        if g + 1 < NG:
            issue_x_loads(g + 1, only=(3, 4))

    S.emit(nc)
    return nc


_PROG = None


def kernel(x_prompt, x_sample, cache_k, cache_v, state_pool, p_prompt, p_sample, rel_bias_table,
           g_mix_pre, w_in, attn_sinks, w_pool, pool_scale, w_out, g_mix_post, g_ffn_pre,
           w_ffn_gate, w_ffn_up, w_ffn_down, g_ffn_post, w_ple, w_ple_gate):
    global _PROG
    f = lambda a: np.ascontiguousarray(np.asarray(a, dtype=np.float32))
    x_prompt, x_sample = f(x_prompt), f(x_sample)
    consts = _consts()
    shared = {
        "table": f(rel_bias_table), "g_mix_pre": f(g_mix_pre), "g_mix_post": f(g_mix_post),
        "g_ffn_pre": f(g_ffn_pre), "g_ffn_post": f(g_ffn_post), "w_in": f(w_in)[0], "sinks": f(attn_sinks),
        "w_pool": f(w_pool)[0], "pool_scale": f(pool_scale).reshape(4, 128), "w_out": f(w_out)[0],
        "w_gate": f(w_ffn_gate)[0], "w_up": f(w_ffn_up)[0], "w_down": f(w_ffn_down)[0],
        "w_ple": f(w_ple)[0], "w_pg": f(w_ple_gate)[0],
    }
    shared.update(consts)
    ck, cv, sp = f(cache_k)[0], f(cache_v)[0], f(state_pool)[0]
    pp, ps = f(p_prompt)[0], f(p_sample)[0]
    in_maps = []
    for c in range(8):
        m = dict(shared)
        m["x_p"] = x_prompt[c]
        m["x_s"] = x_sample[4 * c:4 * c + 4].reshape(NS, D)
        m["ck"] = ck[4 * c:4 * c + 4].reshape(4, 128, 128)
        m["cv"] = cv[4 * c:4 * c + 4].reshape(4, 128, 128)
        m["spool"] = sp[4 * c:4 * c + 4]
        m["p_p"] = pp[c]
        m["p_s"] = ps[4 * c:4 * c + 4].reshape(NS, PLE)
        in_maps.append({k: np.ascontiguousarray(v) for k, v in m.items()})
    if _PROG is None:
        _PROG = build_program()
    res = run_bass_kernel_spmd(_PROG, in_maps, core_ids=list(range(8)))
    R = res.results
    y_prompt = np.stack([R[c]["y_p"] for c in range(8)]).astype(np.float32)
    y_sample = np.concatenate([R[c]["y_s"].reshape(4, 16, D) for c in range(8)]).astype(np.float32)
    nkp = np.stack([R[c]["nk_p"].reshape(128, 2, 64) for c in range(8)])[None].astype(np.float32)
    nvp = np.stack([R[c]["nv_p"].reshape(128, 2, 64) for c in range(8)])[None].astype(np.float32)
    npp = np.stack([R[c]["npool_p"] for c in range(8)])[None].astype(np.float32)
    nks = np.concatenate([R[c]["nk_s"].reshape(4, 16, 2, 64) for c in range(8)])[None].astype(np.float32)
    nvs = np.concatenate([R[c]["nv_s"].reshape(4, 16, 2, 64) for c in range(8)])[None].astype(np.float32)
    nps = np.concatenate([R[c]["npool_s"] for c in range(8)])[None].astype(np.float32)
    return (y_prompt, y_sample, nkp, nvp, npp, nks, nvs, nps)
```

```python
import math
import numpy as np
import concourse.bass as bass
import concourse.mybir as mybir
from concourse.bass_utils import run_bass_kernel_spmd

F32 = mybir.dt.float32
BF16 = mybir.dt.bfloat16
AF = mybir.ActivationFunctionType
ALU = mybir.AluOpType
AX = mybir.AxisListType

D = 1024
S_LEN = 2048
NQH = 8
HD = 64
DFF = 2816
NFC = DFF // 128
PLE = 256
NG = 4
GT = 512
NS = 64
EPS = 1e-6
POOL_W = (2, 4, 8, 16)
N_BUCKETS = 32
MAX_DISTANCE = 128


class Buf:
    __slots__ = ("name", "w", "r")

    def __init__(self, name):
        self.name = name
        self.w = None
        self.r = []


class Op:
    __slots__ = ("eng", "fns", "deps", "ords", "sig", "val", "dma", "idx", "cost", "marker", "fin", "pos", "prio")

    def __init__(self, eng, dma):
        self.eng = eng
        self.fns = []
        self.deps = {}
        self.ords = set()
        self.sig = False
        self.val = 0
        self.dma = dma
        self.idx = 0
        self.cost = 0.0
        self.marker = False
        self.fin = 0.0
        self.pos = 0
        self.prio = None


class Sched:
    ENGS = ("pe", "act", "dve", "pool", "sp")
    LAT = 150.0

    def __init__(self):
        self.ops = []
        self.dma_last = {}
        self.out_keys = set()
        self.cur_bar = {e: None for e in self.ENGS}
        self.since_bar = []
        self.grp = None
        self.q = None

    @staticmethod
    def _cost(eng, n, dma):
        if dma is not None:
            return 2200.0 + n / 200.0
        if eng == "pe":
            return 64.0 + max(n, 32) / 2.4
        if eng == "act":
            return 230.0 + n * 0.84
        if eng == "dve":
            return 130.0 + n * 1.05
        if eng == "pool":
            return 250.0 + n * 2.2
        return 100.0

    def begin(self, eng):
        op = Op(eng, None)
        op.idx = len(self.ops)
        self.ops.append(op)
        self.since_bar.append(op)
        if self.cur_bar[eng] is not None:
            op.ords.add(self.cur_bar[eng])
        self.grp = op

    def end(self):
        self.grp = None

    def add(self, eng, fn, r=(), w=(), dma=None, out=False, n=128, serial=False):
        if self.grp is not None:
            op = self.grp
            assert op.eng == eng and dma is None
        else:
            op = Op(eng, dma)
            op.idx = len(self.ops)
            self.ops.append(op)
            self.since_bar.append(op)
            if self.cur_bar[eng] is not None:
                op.ords.add(self.cur_bar[eng])
        op.fns.append(fn)
        op.cost += self._cost(eng, n, dma)
        for b in r:
            if b.w is not None and b.w is not op:
                op.deps[b.w] = "RAW"
        for b in w:
            if b.w is not None and b.w is not op and b.w not in op.deps:
                op.deps[b.w] = "WAW"
            for o in b.r:
                if o is not op and o not in op.deps:
                    op.deps[o] = "WAR"
        if dma is not None:
            prev = self.dma_last.get(dma)
            if prev is not None:
                if serial and prev not in op.deps:
                    op.deps[prev] = "RAW"
                op.ords.add(prev)
            self.dma_last[dma] = op
            if out:
                self.out_keys.add(dma)
        for b in r:
            if not b.r or b.r[-1] is not op:
                b.r.append(op)
        for b in w:
            b.w = op
            b.r = []
        return op

    def snapshot(self):
        return list(self.since_bar)

    def barrier(self, prior=None):
        if prior is None:
            prior = list(self.since_bar)
            self.since_bar = []
        else:
            ps = set(prior)
            self.since_bar = [o for o in self.since_bar if o not in ps]
        for e in self.ENGS:
            op = Op(e, None)
            op.marker = True
            op.idx = len(self.ops)
            self.ops.append(op)
            for o in prior:
                op.deps[o] = "BAR"
            if self.cur_bar[e] is not None:
                op.ords.add(self.cur_bar[e])
            self.cur_bar[e] = op

    def _keep(self, op, dep, kind):
        if dep.marker:
            return False
        if dep.dma is not None or op.dma is not None:
            return True
        if dep.eng != op.eng:
            return True
        if op.marker:
            return False
        if op.eng == "pe":
            return False
        return True

    def schedule(self):
        import heapq
        ops = self.ops
        succ = {}
        indeg = {}
        for op in ops:
            alld = set(op.deps.keys()) | op.ords
            indeg[op] = len(alld)
            for d in alld:
                succ.setdefault(d, []).append(op)
        free_at = {e: 0.0 for e in self.ENGS}
        pending = {e: [] for e in self.ENGS}
        avail = {e: [] for e in self.ENGS}
        q = {e: [] for e in self.ENGS}

        def ready_time(op):
            t = 0.0
            for d in list(op.deps.keys()) + list(op.ords):
                lat = 0.0 if (d.eng == op.eng and d.dma is None and op.eng == "pe") else self.LAT
                if d.marker:
                    lat = 0.0
                t = max(t, d.fin + lat)
            return t

        cp = {}
        outdeg = {op: len(succ.get(op, ())) for op in ops}
        preds = {op: list(set(op.deps.keys()) | op.ords) for op in ops}
        stack = [op for op in ops if outdeg[op] == 0]
        rtopo = []
        while stack:
            o_ = stack.pop()
            rtopo.append(o_)
            for p_ in preds[o_]:
                outdeg[p_] -= 1
                if outdeg[p_] == 0:
                    stack.append(p_)
        assert len(rtopo) == len(ops)
        for op in rtopo:
            m = 0.0
            for s_ in succ.get(op, ()):
                v = cp[s_] + (0.0 if (s_.marker or op.marker) else self.LAT)
                if v > m:
                    m = v
            cp[op] = (0.0 if op.marker else op.cost) + m
        for op in ops:
            if op.prio is not None and op.prio >= 3000.0:
                op.prio = 1e12 + op.prio
            else:
                op.prio = -cp[op]
        for op in ops:
            if indeg[op] == 0:
                heapq.heappush(pending[op.eng], (0.0, op.prio, op.idx, op))
        nleft = len(ops)
        while nleft:
            best = None
            for e in self.ENGS:
                pe_, av = pending[e], avail[e]
                while pe_ and pe_[0][0] <= free_at[e]:
                    rt, pr, idx, op = heapq.heappop(pe_)
                    heapq.heappush(av, (pr, idx, rt, op))
                if av:
                    pr, idx, rt, op = av[0]
                    start = max(free_at[e], rt)
                    cand = (start, pr, idx, e, True)
                elif pe_:
                    rt, pr, idx, op = pe_[0]
                    cand = (max(free_at[e], rt), pr, idx, e, False)
                else:
                    continue
                if best is None or cand < best:
                    best = cand
            assert best is not None, "scheduler deadlock (dependency cycle?)"
            start, pr, idx, e, from_av = best
            if from_av:
                pr, idx, rt, op = heapq.heappop(avail[e])
            else:
                rt, pr, idx, op = heapq.heappop(pending[e])
            if op.marker:
                op.fin = start
                free_at[e] = start
            elif op.dma is not None:
                op.fin = start + op.cost
                free_at[e] = start + (900.0 if e == "pool" else 80.0)
            else:
                op.fin = start + op.cost
                free_at[e] = op.fin
            op.pos = len(q[e])
            q[e].append(op)
            nleft -= 1
            for s_ in succ.get(op, ()):
                indeg[s_] -= 1
                if indeg[s_] == 0:
                    heapq.heappush(pending[s_.eng], (ready_time(s_), s_.prio, s_.idx, s_))
        self.q = q
        self.est_ns = max(free_at.values())

    def emit(self, nc, do_schedule=True):
        if do_schedule:
            self.schedule()
        else:
            self.q = {e: [o for o in self.ops if o.eng == e] for e in self.ENGS}
            for e in self.ENGS:
                for i, o in enumerate(self.q[e]):
                    o.pos = i
        for op in self.ops:
            if op.marker:
                last = {}
                for d in op.deps:
                    if d.marker:
                        continue
                    key = ("d", d.dma) if d.dma is not None else ("e", d.eng)
                    if key not in last or d.pos > last[key].pos:
                        last[key] = d
                op.deps = {d: "BAR" for d in last.values() if not (d.dma is None and d.eng == op.eng)}
        for op in self.ops:
            for dep, kind in op.deps.items():
                if self._keep(op, dep, kind):
                    dep.sig = True
        cnt = {e: 0 for e in self.ENGS}
        dcnt = {}
        for e in self.ENGS:
            for op in self.q[e]:
                if op.marker:
                    continue
                if op.dma is not None:
                    dcnt[op.dma] = dcnt.get(op.dma, 0) + 16
                    op.val = dcnt[op.dma]
                elif op.sig:
                    cnt[e] += 1
                    op.val = cnt[e]
        import contextlib
        with contextlib.ExitStack() as st:
            esem = {e: st.enter_context(nc.semaphore("sem_" + e)) for e in self.ENGS}
            dsem = {k: st.enter_context(nc.semaphore("dsem_" + str(k))) for k in dcnt}
            block = st.enter_context(nc.Block())

            def run(ename, eng):
                waited = {}
                for op in self.q[ename]:
                    need = {}
                    for dep, kind in op.deps.items():
                        if not self._keep(op, dep, kind):
                            continue
                        sem = dsem[dep.dma] if dep.dma is not None else esem[dep.eng]
                        if need.get(sem.name, (None, 0))[1] < dep.val:
                            need[sem.name] = (sem, dep.val)
                    for sname, (sem, val) in need.items():
                        if waited.get(sname, 0) >= val:
                            continue
                        eng.wait_ge(sem, val)
                        waited[sname] = val
                    ins = None
                    for fn in op.fns:
                        ins = fn(eng)
                    if ins is None:
                        continue
                    if op.dma is not None:
                        ins.then_inc(dsem[op.dma], 16)
                    elif op.sig:
                        ins.then_inc(esem[ename], 1)
                if ename == "sp":
                    for k in sorted(self.out_keys, key=str):
                        eng.wait_ge(dsem[k], dcnt[k])

            block.tensor(lambda e: run("pe", e))
            block.scalar(lambda e: run("act", e))
            block.vector(lambda e: run("dve", e))
            block.gpsimd(lambda e: run("pool", e))
            block.sync(lambda e: run("sp", e))


def _t5_bucket_np(rel):
    half = N_BUCKETS // 2
    max_exact = half // 2
    ret = np.where(rel > 0, half, 0)
    n = np.abs(rel)
    nf = np.maximum(n, 1).astype(np.float32)
    large = max_exact + (np.log(nf / np.float32(max_exact)) / np.float32(math.log(MAX_DISTANCE / max_exact))
                         * np.float32(half - max_exact)).astype(np.int32)
    large = np.minimum(large, half - 1)
    return ret + np.where(n < max_exact, n, large)


def _consts():
    rel = np.arange(256) - 191
    bk = _t5_bucket_np(rel)
    onehot = np.zeros((32, 256), np.float32)
    onehot[bk, np.arange(256)] = 1.0
    identf = np.eye(128, dtype=np.float32)
    j2 = np.zeros((128, 128), np.float32)
    for p in range(64):
        j2[p, 63 - p] = 1.0
        j2[64 + p, 127 - p] = 1.0
    invc = np.zeros((128, 4, 16), np.float32)
    for g, w in enumerate(POOL_W):
        for pos in range(16):
            invc[:, g, pos] = 1.0 / min(pos + 1, w)
    return {"c_onehot": onehot, "c_ident": identf, "c_j2": j2, "c_invc": invc.reshape(128, 64)}


def build_program(stage=3):
    nc = bass.Bass("TRN2", target_bir_lowering=False)
    S = Sched()

    def din(name, shape):
        return nc.dram_tensor(name, list(shape), F32, kind="ExternalInput").ap()

    def dout(name, shape):
        return nc.dram_tensor(name, list(shape), F32, kind="ExternalOutput").ap()

    x_p = din("x_p", (S_LEN, D)); x_s = din("x_s", (NS, D))
    ck = din("ck", (4, 128, 128)); cv = din("cv", (4, 128, 128)); spool = din("spool", (4, 15, 512))
    p_p = din("p_p", (S_LEN, PLE)); p_s = din("p_s", (NS, PLE))
    table = din("table", (32, 8))
    g_mix_pre = din("g_mix_pre", (1, D)); g_mix_post = din("g_mix_post", (1, D))
    g_ffn_pre = din("g_ffn_pre", (1, D)); g_ffn_post = din("g_ffn_post", (1, D))
    w_in = din("w_in", (D, 1280)); sinks = din("sinks", (1, 8))
    w_pool = din("w_pool", (4, 128, 128)); pool_scale = din("pool_scale", (4, 128))
    w_out = din("w_out", (D, D)); w_gate = din("w_gate", (D, DFF)); w_up = din("w_up", (D, DFF))
    w_down = din("w_down", (DFF, D)); w_ple = din("w_ple", (PLE, D)); w_pg = din("w_pg", (D, D))
    c_onehot = din("c_onehot", (32, 256)); c_ident = din("c_ident", (128, 128))
    c_j2 = din("c_j2", (128, 128)); c_invc = din("c_invc", (128, 64))

    y_p = dout("y_p", (S_LEN, D)); y_s = dout("y_s", (NS, D))
    nk_p = dout("nk_p", (128, 128)); nv_p = dout("nv_p", (128, 128)); npool_p = dout("npool_p", (15, 512))
    nk_s = dout("nk_s", (NS, 128)); nv_s = dout("nv_s", (NS, 128)); npool_s = dout("npool_s", (4, 15, 512))
    rb_dram = nc.dram_tensor("rb_scratch", [8, 256], F32).ap()
    sc_g = nc.dram_tensor("sc_gate", [11, 128, 2048], BF16).ap()
    sc_u = nc.dram_tensor("sc_up", [11, 128, 2048], BF16).ap()
    sc_d = nc.dram_tensor("sc_down", [NFC, 128, 1024], BF16).ap()

    sb = nc.alloc_sbuf_tensor
    wA = sb("wA", [128, 8, 1408], BF16)
    wo = sb("wo", [128, 8, 1024], BF16)
    wpl = sb("wpl", [128, 4, 128], BF16)
    wpg = sb("wpg", [128, 8, 1024], BF16)
    wpe = sb("wpe", [128, 2, 1024], BF16)
    gb = {i: sb("gb%d" % i, [128, 1024], F32) for i in (1, 3)}
    gcol = {0: sb("gcol0", [128, 8], F32), 2: sb("gcol2", [128, 8], F32)}
    xs = [sb("xs%d" % i, [128, 1024], F32) for i in range(7)]
    hT = sb("hT", [128, 8, 576], BF16)
    actT = sb("actT", [128, NFC, 576], BF16)
    wd = sb("wd", [128, NFC, 1024], BF16)
    ring = [(sb("rg%d" % i, [128, 8, 256], BF16), sb("ru%d" % i, [128, 8, 256], BF16)) for i in range(2)]
    kT = sb("kT", [128, 128 + 512], BF16)
    VdA = sb("VdA", [128, 5, 256], BF16)
    VdB = sb("VdB", [128, 5, 256], BF16)
    bias = sb("bias", [128, 4, 192], F32)
    ident = sb("ident", [128, 128], BF16)
    identf = sb("identf", [128, 128], F32)
    sinkc = sb("sinkc", [128, 4], F32)
    pscol = sb("pscol", [128, 4], F32)
    invc = sb("invc", [128, 64], F32)
    ucarry = sb("ucarry", [128, 4, 16], F32)
    stats = sb("stats", [128, 512], F32)
    sg = [sb("sg%d" % i, [128, 512], F32) for i in range(2)]
    pbf_ = [sb("pbf%d" % i, [128, 256], BF16) for i in range(2)]
    pT_ = [sb("pT%d" % i, [128, 256], BF16) for i in range(2)]

    regions = [wd[:].rearrange("p a b -> p (a b)"), actT[:].rearrange("p a b -> p (a b)"),
               ring[0][0][:].rearrange("p a b -> p (a b)"), ring[0][1][:].rearrange("p a b -> p (a b)"),
               ring[1][0][:].rearrange("p a b -> p (a b)"), ring[1][1][:].rearrange("p a b -> p (a b)")]
    rsize = [NFC * 1024, NFC * 576, 2048, 2048, 2048, 2048]
    roff = [0] * len(regions)

    def new_pass():
        for i in range(len(roff)):
            roff[i] = 0

    def carve(nbytes, dtype):
        n16 = (nbytes + 63) // 64 * 32
        for ri in range(len(regions)):
            if roff[ri] + n16 <= rsize[ri]:
                a = roff[ri]
                roff[ri] += n16
                v = regions[ri][:, a:a + nbytes // 2]
                if dtype == F32:
                    v = v.bitcast(F32)
                return v
        raise AssertionError("carve: out of transient space")

    roff[0] = rsize[0]; roff[1] = rsize[1]
    Hk = carve(192 * 4, F32)
    rsb = carve(256 * 4, F32)
    tb32 = carve(8 * 4, F32)
    oh = carve(256 * 4, F32)
    j2 = carve(128 * 4, F32)
    gtmp = carve(128 * 4, F32)
    gtmp2 = carve(128 * 4, F32)
    stqa = carve(8 * 256 * 2, BF16)
    stqb = carve(8 * 256 * 2, BF16)
    stv = carve(8 * 128 * 2, BF16)
    new_pass()
    for _ri in (0, 1):
        roff[_ri] = rsize[_ri]
    tmpfb = [carve(4096, F32) for _ in range(2)]
    junkb = carve(2048, BF16)
    new_pass()
    for _ri in (0, 1):
        roff[_ri] = rsize[_ri]
    tmpf3_ = [carve(4096, F32) for _ in range(2)]
    x2b_ = [carve(2048, BF16) for _ in range(2)]
    x2T_ = [carve(2048, BF16) for _ in range(2)]
    new_pass()
    for _ri in (2, 3, 4, 5):
        roff[_ri] = rsize[_ri]
    hb = [carve(2048, BF16) for _ in range(2)]
    junks = [carve(2048, BF16) for _ in range(2)]
    junk = junks[0]
    qT2 = carve(12 * 4 * 64 * 2, BF16)
    uT = carve(4 * 528 * 4, F32)
    tmpf = [uT[:, 0:1024], uT[:, 1024:2048]]
    uTs = carve(4 * 4 * 32 * 4, F32)
    pa = carve(528 * 4, F32); pb_ = carve(528 * 4, F32)
    dT = carve(4 * 576 * 2, BF16)
    aT = carve(4 * 576 * 2, BF16)
    plT = carve(4 * 576 * 2, BF16)
    Sb = [carve(2 * 193 * 4, F32) for _ in range(2)]
    Pf = [carve(2 * 193 * 4, F32) for _ in range(2)]
    Pn = [carve(2 * 192 * 2, BF16) for _ in range(2)]
    PnT = [carve(512 * 2, BF16) for _ in range(2)]
    ksT = carve(4 * 144 * 2, BF16)
    ckb = carve(4 * 128 * 2, BF16)
    Vsc = carve(4 * 256 * 2, BF16)
    Vsn = carve(4 * 256 * 2, BF16)
    stg = carve(896 * 4, F32)
    spx = carve(512 * 4, F32)

    PSB = [nc.alloc_psum_tensor("PS%d" % i, [128, 1024], BF16) for i in range(8)]
    PF = {i: PSB[i][:].bitcast(F32) for i in range(8)}
    PB0 = PSB[0]
    PB5 = PSB[7]

    B = {}

    def buf(name):
        if name not in B:
            B[name] = Buf(name)
        return B[name]

    bPF = {i: buf("PS%d" % i) for i in range(8)}
    bPB0, bPB5 = bPF[0], bPF[7]
    stat_col = [0]
    NSG = 64
    stat_bufs = [Buf("st%d" % i) for i in range(NSG)]

    def new_stat_group():
        gi_ = stat_col[0] % NSG
        stat_col[0] += 1
        return gi_ * 8, stat_bufs[gi_]

    def rstd_from_ms(c, rows, bst):
        ms = stats[:, c:c + 1]; t = stats[:, c + 2:c + 3]; l = stats[:, c + 3:c + 4]; rstd = stats[:, c + 4:c + 5]
        S.add("dve", lambda e: e.tensor_scalar(out=t[0:rows, :], in0=ms[0:rows, :], scalar1=EPS, scalar2=None, op0=ALU.add),
              r=[bst], w=[bst], n=1)
        S.add("act", lambda e: e.activation(out=l[0:rows, :], in_=t[0:rows, :], func=AF.Ln), r=[bst], w=[bst], n=1)
        S.add("act", lambda e: e.activation(out=rstd[0:rows, :], in_=l[0:rows, :], func=AF.Exp, scale=-0.5), r=[bst], w=[bst], n=1)
        return rstd, bst

    S.add("pool", lambda e: e.memset(ucarry[:].rearrange("p a b -> p (a b)"), 0.0), w=[buf("ucarry")])
    S.add("pool", lambda e: e.memset(VdA[:, 0, :], 0.0), w=[buf("VdA0")])
    S.add("pool", lambda e: e.memset(kT[:, 0:128], 0.0), w=[buf("kTc")])
    S.add("sp", lambda e: e.dma_start(out=identf[:], in_=c_ident[:, :]), w=[buf("identf")], dma="identf")
    S.add("dve", lambda e: e.tensor_copy(out=ident[:], in_=identf[:]), r=[buf("identf")], w=[buf("ident")])
    S.add("sp", lambda e: e.dma_start(out=invc[:], in_=c_invc[:, :]), w=[buf("invc")], dma="invc")
    for i, gsrc in ((1, g_mix_post), (3, g_ffn_post)):
        S.add("sp", lambda e, i=i, gsrc=gsrc: e.dma_start(out=gb[i][:], in_=bass.AP(tensor=gsrc.tensor, offset=0, ap=[[0, 128], [1, 1024]])),
              w=[buf("gb%d" % i)], dma="gb%d" % i)
    for i, gsrc in ((0, g_mix_pre), (2, g_ffn_pre)):
        S.add("sp", lambda e, gsrc=gsrc: e.dma_start(out=gtmp2[0:8, 0:128], in_=bass.AP(tensor=gsrc.tensor, offset=0, ap=[[128, 8], [1, 128]])),
              w=[buf("gtmp2")], dma="gtmp2")
        S.add("pe", lambda e: e.matmul(PF[4][:, 0:8], gtmp2[0:8, 0:128], identf[0:8, 0:8], start=True, stop=True),
              r=[buf("gtmp2"), buf("identf")], w=[bPF[4]])
        S.add("dve", lambda e, i=i: e.tensor_copy(out=gcol[i][:], in_=PF[4][:, 0:8]), w=[bPF[4], buf("gcol%d" % i)])
    for pi in range(4):
        for hh in range(2):
            h = 2 * pi + hh
            S.add("sp", lambda e, pi=pi, hh=hh, h=h: e.dma_start(
                out=sinkc[hh * 64:(hh + 1) * 64, pi:pi + 1], in_=bass.AP(tensor=sinks.tensor, offset=h, ap=[[0, 64], [1, 1]])),
                w=[buf("sinkc")], dma="sinkc")
    S.add("sp", lambda e: e.dma_start(out=gtmp[0:4, 0:128], in_=pool_scale[:, :]), w=[buf("gtmp")], dma="gtmp")
    S.add("pe", lambda e: e.matmul(PF[1][:, 0:4], gtmp[0:4, 0:128], identf[0:4, 0:4], start=True, stop=True),
          r=[buf("gtmp"), buf("identf")], w=[bPF[1]])
    S.add("dve", lambda e: e.tensor_copy(out=pscol[:], in_=PF[1][:, 0:4]), w=[bPF[1], buf("pscol")])

    def wload(dst_ap, src_ap, key, bname):
        S.add("pool", lambda e: e.dma_start(out=dst_ap, in_=src_ap), w=[buf(bname)], dma=key)

    w_in_v = w_in.rearrange("(k p) c -> p k c", p=128)
    stqa3 = stqa.rearrange("p (k c) -> p k c", k=8)
    stqb3 = stqb.rearrange("p (k c) -> p k c", k=8)
    stv3 = stv.rearrange("p (k c) -> p k c", k=8)
    wload(stqa3, w_in_v[:, :, 0:256], "stqa", "stqa")
    wload(stqb3, w_in_v[:, :, 256:512], "stqb", "stqb")
    wload(wA[:, :, 512:640], w_in_v[:, :, 512:640], "wAk", "wAk")
    wload(stv3, w_in_v[:, :, 640:768], "stv", "stv")
    wload(wA[:, :, 896:1408], w_in_v[:, :, 768:1280], "wAu", "wAu")
    S.add("dve", lambda e: e.tensor_copy(out=wA[:, :, 0:512].rearrange("p k (j c) -> p k j c", j=4)[:, :, :, 0:64],
                                         in_=stqa3.rearrange("p k (j c) -> p k j c", j=4)), r=[buf("stqa")], w=[buf("wAq")], n=2048)
    S.add("pool", lambda e: e.tensor_copy(out=wA[:, :, 0:512].rearrange("p k (j c) -> p k j c", j=4)[:, :, :, 64:128],
                                          in_=stqb3.rearrange("p k (j c) -> p k j c", j=4)), r=[buf("stqb")], w=[buf("wAq")], n=2048)
    for kv in range(2):
        for dup in range(2):
            c0 = 640 + kv * 128 + dup * 64
            eng = "dve" if dup == 0 else "pool"
            S.add(eng, lambda e, kv=kv, c0=c0: e.tensor_copy(out=wA[:, :, c0:c0 + 64], in_=stv3[:, :, kv * 64:(kv + 1) * 64]),
                  r=[buf("stv")], w=[buf("wAv")], n=512)
    wload(wpl[:], w_pool.rearrange("g d e -> d g e"), "wpl", "wpl")
    wload(wo[:], w_out.rearrange("(k p) c -> p k c", p=128), "wo", "wo")
    wgv = w_gate.rearrange("(k p) c -> p k c", p=128)
    wuv = w_up.rearrange("(k p) c -> p k c", p=128)
    conv_ops = []
    for blk in range(11):
        conv_ops.append(S.add("pool", lambda e, blk=blk: e.dma_start(out=sc_g[blk].rearrange("p (k c) -> p k c", k=8),
                                                                     in_=wgv[:, :, blk * 256:(blk + 1) * 256]),
                              w=[buf("scg")], dma="cvg", n=1048576))
        conv_ops.append(S.add("pool", lambda e, blk=blk: e.dma_start(out=sc_u[blk].rearrange("p (k c) -> p k c", k=8),
                                                                     in_=wuv[:, :, blk * 256:(blk + 1) * 256]),
                              w=[buf("scu")], dma="cvu", n=1048576))
    for q4 in range(2):
        conv_ops.append(S.add("pool", lambda e, q4=q4: e.dma_start(out=sc_d[q4 * 11:(q4 + 1) * 11].rearrange("c p n -> (c p) n"),
                                                                   in_=w_down[q4 * 1408:(q4 + 1) * 1408, :]),
                              w=[buf("scd")], dma="cvd", n=5767168))

    S.add("sp", lambda e: e.dma_start(out=tb32[0:32, 0:8], in_=table[:, :]), w=[buf("tb32")], dma="tb32")
    S.add("sp", lambda e: e.dma_start(out=oh[0:32, 0:256], in_=c_onehot[:, :]), w=[buf("oh")], dma="oh")
    S.add("sp", lambda e: e.dma_start(out=j2[:], in_=c_j2[:, :]), w=[buf("j2")], dma="j2")
    S.add("pe", lambda e: e.matmul(PF[2][0:8, 0:256], tb32[0:32, 0:8], oh[0:32, 0:256], start=True, stop=True),
          r=[buf("tb32"), buf("oh")], w=[bPF[2]])
    S.add("dve", lambda e: e.tensor_copy(out=rsb[0:8, 0:256], in_=PF[2][0:8, 0:256]), w=[bPF[2], buf("rsb")])
    S.add("sp", lambda e: e.dma_start(out=rb_dram[:, :], in_=rsb[0:8, 0:256]), r=[buf("rsb")], w=[buf("rb_dram")], dma="rbw")
    Hks = [sg[0][:, 0:192], sg[0][:, 192:384], sg[1][:, 0:192], sg[1][:, 192:384]]
    for pi in range(4):
        for hh in range(2):
            h = 2 * pi + hh
            src = bass.AP(tensor=rb_dram.tensor, offset=h * 256, ap=[[1, 64], [1, 192]])
            S.add("sp", lambda e, hh=hh, src=src, pi=pi: e.dma_start(out=Hks[pi][hh * 64:(hh + 1) * 64, :], in_=src),
                  r=[buf("rb_dram")], w=[buf("Hk%d" % pi)], dma="Hk%d" % pi)
    for pi in range(4):
        pbk = 3 + (pi % 2)
        S.add("pe", lambda e, pi=pi, pbk=pbk: e.matmul(PF[pbk][:, 0:192], j2[:], Hks[pi], start=True, stop=True),
              r=[buf("j2"), buf("Hk%d" % pi)], w=[bPF[pbk]])
        S.add("dve", lambda e, pi=pi, pbk=pbk: e.tensor_copy(out=bias[:, pi, :], in_=PF[pbk][:, 0:192]),
              w=[bPF[pbk], buf("bias")])

    for ci, o in enumerate(conv_ops):
        o.prio = 3000.0 + ci
        for bn in ("bias", "wo", "wAu", "wAk", "stqa", "stqb", "stv", "wpl", "xs0", "xs1", "xs2", "xs3"):
            w_ = B[bn].w if bn in B else None
            if w_ is not None and w_ is not o:
                o.deps[w_] = "RAW"
    jsel = [0]

    def rms_stats(src_ap, rows, bsrc, eng_sq="act"):
        c, bst = new_stat_group()
        ms = stats[:, c:c + 1]
        jsel[0] ^= 1
        jk = junks[jsel[0]]; bjk = buf("junk%d" % jsel[0])
        S.add("act", lambda e: e.activation(out=jk[0:rows, :], in_=src_ap, func=AF.Square, scale=1.0 / 32.0,
                                            accum_out=ms[0:rows, :]),
              r=bsrc, w=[bjk, bst], n=1024)
        return rstd_from_ms(c, rows, bst)

    def transposes(src_tile, rows, bsrc, dst_fn, bdst, nk, psum=PB0, bps=None):
        bps = bps or bPB0
        S.begin("pe")
        for k in range(nk):
            S.add("pe", lambda e, k=k: e.transpose(psum[:, k * 128:k * 128 + rows], src_tile[0:rows, k * 128:(k + 1) * 128],
                                                   ident[0:rows, 0:rows]),
                  r=bsrc + [buf("ident")], w=[bps], n=rows)
        S.end()
        return bps

    def issue_x_loads(g, only=None):
        tl = [(i, 128) for i in range(4)] + ([(4, NS)] if g == NG - 1 else [])
        if only is not None:
            tl = [t_ for t_ in tl if t_[0] in only]
        for (i, rows) in tl:
            src = x_p[(4 * g + i) * 128:(4 * g + i + 1) * 128, :] if i < 4 else x_s[:, :]
            si = (4 * g + i) % 7
            S.add("sp", lambda e, si=si, rows=rows, src=src: e.dma_start(out=xs[si][0:rows, :], in_=src),
                  w=[buf("xs%d" % si)], dma="xs%d" % si, n=rows * 4096)

    hw = 0
    for g in range(NG):
        has_s = (g == NG - 1)
        xsl = (lambda i, g=g: (4 * g + i) % 7)
        T = GT + (NS if has_s else 0)
        halves = [(0, 320), (320, 576)] if has_s else [(0, 256), (256, 512)]
        tiles = [(i, 128) for i in range(4)] + ([(4, NS)] if has_s else [])
        gtile0 = 4 * g

        if has_s:
            S.add("pool", lambda e: e.memset(qT2[:, 8 * 256:12 * 256], 0.0), w=[buf("qT2s")])
        if g == 0:
            issue_x_loads(0)
        wgv = w_gate.rearrange("(k p) c -> p k c", p=128)
        wuv = w_up.rearrange("(k p) c -> p k c", p=128)

        def ring_load(blk):
            rg, ru = ring[blk % 2]
            rgf = rg[:].rearrange("p k c -> p (k c)")
            ruf = ru[:].rearrange("p k c -> p (k c)")
            bg, bu = buf("rg%d" % (blk % 2)), buf("ru%d" % (blk % 2))
            S.add("sp", lambda e, blk=blk, rgf=rgf: e.dma_start(out=rgf, in_=sc_g[blk, :, :]),
                  r=[buf("scg")], w=[bg], dma="rgh%d" % (blk % 2), n=524288)
            S.add("sp", lambda e, blk=blk, ruf=ruf: e.dma_start(out=ruf, in_=sc_u[blk, :, :]),
                  r=[buf("scu")], w=[bu], dma="ruh%d" % (blk % 2), n=524288)

        def wd_load(c):
            S.add("sp", lambda e, c=c: e.dma_start(out=wd[:, c, :], in_=sc_d[c, :, :]),
                  r=[buf("scd")], w=[buf("wd%d" % c)], dma="wdh%d" % (c % 4), n=262144, serial=True)

        if g > 0 and stage >= 2:
            ring_load(0)
            ring_load(1)
        for (i, rows) in tiles:
            bx = buf("xs%d" % xsl(i))
            rstd, brs = rms_stats(xs[xsl(i)][0:rows, :], rows, [bx])
            hbuf = hb[hw % 2]; bh = buf("hb%d" % (hw % 2)); hw += 1
            S.add("dve", lambda e, i=i, rows=rows, rstd=rstd, hbuf=hbuf, xi=xsl(i): e.tensor_scalar(
                out=hbuf[0:rows, :], in0=xs[xi][0:rows, :], scalar1=rstd[0:rows, :], scalar2=None, op0=ALU.mult),
                r=[bx, brs], w=[bh], n=512)
            tb = (0, 7)[i % 2]
            transposes(hbuf, rows, [bh], None, None, 8, psum=PSB[tb], bps=bPF[tb])
            c0 = i * 128
            S.add("dve", lambda e, c0=c0, rows=rows, tb=tb: e.tensor_tensor(
                out=hT[:, :, c0:c0 + rows], in0=PSB[tb][:].rearrange("p (k t) -> p k t", k=8)[:, :, 0:rows],
                in1=gcol[0][:].unsqueeze(2).to_broadcast([128, 8, rows]), op=ALU.mult),
                r=[buf("gcol0")], w=[bPF[tb], buf("hT%d" % i)], n=8 * rows)
        bhT_all = [buf("hT%d" % i) for (i, _) in tiles]

        qv = qT2.rearrange("p (c j q) -> p c j q", j=4, q=64)
        uv = uT.rearrange("p (g t) -> p g t", g=4)
        usv = uTs.rearrange("p (g b t) -> p g b t", g=4, b=4)
        ksv = ksT.rearrange("p (b t) -> p b t", b=4)
        S.add("dve", lambda e: e.tensor_copy(out=uv[:, :, 0:16], in_=ucarry[:]), r=[buf("ucarry")], w=[buf("uT")], n=64)
        def v_for_tile(i):
            if i < 4:
                S.begin("pe")
                for k in range(8):
                    S.add("pe", lambda e, i=i, k=k: e.matmul(PF[6][:, 0:256], hT[:, k, i * 128:(i + 1) * 128],
                                                             wA[:, k, 640:896], start=(k == 0), stop=(k == 7)),
                          r=[buf("wAv"), buf("hT%d" % i)], w=[bPF[6]], n=256)
                S.end()
                S.add("act", lambda e, i=i: e.activation(out=VdA[:, i + 1, :], in_=PF[6][:, 0:256], func=AF.Copy),
                      w=[bPF[6], buf("VdA%d" % (i + 1))], n=256)
                if i == 0:
                    S.add("sp", lambda e: e.dma_start(out=VdB[0:64, 0, :], in_=VdA[64:128, 0, :]),
                          r=[buf("VdA0")], w=[buf("VdBlo0")], dma="VdBlo0")
                t = i + 1
                S.add("sp", lambda e, t=t: e.dma_start(out=VdB[0:64, t, :], in_=VdA[64:128, t, :]),
                      r=[buf("VdA%d" % t)], w=[buf("VdBlo%d" % t)], dma="VdBlo%d" % t)
                S.add("sp", lambda e, t=t: e.dma_start(out=VdB[64:128, t - 1, :], in_=VdA[0:64, t, :]),
                      r=[buf("VdA%d" % t)], w=[buf("VdBhi%d" % (t - 1))], dma="VdBhi%d" % (t - 1))
            else:
                for b in range(4):
                    S.begin("pe")
                    for k in range(8):
                        S.add("pe", lambda e, b=b, k=k: e.matmul(PF[6][0:16, 0:256], hT[:, k, 512 + 16 * b:512 + 16 * b + 16],
                                                                 wA[:, k, 640:896], start=(k == 0), stop=(k == 7)),
                              r=[buf("wAv"), buf("hT4")], w=[bPF[6]], n=64)
                    S.end()
                    S.add("act", lambda e, b=b: e.activation(out=Vsn[0:16, b * 256:(b + 1) * 256], in_=PF[6][0:16, 0:256],
                                                             func=AF.Copy), w=[bPF[6], buf("Vsn")], n=256)

        half_first_idx = []
        for hi, (a, b_) in enumerate(halves):
            half_first_idx.append(len(S.ops))
            hT_need = [buf("hT%d" % t_) for t_ in range(a // 128, (b_ - 1) // 128 + 1)]
            for oc in range(9):
                pbank = PF[3 + (oc % 2)]; bp = bPF[3 + (oc % 2)]
                n = b_ - a
                S.begin("pe")
                for k in range(8):
                    S.add("pe", lambda e, oc=oc, k=k, a=a, b_=b_, n=n, pbank=pbank: e.matmul(
                        pbank[:, 0:n], wA[:, k, (oc if oc < 5 else oc + 2) * 128:((oc if oc < 5 else oc + 2) + 1) * 128],
                        hT[:, k, a:b_], start=(k == 0), stop=(k == 7)),
                        r=[buf("wAq"), buf("wAk"), buf("wAu")] + hT_need, w=[bp], n=n)
                S.end()
                npr = min(b_, GT) - a
                ncp = npr // 64
                cbase = a // 64
                if oc < 4:
                    S.add("act", lambda e, oc=oc, pbank=pbank, npr=npr, ncp=ncp, cbase=cbase: e.activation(
                        out=qv[:, cbase:cbase + ncp, oc, :], in_=pbank[:, 0:npr].rearrange("p (c q) -> p c q", q=64),
                        func=AF.Copy, scale=0.125), w=[bp, buf("qT2h%d" % hi)])
                    if has_s and hi == 1:
                        S.add("act", lambda e, oc=oc, pbank=pbank, npr=npr: e.activation(
                            out=qv[:, 8:12, oc, 0:16], in_=pbank[:, npr:npr + 64].rearrange("p (c q) -> p c q", q=16),
                            func=AF.Copy, scale=0.125), w=[bp, buf("qT2s")])
                elif oc == 4:
                    S.add("act", lambda e, pbank=pbank, npr=npr, a=a: e.activation(
                        out=kT[:, 128 + a:128 + a + npr], in_=pbank[:, 0:npr], func=AF.Copy), w=[bp, buf("kT%d" % hi)])
                    if has_s and hi == 1:
                        S.add("act", lambda e, pbank=pbank, npr=npr: e.activation(
                            out=ksv[:, :, 128:144], in_=pbank[:, npr:npr + 64].rearrange("p (b t) -> p b t", t=16),
                            func=AF.Copy), w=[bp, buf("ksT")])
                else:
                    gi = oc - 5
                    S.add("dve", lambda e, gi=gi, pbank=pbank, npr=npr, a=a: e.tensor_copy(
                        out=uv[:, gi, 16 + a:16 + a + npr], in_=pbank[:, 0:npr]), w=[bp, buf("uT")])
                    if has_s and hi == 1:
                        S.add("dve", lambda e, gi=gi, pbank=pbank, npr=npr: e.tensor_copy(
                            out=usv[:, gi, :, 16:32], in_=pbank[:, npr:npr + 64].rearrange("p (b t) -> p b t", t=16)),
                            w=[bp, buf("uTs")])
            for (i_, rows_) in tiles:
                if a <= i_ * 128 < b_:
                    v_for_tile(i_)


        out_tiles = []
        if g == NG - 1:
            out_tiles = [(3, 128, 3 * 128), (4, NS, 512)]
        for (i, rows, c0) in out_tiles:
            for (pa_, pb2, bank) in ((0, 512, 4), (512, 896, 6)):
                for k in range(8):
                    S.add("pe", lambda e, k=k, c0=c0, rows=rows, pa_=pa_, pb2=pb2, bank=bank: e.matmul(
                        PF[bank][0:rows, 0:pb2 - pa_], hT[:, k, c0:c0 + rows], wA[:, k, 512 + pa_:512 + pb2],
                        start=(k == 0), stop=(k == 7)), r=[buf("wAq"), buf("wAk"), buf("wAv"), buf("wAu"), buf("hT%d" % i)], w=[bPF[bank]])
                S.add("dve", lambda e, rows=rows, pa_=pa_, pb2=pb2, bank=bank: e.tensor_copy(
                    out=stg[0:rows, pa_:pb2], in_=PF[bank][0:rows, 0:pb2 - pa_]), w=[bPF[bank], buf("stg")])
            bs = buf("stg")
            if i == 3:
                S.add("sp", lambda e: e.dma_start(out=nk_p[:, :], in_=stg[:, 0:128]), r=[bs], dma="o_nkp", out=True)
                S.add("sp", lambda e: e.dma_start(out=nv_p[:, 0:64], in_=stg[:, 128:192]), r=[bs], dma="o_nvp", out=True)
                S.add("sp", lambda e: e.dma_start(out=nv_p[:, 64:128], in_=stg[:, 256:320]), r=[bs], dma="o_nvp", out=True)
                S.add("sp", lambda e: e.dma_start(out=npool_p[:, :], in_=stg[113:128, 384:896]), r=[bs], dma="o_npp", out=True)
            else:
                S.add("sp", lambda e: e.dma_start(out=nk_s[:, :], in_=stg[0:NS, 0:128]), r=[bs], dma="o_nks", out=True)
                S.add("sp", lambda e: e.dma_start(out=nv_s[:, 0:64], in_=stg[0:NS, 128:192]), r=[bs], dma="o_nvs", out=True)
                S.add("sp", lambda e: e.dma_start(out=nv_s[:, 64:128], in_=stg[0:NS, 256:320]), r=[bs], dma="o_nvs", out=True)
                for b in range(4):
                    S.add("sp", lambda e, b=b: e.dma_start(out=npool_s[b, :, :], in_=stg[16 * b + 1:16 * b + 16, 384:896]),
                          r=[bs], dma="o_nps", out=True)

        if has_s:
            for b in range(4):
                S.add("pool", lambda e, b=b: e.dma_start(out=ckb[:, b * 128:(b + 1) * 128], in_=ck[b, :, :]),
                      w=[buf("ckb")], dma="ckb")
                for kv in range(2):
                    for dup in range(2):
                        c0 = b * 256 + kv * 128 + dup * 64
                        S.add("pool", lambda e, b=b, kv=kv, c0=c0: e.dma_start(
                            out=Vsc[:, c0:c0 + 64], in_=cv[b, :, kv * 64:(kv + 1) * 64]), w=[buf("Vsc")], dma="Vsc")
            for b in range(4):
                S.add("pe", lambda e, b=b: e.transpose(PB5[:, b * 128:(b + 1) * 128], ckb[:, b * 128:(b + 1) * 128], ident[:]),
                      r=[buf("ckb"), buf("ident")], w=[bPB5])
            S.add("act", lambda e: e.activation(out=ksv[:, :, 0:128], in_=PB5[:, 0:512].rearrange("p (b t) -> p b t", b=4),
                                                func=AF.Copy), w=[bPB5, buf("ksT")])
            for b in range(4):
                S.add("sp", lambda e, b=b: e.dma_start(out=spx[0:15, :], in_=spool[b, :, :]), w=[buf("spx")], dma="spx")
                for gi in range(4):
                    S.add("pe", lambda e, gi=gi: e.matmul(PF[4][:, gi * 16:gi * 16 + 15], spx[0:15, gi * 128:(gi + 1) * 128],
                                                          identf[0:15, 0:15], start=True, stop=True),
                          r=[buf("spx"), buf("identf")], w=[bPF[4]])
                S.add("dve", lambda e, b=b: e.tensor_copy(
                    out=usv[:, :, b, 1:16], in_=PF[4][:, 0:64].rearrange("p (g t) -> p g t", t=16)[:, :, 0:15]),
                    w=[bPF[4], buf("uTs")])

        def pool_windows(src3, L, nb, gi, w, dst3, first16):
            pav = pa[:, 0:nb * (16 + L)].rearrange("p (b t) -> p b t", b=nb)
            pbv = pb_[:, 0:nb * (16 + L)].rearrange("p (b t) -> p b t", b=nb)
            cur = src3
            step = 1
            tmpsel = [pav, pbv]
            ti = 0
            rb_ = [buf("uT"), buf("uTs")]
            while step < w:
                nxt = tmpsel[ti]; ti ^= 1
                lo = 2 * step
                S.add("pool", lambda e, cur=cur, nxt=nxt, lo=lo, step=step, L=L: e.tensor_tensor(
                    out=nxt[:, :, lo:16 + L], in0=cur[:, :, lo:16 + L], in1=cur[:, :, lo - step:16 + L - step], op=ALU.add),
                    r=rb_ + [buf("ptmp")], w=[buf("ptmp")], n=nb * (16 + L) // 2)
                cur = nxt
                step *= 2
            tq = pbv if cur is pav else pav
            if first16:
                S.add("dve", lambda e, cur=cur, gi=gi, tq=tq: e.tensor_tensor(
                    out=tq[:, 0, 0:16], in0=cur[:, 0, 16:32], in1=invc[:, gi * 16:(gi + 1) * 16], op=ALU.mult),
                    r=[buf("ptmp"), buf("invc")], w=[buf("ptmp")], n=16)
            S.add("dve", lambda e, cur=cur, w=w, L=L: e.scalar_tensor_tensor(
                out=dst3[:, :, 0:L], in0=cur[:, :, 16:16 + L], scalar=1.0 / w, in1=src3[:, :, 16:16 + L],
                op0=ALU.mult, op1=ALU.subtract), r=rb_ + [buf("ptmp")], w=[buf("dT")], n=nb * L)
            if first16:
                S.add("dve", lambda e, tq=tq: e.tensor_tensor(
                    out=dst3[:, 0, 0:16], in0=tq[:, 0, 0:16], in1=src3[:, 0, 16:32], op=ALU.subtract),
                    r=[buf("ptmp"), buf("uT")], w=[buf("dT")], n=16)

        dv = dT.rearrange("p (g t) -> p g t", g=4)
        pool_first_idx = len(S.ops)
        for gi, w in enumerate(POOL_W):
            pool_windows(uv[:, gi:gi + 1, :], 512, 1, gi, w, dv[:, gi:gi + 1, 0:512], first16=(g == 0))
            if has_s:
                pool_windows(usv[:, gi, :, :], 16, 4, gi, w,
                             dv[:, gi, 512:576].rearrange("p (b t) -> p b t", b=4), first16=False)
        S.add("dve", lambda e: e.tensor_copy(out=ucarry[:], in_=uv[:, :, 512:528]), r=[buf("uT")], w=[buf("ucarry")], n=64)
        plv = plT.rearrange("p (g t) -> p g t", g=4)
        for gi in range(4):
            for hi, (a, b_) in enumerate(halves):
                n = b_ - a
                S.add("pe", lambda e, gi=gi, a=a, b_=b_, n=n, hi=hi: e.matmul(
                    PF[5 + hi][:, 0:n], wpl[:, gi, :], dv[:, gi, a:b_], start=True, stop=True),
                    r=[buf("wpl"), buf("dT")], w=[bPF[5 + hi]], n=n)
                S.add("act", lambda e, gi=gi, a=a, b_=b_, n=n, hi=hi: e.activation(
                    out=plv[:, gi, a:b_], in_=PF[5 + hi][:, 0:n], func=AF.Copy, scale=pscol[:, gi:gi + 1]),
                    r=[buf("pscol")], w=[bPF[5 + hi], buf("plT")], n=n)

        pool_last_idx = len(S.ops)
        av = aT.rearrange("p (j t) -> p j t", j=4)
        Sbv = [x.rearrange("p (a t) -> p a t", a=2) for x in Sb]
        Pfv = [x.rearrange("p (a t) -> p a t", a=2) for x in Pf]
        Pnv = [x.rearrange("p (a t) -> p a t", a=2) for x in Pn]
        PnTv = [x.rearrange("p (b a t) -> p b a t", b=2, a=2) for x in PnT]
        for kv in range(2):
            S.add("dve", lambda e, kv=kv: e.tensor_copy(out=Sbv[kv][:, :, 0:1],
                                                        in_=sinkc[:, 2 * kv:2 * kv + 2].rearrange("p (a o) -> p a o", o=1)),
                  r=[buf("sinkc")], w=[buf("Sb%d" % kv)], n=2)
        st_i = 0

        def s_pair(lhs_fn, rhs_ap, nkeys, bias_lo, kv, blocks, out_cols, nq, rbufs):
            nonlocal st_i
            sl = st_i % 2; st_i += 1
            sbank = 1 + sl; tbank = (0, 7)[sl]
            psS = PF[sbank][:, 0:384].rearrange("p (a t) -> p a t", a=2); bpsS = bPF[sbank]
            psT = PSB[tbank][:, 0:512].rearrange("p (b a t) -> p b a t", b=2, a=2); bpsT = bPF[tbank]
            psO = PF[tbank][:, 256:512].rearrange("p (a t) -> p a t", a=2); bpsO = bPF[tbank]
            kvp = slice(kv * 64, kv * 64 + 64)
            S.begin("pe")
            for pp in range(2):
                S.add("pe", lambda e, pp=pp: e.matmul(psS[:, pp, 0:nkeys], lhs_fn(kvp, pp), rhs_ap(kvp), start=True, stop=True),
                      r=rbufs, w=[bpsS], n=nkeys)
            S.end()
            bSb = buf("Sb%d" % kv); bPf = buf("Pf%d" % sl); bPn = buf("Pn%d" % sl); bPnT = buf("PnT%d" % sl)
            S.add("dve", lambda e: e.tensor_tensor(out=Sbv[kv][:, :, 1:1 + nkeys], in0=psS[:, :, 0:nkeys],
                                                   in1=bias[:, 2 * kv:2 * kv + 2, bias_lo:bias_lo + nkeys], op=ALU.add),
                  r=[buf("bias")], w=[bpsS, bSb], n=2 * nkeys)
            c0, bstp = new_stat_group()
            negm = stats[:, c0:c0 + 2]; rsum = stats[:, c0 + 2:c0 + 4]; rr = stats[:, c0 + 4:c0 + 6]
            bng, brsum, brr = bstp, bstp, bstp
            S.add("dve", lambda e: e.reduce_max(out=negm, in_=Sbv[kv][:, :, 0:1 + nkeys], axis=AX.X, negate=True),
                  r=[bSb], w=[bng], n=2 * nkeys)
            for pp in range(2):
                S.add("act", lambda e, pp=pp: e.activation(out=Pfv[sl][:, pp, 0:1 + nkeys], in_=Sbv[kv][:, pp, 0:1 + nkeys],
                                                           func=AF.Exp, bias=negm[:, pp:pp + 1], scale=1.0,
                                                           accum_out=rsum[:, pp:pp + 1]),
                      r=[bSb, bng], w=[bPf, brsum], n=nkeys)
            S.add("dve", lambda e: e.reciprocal(out=rr, in_=rsum), r=[brsum], w=[brr], n=2)
            S.add("pool", lambda e: e.tensor_tensor(out=Pnv[sl][:, :, 0:nkeys], in0=Pfv[sl][:, :, 1:1 + nkeys],
                                                    in1=rr.unsqueeze(2).to_broadcast([128, 2, nkeys]), op=ALU.mult),
                  r=[bPf, brr], w=[bPn], n=nkeys)
            S.begin("pe")
            off = 0
            for bi, (nk_, v_ap, vb) in enumerate(blocks):
                for pp in range(2):
                    S.add("pe", lambda e, bi=bi, pp=pp, off=off, nk_=nk_: e.transpose(
                        psT[0:nk_, bi, pp, :], Pnv[sl][:, pp, off:off + nk_], ident[:]),
                        r=[bPn, buf("ident")], w=[bpsT], n=128)
                off += nk_
            S.end()
            for bi, (nk_, v_ap, vb) in enumerate(blocks):
                if bi == 0:
                    S.add("act", lambda e, bi=bi, nk_=nk_: e.activation(out=PnTv[sl][0:nk_, bi, :, :], in_=psT[0:nk_, bi, :, :],
                                                                        func=AF.Copy), w=[bpsT, bPnT], n=256)
                else:
                    S.add("dve", lambda e, bi=bi, nk_=nk_: e.tensor_copy(out=PnTv[sl][0:nk_, bi, :, :], in_=psT[0:nk_, bi, :, :]),
                          w=[bpsT, bPnT], n=200)
            S.begin("pe")
            for pp in range(2):
                for bi, (nk_, v_ap, vb) in enumerate(blocks):
                    S.add("pe", lambda e, bi=bi, pp=pp, nk_=nk_, v_ap=v_ap: e.matmul(
                        psO[:, pp, :], v_ap, PnTv[sl][0:nk_, bi, pp, :],
                        start=(bi == 0), stop=(bi == len(blocks) - 1)), r=[bPnT] + vb, w=[bpsO], n=128)
            S.end()
            for hh in range(2):
                eng = "act" if hh == 0 else "dve"
                if eng == "act":
                    S.add("act", lambda e, hh=hh: e.activation(
                        out=av[hh * 64:(hh + 1) * 64, 2 * kv:2 * kv + 2, out_cols:out_cols + nq],
                        in_=psO[hh * 64:(hh + 1) * 64, :, hh * 64:hh * 64 + nq], func=AF.Copy), w=[bpsO, buf("aT%d" % (out_cols // 128))], n=2 * nq)
                else:
                    S.add("dve", lambda e, hh=hh: e.tensor_copy(
                        out=av[hh * 64:(hh + 1) * 64, 2 * kv:2 * kv + 2, out_cols:out_cols + nq],
                        in_=psO[hh * 64:(hh + 1) * 64, :, hh * 64:hh * 64 + nq]), w=[bpsO, buf("aT%d" % (out_cols // 128))], n=2 * nq)

        early_att_first = len(S.ops)
        early_att_last = early_att_first
        for c in range(8):
            gc = 8 * g + c
            t = c // 2
            if c * 64 == halves[0][1] or (c == 4 and halves[0][1] > 256):
                pass
            if (c + 1) * 64 <= 256:
                early_att_last = None
            elif early_att_last is None:
                early_att_last = len(S.ops)
            for kv in range(2):
                vs = slice(kv * 128, (kv + 1) * 128)
                if gc == 0:
                    kc0, nkeys, blo = 128, 64, 128
                    blocks = [(64, VdA[0:64, 1, vs], [buf("VdA1")])]
                elif gc == 1:
                    kc0, nkeys, blo = 128, 128, 64
                    blocks = [(128, VdA[:, 1, vs], [buf("VdA1")])]
                else:
                    kc0, nkeys, blo = 128 + (c - 2) * 64, 192, 0
                    if c % 2 == 0:
                        blocks = [(128, VdA[:, t, vs], [buf("VdA%d" % t)]),
                                  (64, VdA[0:64, t + 1, vs], [buf("VdA%d" % (t + 1))])]
                    else:
                        blocks = [(128, VdB[:, t, vs], [buf("VdBlo%d" % t), buf("VdBhi%d" % t)]),
                                  (64, VdB[0:64, t + 1, vs], [buf("VdBlo%d" % (t + 1))])]
                lhs_fn = (lambda kvp, pp, c=c: qT2[kvp, (c * 4 + 2 * pp) * 64:(c * 4 + 2 * pp + 2) * 64])
                rhs_fn = (lambda kvp, kc0=kc0, nkeys=nkeys: kT[kvp, kc0:kc0 + nkeys])
                hq = 0 if c * 64 < halves[0][1] else 1
                kb = set()
                for cc_ in range(max(c - 2, -2), c + 1):
                    kb.add("kTc" if cc_ < 0 else ("kT0" if cc_ * 64 < halves[0][1] else "kT1"))
                s_pair(lhs_fn, rhs_fn, nkeys, blo, kv, blocks, c * 64, 64, [buf("qT2h%d" % hq)] + [buf(x) for x in sorted(kb)])
        if has_s:
            for b in range(4):
                for kv in range(2):
                    vs = slice(b * 256 + kv * 128, b * 256 + (kv + 1) * 128)
                    blocks = [(128, Vsc[:, vs], [buf("Vsc")]), (16, Vsn[0:16, vs], [buf("Vsn")])]
                    lhs_fn = (lambda kvp, pp, b=b: qT2[kvp, ((8 + b) * 4 + 2 * pp) * 64:((8 + b) * 4 + 2 * pp + 2) * 64])
                    rhs_fn = (lambda kvp, b=b: ksT[kvp, b * 144:(b + 1) * 144])
                    s_pair(lhs_fn, rhs_fn, 144, 0, kv, blocks, 512 + 16 * b, 16, [buf("qT2s"), buf("ksT")])

        if early_att_last is not None and len(half_first_idx) > 1:
            n_e = max(early_att_last - early_att_first, 1)
            for j, o in enumerate(S.ops[early_att_first:early_att_last]):
                o.prio = half_first_idx[1] - 0.5 + 0.4 * j / n_e
        att_last_idx = len(S.ops)
        pool_ops = S.ops[pool_first_idx:pool_last_idx]
        for j, o in enumerate(pool_ops):
            o.prio = pool_last_idx + (j + 1) * (att_last_idx - pool_last_idx) * 0.4 / (len(pool_ops) + 1)
        if g < NG - 1:
            S.add("act", lambda e: e.activation(out=kT[:, 0:128], in_=kT[:, 512:640], func=AF.Copy),
                  r=[buf("kT1")], w=[buf("kTc")])
            S.add("act", lambda e: e.activation(out=VdA[:, 0, :], in_=VdA[:, 4, :], func=AF.Copy),
                  r=[buf("VdA4")], w=[buf("VdA0")])

        def post_norm_residual(i, rows, banks, gidx, tf, btf, jk, bjk):
            c, bms = new_stat_group()
            ms = stats[:, c:c + 1]; ms2 = stats[:, c + 1:c + 2]
            S.add("act", lambda e: e.activation(out=jk[0:rows, 0:512], in_=PF[banks[0]][0:rows, :], func=AF.Square,
                                                scale=1.0 / 32.0, accum_out=ms[0:rows, :]), w=[bPF[banks[0]], bjk, bms], n=512)
            S.add("act", lambda e: e.activation(out=jk[0:rows, 512:1024], in_=PF[banks[1]][0:rows, :], func=AF.Square,
                                                scale=1.0 / 32.0, accum_out=ms2[0:rows, :]), w=[bPF[banks[1]], bjk, bms], n=512)
            S.add("dve", lambda e: e.tensor_tensor(out=ms[0:rows, :], in0=ms[0:rows, :], in1=ms2[0:rows, :], op=ALU.add),
                  r=[bms], w=[bms], n=1)
            rstd, brs = rstd_from_ms(c, rows, bms)
            for hf in range(2):
                S.add("dve", lambda e, hf=hf: e.scalar_tensor_tensor(
                    out=tf[0:rows, hf * 512:(hf + 1) * 512], in0=PF[banks[hf]][0:rows, :], scalar=rstd[0:rows, :],
                    in1=gb[gidx][0:rows, hf * 512:(hf + 1) * 512], op0=ALU.mult, op1=ALU.mult),
                    r=[brs, buf("gb%d" % gidx)], w=[bPF[banks[hf]], btf], n=512)
            r_eng = "dve" if gidx == 1 else "pool"
            S.add(r_eng, lambda e, xi=xsl(i): e.tensor_tensor(out=xs[xi][0:rows, :], in0=xs[xi][0:rows, :], in1=tf[0:rows, :], op=ALU.add),
                  r=[btf], w=[buf("xs%d" % xsl(i))], n=1024)

        for ti, (i, rows) in enumerate(tiles):
            c0 = i * 128
            mb = (5, 6) if ti % 2 == 0 else (3, 4)
            S.begin("pe")
            for hf in range(2):
                for j in range(8):
                    src = av if j < 4 else plv
                    S.add("pe", lambda e, j=j, hf=hf, c0=c0, rows=rows, src=src, mb=mb: e.matmul(
                        PF[mb[hf]][0:rows, :], src[:, j % 4, c0:c0 + rows], wo[:, j, hf * 512:(hf + 1) * 512],
                        start=(j == 0), stop=(j == 7)), r=[buf("wo"), buf("aT%d" % i), buf("plT")], w=[bPF[mb[hf]]], n=512)
            S.end()
            post_norm_residual(i, rows, mb, 1, tmpf[ti % 2], buf("tmpf%d" % (ti % 2)), junks[ti % 2], buf("junk%d" % (ti % 2)))

        for (i, rows) in tiles:
            bx = buf("xs%d" % xsl(i))
            rstd, brs = rms_stats(xs[xsl(i)][0:rows, :], rows, [bx])
            hbuf = hb[hw % 2]; bh = buf("hb%d" % (hw % 2)); hw += 1
            S.add("dve", lambda e, i=i, rows=rows, rstd=rstd, hbuf=hbuf, xi=xsl(i): e.tensor_scalar(
                out=hbuf[0:rows, :], in0=xs[xi][0:rows, :], scalar1=rstd[0:rows, :], scalar2=None, op0=ALU.mult),
                r=[bx, brs], w=[bh], n=512)
            tb = (0, 7)[i % 2]
            transposes(hbuf, rows, [bh], None, None, 8, psum=PSB[tb], bps=bPF[tb])
            c0 = i * 128
            S.add("dve", lambda e, c0=c0, rows=rows, tb=tb: e.tensor_tensor(
                out=hT[:, :, c0:c0 + rows], in0=PSB[tb][:].rearrange("p (k t) -> p k t", k=8)[:, :, 0:rows],
                in1=gcol[2][:].unsqueeze(2).to_broadcast([128, 8, rows]), op=ALU.mult),
                r=[buf("gcol2")], w=[bPF[tb], buf("hT%d" % i)], n=8 * rows)

        if g == 0:
            wload(wpg[:], w_pg.rearrange("(k p) c -> p k c", p=128), "wpg", "wpg")
            wload(wpe[:], w_ple.rearrange("(k p) c -> p k c", p=128), "wpe", "wpe")
        if g + 1 < NG:
            issue_x_loads(g + 1, only=(0, 1, 2))
        def p2a_front(c):
            blk = c // 2
            cc = c % 2
            rg, ru = ring[blk % 2]
            for hi, (a, b_) in enumerate([(0, 512)] + ([(512, 576)] if has_s else [])):
                n = b_ - a
                gbank = (1, 2)[hi] if c % 2 == 0 else (5, 6)[hi]
                ubank = (3, 4)[hi] if c % 2 == 0 else (0, 7)[hi]
                if hi == 0 and c < 4:
                    cblocks = [(0, 384), (384, 512)]
                else:
                    cblocks = [(a, b_)]
                for (wt, wbuf, bank) in ((rg, "rg%d" % (blk % 2), gbank), (ru, "ru%d" % (blk % 2), ubank)):
                    for (ca, cb) in cblocks:
                        need = [buf("hT%d" % t_) for t_ in range(ca // 128, min((cb - 1) // 128, 4) + 1)]
                        S.begin("pe")
                        for k in range(8):
                            S.add("pe", lambda e, k=k, ca=ca, cb=cb, a=a, wt=wt, cc=cc, bank=bank: e.matmul(
                                PF[bank][:, ca - a:cb - a], wt[:, k, cc * 128:(cc + 1) * 128], hT[:, k, ca:cb],
                                start=(k == 0), stop=(k == 7)), r=[buf(wbuf)] + need, w=[bPF[bank]], n=cb - ca)
                        S.end()
                sgi = hi
                S.add("act", lambda e, n=n, gbank=gbank, sgi=sgi: e.activation(out=sg[sgi][:, 0:n], in_=PF[gbank][:, 0:n],
                                                                               func=AF.Silu), w=[bPF[gbank], buf("sg%d" % sgi)], n=n)

        def p2a_back(c):
            for hi, (a, b_) in enumerate([(0, 512)] + ([(512, 576)] if has_s else [])):
                n = b_ - a
                ubank = (3, 4)[hi] if c % 2 == 0 else (0, 7)[hi]
                sgi = hi
                S.add("dve", lambda e, c=c, a=a, b_=b_, n=n, ubank=ubank, sgi=sgi: e.tensor_tensor(
                    out=actT[:, c, a:b_], in0=PF[ubank][:, 0:n], in1=sg[sgi][:, 0:n], op=ALU.mult),
                    r=[buf("sg%d" % sgi)], w=[bPF[ubank], buf("actT%d" % c)], n=n)

        pre = 0
        if stage >= 2 and g > 0:
            for c in range(1):
                p2a_front(c)
            pre = 1
        S.barrier()

        if stage >= 2:
            if g == 0:
                ring_load(0)
                ring_load(1)
            for c in range(pre):
                p2a_back(c)
            for c in range(pre, NFC):
                blk = c // 2
                cc = c % 2
                p2a_front(c)
                p2a_back(c)
                if cc == 1:
                    if blk + 2 < NFC // 2:
                        ring_load(blk + 2)
                    wd_load(2 * blk)
                    wd_load(2 * blk + 1)

            def p_load(i, rows):
                pr = i % 2
                psrc = p_p[(gtile0 + i) * 128:(gtile0 + i + 1) * 128, :] if i < 4 else p_s[:, :]
                S.add("pool", lambda e, rows=rows, psrc=psrc, pr=pr: e.dma_start(out=pbf_[pr][0:rows, :], in_=psrc),
                      w=[buf("pbf%d" % pr)], dma="pbf%d" % pr, n=rows * 1024)

            if stage >= 3:
                p_load(0, 128)
                p_load(1, 128)
            for ti, (i, rows) in enumerate(tiles):
                c0 = i * 128
                banks = (1, 2) if ti % 2 == 0 else (5, 6)
                S.begin("pe")
                for hf in range(2):
                    for c in range(NFC):
                        S.add("pe", lambda e, c=c, hf=hf, c0=c0, rows=rows, banks=banks: e.matmul(
                            PF[banks[hf]][0:rows, :], actT[:, c, c0:c0 + rows], wd[:, c, hf * 512:(hf + 1) * 512],
                            start=(c == 0), stop=(c == NFC - 1)), r=[buf("actT%d" % c), buf("wd%d" % c)], w=[bPF[banks[hf]]], n=512)
                S.end()
                post_norm_residual(i, rows, banks, 3, tmpfb[ti % 2], buf(("rg0", "ru0")[ti % 2]), junkb, buf("rg1"))

        p2b_done = S.snapshot()

        for (i, rows) in tiles:
            bx = buf("xs%d" % xsl(i))
            if stage >= 3:
                pr = i % 2
                x2b, x2T, pbf, pT, tf = x2b_[pr], x2T_[pr], pbf_[pr], pT_[pr], tmpf3_[pr]
                bx2b, bx2T, bpbf, bpT, btf = (buf("rg1"), buf("ru1"), buf("pbf%d" % pr), buf("pT%d" % pr),
                                              buf(("rg0", "ru0")[pr]))
                tbx, tbp = ((0, 7) if pr == 0 else (7, 0))
                if i >= 2:
                    p_load(i, rows)
                S.add("act", lambda e, i=i, rows=rows, x2b=x2b, xi=xsl(i): e.activation(out=x2b[0:rows, :], in_=xs[xi][0:rows, :], func=AF.Copy),
                      r=[bx], w=[bx2b], n=1024)
                transposes(x2b, rows, [bx2b], None, None, 8, psum=PSB[tbx], bps=bPF[tbx])
                S.add("act", lambda e, rows=rows, x2T=x2T, tbx=tbx: e.activation(
                    out=x2T.rearrange("p (k t) -> p k t", k=8)[:, :, 0:rows],
                    in_=PSB[tbx][:].rearrange("p (k t) -> p k t", k=8)[:, :, 0:rows], func=AF.Copy), w=[bPF[tbx], bx2T], n=8 * rows)
                transposes(pbf, rows, [bpbf], None, None, 2, psum=PSB[tbp], bps=bPF[tbp])
                S.add("dve", lambda e, rows=rows, pT=pT, tbp=tbp: e.tensor_copy(
                    out=pT.rearrange("p (k t) -> p k t", k=2)[:, :, 0:rows],
                    in_=PSB[tbp][:, 0:256].rearrange("p (k t) -> p k t", k=2)[:, :, 0:rows]), w=[bPF[tbp], bpT], n=2 * rows)
                x2Tv = x2T.rearrange("p (k t) -> p k t", k=8)
                pTv = pT.rearrange("p (k t) -> p k t", k=2)
                gbk = (1, 2) if pr == 0 else (5, 6)
                for hf in range(2):
                    S.begin("pe")
                    for k in range(8):
                        S.add("pe", lambda e, k=k, hf=hf, rows=rows, gbk=gbk, x2Tv=x2Tv: e.matmul(
                            PF[gbk[hf]][0:rows, :], x2Tv[:, k, 0:rows], wpg[:, k, hf * 512:(hf + 1) * 512],
                            start=(k == 0), stop=(k == 7)), r=[bx2T, buf("wpg")], w=[bPF[gbk[hf]]], n=512)
                    S.end()
                    S.begin("pe")
                    for k in range(2):
                        S.add("pe", lambda e, k=k, hf=hf, rows=rows, pTv=pTv: e.matmul(
                            PF[3 + hf][0:rows, :], pTv[:, k, 0:rows], wpe[:, k, hf * 512:(hf + 1) * 512],
                            start=(k == 0), stop=(k == 1)), r=[bpT, buf("wpe")], w=[bPF[3 + hf]], n=512)
                    S.end()
                for hf in range(2):
                    S.add("act", lambda e, hf=hf, rows=rows, gbk=gbk, tf=tf: e.activation(
                        out=tf[0:rows, hf * 512:(hf + 1) * 512], in_=PF[gbk[hf]][0:rows, :], func=AF.Sigmoid),
                        w=[bPF[gbk[hf]], btf], n=512)
                    S.add("dve", lambda e, hf=hf, rows=rows, tf=tf: e.tensor_tensor(
                        out=tf[0:rows, hf * 512:(hf + 1) * 512], in0=PF[3 + hf][0:rows, :], in1=tf[0:rows, hf * 512:(hf + 1) * 512],
                        op=ALU.mult), r=[btf], w=[bPF[3 + hf], btf], n=512)
                S.add("pool", lambda e, i=i, rows=rows, tf=tf, xi=xsl(i): e.tensor_tensor(out=xs[xi][0:rows, :], in0=xs[xi][0:rows, :],
                                                                               in1=tf[0:rows, :], op=ALU.add),
                      r=[btf], w=[bx], n=1024)
            dst = y_p[(gtile0 + i) * 128:(gtile0 + i + 1) * 128, :] if i < 4 else y_s[:, :]
            S.add("sp", lambda e, i=i, rows=rows, dst=dst, xi=xsl(i): e.dma_start(out=dst, in_=xs[xi][0:rows, :]),
                  r=[bx], dma="o_xs%d" % xsl(i), out=True)

        S.barrier(prior=p2b_done)
        if g + 1 < NG:
            issue_x_loads(g + 1, only=(3, 4))

    S.emit(nc)
    return nc


_PROG = None


def kernel(x_prompt, x_sample, cache_k, cache_v, state_pool, p_prompt, p_sample, rel_bias_table,
           g_mix_pre, w_in, attn_sinks, w_pool, pool_scale, w_out, g_mix_post, g_ffn_pre,
           w_ffn_gate, w_ffn_up, w_ffn_down, g_ffn_post, w_ple, w_ple_gate):
    global _PROG
    f = lambda a: np.ascontiguousarray(np.asarray(a, dtype=np.float32))
    x_prompt, x_sample = f(x_prompt), f(x_sample)
    consts = _consts()
    shared = {
        "table": f(rel_bias_table), "g_mix_pre": f(g_mix_pre), "g_mix_post": f(g_mix_post),
        "g_ffn_pre": f(g_ffn_pre), "g_ffn_post": f(g_ffn_post), "w_in": f(w_in)[0], "sinks": f(attn_sinks),
        "w_pool": f(w_pool)[0], "pool_scale": f(pool_scale).reshape(4, 128), "w_out": f(w_out)[0],
        "w_gate": f(w_ffn_gate)[0], "w_up": f(w_ffn_up)[0], "w_down": f(w_ffn_down)[0],
        "w_ple": f(w_ple)[0], "w_pg": f(w_ple_gate)[0],
    }
    shared.update(consts)
    ck, cv, sp = f(cache_k)[0], f(cache_v)[0], f(state_pool)[0]
    pp, ps = f(p_prompt)[0], f(p_sample)[0]
    in_maps = []
    for c in range(8):
        m = dict(shared)
        m["x_p"] = x_prompt[c]
        m["x_s"] = x_sample[4 * c:4 * c + 4].reshape(NS, D)
        m["ck"] = ck[4 * c:4 * c + 4].reshape(4, 128, 128)
        m["cv"] = cv[4 * c:4 * c + 4].reshape(4, 128, 128)
        m["spool"] = sp[4 * c:4 * c + 4]
        m["p_p"] = pp[c]
        m["p_s"] = ps[4 * c:4 * c + 4].reshape(NS, PLE)
        in_maps.append({k: np.ascontiguousarray(v) for k, v in m.items()})
    if _PROG is None:
        _PROG = build_program()
    res = run_bass_kernel_spmd(_PROG, in_maps, core_ids=list(range(8)))
    R = res.results
    y_prompt = np.stack([R[c]["y_p"] for c in range(8)]).astype(np.float32)
    y_sample = np.concatenate([R[c]["y_s"].reshape(4, 16, D) for c in range(8)]).astype(np.float32)
    nkp = np.stack([R[c]["nk_p"].reshape(128, 2, 64) for c in range(8)])[None].astype(np.float32)
    nvp = np.stack([R[c]["nv_p"].reshape(128, 2, 64) for c in range(8)])[None].astype(np.float32)
    npp = np.stack([R[c]["npool_p"] for c in range(8)])[None].astype(np.float32)
    nks = np.concatenate([R[c]["nk_s"].reshape(4, 16, 2, 64) for c in range(8)])[None].astype(np.float32)
    nvs = np.concatenate([R[c]["nv_s"].reshape(4, 16, 2, 64) for c in range(8)])[None].astype(np.float32)
    nps = np.concatenate([R[c]["npool_s"] for c in range(8)])[None].astype(np.float32)
    return (y_prompt, y_sample, nkp, nvp, npp, nks, nvs, nps)
```

```python
import math
import numpy as np
import concourse.bass as bass
import concourse.mybir as mybir
from concourse.bass_utils import run_bass_kernel_spmd

F32 = mybir.dt.float32
BF16 = mybir.dt.bfloat16
AF = mybir.ActivationFunctionType
ALU = mybir.AluOpType
AX = mybir.AxisListType

D = 1024
S_LEN = 2048
NQH = 8
HD = 64
DFF = 2816
NFC = DFF // 128
PLE = 256
NG = 4
GT = 512
NS = 64
EPS = 1e-6
POOL_W = (2, 4, 8, 16)
N_BUCKETS = 32
MAX_DISTANCE = 128


class Buf:
    __slots__ = ("name", "w", "r")

    def __init__(self, name):
        self.name = name
        self.w = None
        self.r = []


class Op:
    __slots__ = ("eng", "fns", "deps", "ords", "sig", "val", "dma", "idx", "cost", "marker", "fin", "pos", "prio")

    def __init__(self, eng, dma):
        self.eng = eng
        self.fns = []
        self.deps = {}
        self.ords = set()
        self.sig = False
        self.val = 0
        self.dma = dma
        self.idx = 0
        self.cost = 0.0
        self.marker = False
        self.fin = 0.0
        self.pos = 0
        self.prio = None


class Sched:
    ENGS = ("pe", "act", "dve", "pool", "sp")
    LAT = 130.0

    def __init__(self):
        self.ops = []
        self.dma_last = {}
        self.out_keys = set()
        self.cur_bar = {e: None for e in self.ENGS}
        self.since_bar = []
        self.grp = None
        self.q = None

    @staticmethod
    def _cost(eng, n, dma):
        if dma is not None:
            return 2200.0 + n / 200.0
        if eng == "pe":
            return 64.0 + max(n, 32) / 2.4
        if eng == "act":
            return 230.0 + n * 0.84
        if eng == "dve":
            return 130.0 + n * 1.05
        if eng == "pool":
            return 250.0 + n * 2.2
        return 100.0

    def begin(self, eng):
        op = Op(eng, None)
        op.idx = len(self.ops)
        self.ops.append(op)
        self.since_bar.append(op)
        if self.cur_bar[eng] is not None:
            op.ords.add(self.cur_bar[eng])
        self.grp = op

    def end(self):
        self.grp = None

    def add(self, eng, fn, r=(), w=(), dma=None, out=False, n=128, serial=False):
        if self.grp is not None:
            op = self.grp
            assert op.eng == eng and dma is None
        else:
            op = Op(eng, dma)
            op.idx = len(self.ops)
            self.ops.append(op)
            self.since_bar.append(op)
            if self.cur_bar[eng] is not None:
                op.ords.add(self.cur_bar[eng])
        op.fns.append(fn)
        op.cost += self._cost(eng, n, dma)
        for b in r:
            if b.w is not None and b.w is not op:
                op.deps[b.w] = "RAW"
        for b in w:
            if b.w is not None and b.w is not op and b.w not in op.deps:
                op.deps[b.w] = "WAW"
            for o in b.r:
                if o is not op and o not in op.deps:
                    op.deps[o] = "WAR"
        if dma is not None:
            prev = self.dma_last.get(dma)
            if prev is not None:
                if serial and prev not in op.deps:
                    op.deps[prev] = "RAW"
                op.ords.add(prev)
            self.dma_last[dma] = op
            if out:
                self.out_keys.add(dma)
        for b in r:
            if not b.r or b.r[-1] is not op:
                b.r.append(op)
        for b in w:
            b.w = op
            b.r = []
        return op

    def snapshot(self):
        return list(self.since_bar)

    def barrier(self, prior=None):
        if prior is None:
            prior = list(self.since_bar)
            self.since_bar = []
        else:
            ps = set(prior)
            self.since_bar = [o for o in self.since_bar if o not in ps]
        for e in self.ENGS:
            op = Op(e, None)
            op.marker = True
            op.idx = len(self.ops)
            self.ops.append(op)
            for o in prior:
                op.deps[o] = "BAR"
            if self.cur_bar[e] is not None:
                op.ords.add(self.cur_bar[e])
            self.cur_bar[e] = op

    def _keep(self, op, dep, kind):
        if dep.marker:
            return False
        if dep.dma is not None or op.dma is not None:
            return True
        if dep.eng != op.eng:
            return True
        if op.marker:
            return False
        if op.eng == "pe":
            return False
        return True

    def schedule(self):
        import heapq
        ops = self.ops
        succ = {}
        indeg = {}
        for op in ops:
            alld = set(op.deps.keys()) | op.ords
            indeg[op] = len(alld)
            for d in alld:
                succ.setdefault(d, []).append(op)
        free_at = {e: 0.0 for e in self.ENGS}
        pending = {e: [] for e in self.ENGS}
        avail = {e: [] for e in self.ENGS}
        q = {e: [] for e in self.ENGS}

        def ready_time(op):
            t = 0.0
            for d in list(op.deps.keys()) + list(op.ords):
                lat = 0.0 if (d.eng == op.eng and d.dma is None and op.eng == "pe") else self.LAT
                if d.marker:
                    lat = 0.0
                t = max(t, d.fin + lat)
            return t

        cp = {}
        outdeg = {op: len(succ.get(op, ())) for op in ops}
        preds = {op: list(set(op.deps.keys()) | op.ords) for op in ops}
        stack = [op for op in ops if outdeg[op] == 0]
        rtopo = []
        while stack:
            o_ = stack.pop()
            rtopo.append(o_)
            for p_ in preds[o_]:
                outdeg[p_] -= 1
                if outdeg[p_] == 0:
                    stack.append(p_)
        assert len(rtopo) == len(ops)
        for op in rtopo:
            m = 0.0
            for s_ in succ.get(op, ()):
                v = cp[s_] + (0.0 if (s_.marker or op.marker) else self.LAT)
                if v > m:
                    m = v
            cp[op] = (0.0 if op.marker else op.cost) + m
        for op in ops:
            if op.prio is not None and op.prio >= 3000.0:
                op.prio = 1e12 + op.prio
            else:
                op.prio = -cp[op]
        for op in ops:
            if indeg[op] == 0:
                heapq.heappush(pending[op.eng], (0.0, op.prio, op.idx, op))
        nleft = len(ops)
        while nleft:
            best = None
            for e in self.ENGS:
                pe_, av = pending[e], avail[e]
                while pe_ and pe_[0][0] <= free_at[e]:
                    rt, pr, idx, op = heapq.heappop(pe_)
                    heapq.heappush(av, (pr, idx, rt, op))
                if av:
                    pr, idx, rt, op = av[0]
                    start = max(free_at[e], rt)
                    cand = (start, pr, idx, e, True)
                elif pe_:
                    rt, pr, idx, op = pe_[0]
                    cand = (max(free_at[e], rt), pr, idx, e, False)
                else:
                    continue
                if best is None or cand < best:
                    best = cand
            assert best is not None, "scheduler deadlock (dependency cycle?)"
            start, pr, idx, e, from_av = best
            if from_av:
                pr, idx, rt, op = heapq.heappop(avail[e])
            else:
                rt, pr, idx, op = heapq.heappop(pending[e])
            if op.marker:
                op.fin = start
                free_at[e] = start
            elif op.dma is not None:
                op.fin = start + op.cost
                free_at[e] = start + (900.0 if e == "pool" else 80.0)
            else:
                op.fin = start + op.cost
                free_at[e] = op.fin
            op.pos = len(q[e])
            q[e].append(op)
            nleft -= 1
            for s_ in succ.get(op, ()):
                indeg[s_] -= 1
                if indeg[s_] == 0:
                    heapq.heappush(pending[s_.eng], (ready_time(s_), s_.prio, s_.idx, s_))
        self.q = q
        self.est_ns = max(free_at.values())

    def emit(self, nc, do_schedule=True):
        if do_schedule:
            self.schedule()
        else:
            self.q = {e: [o for o in self.ops if o.eng == e] for e in self.ENGS}
            for e in self.ENGS:
                for i, o in enumerate(self.q[e]):
                    o.pos = i
        for op in self.ops:
            if op.marker:
                last = {}
                for d in op.deps:
                    if d.marker:
                        continue
                    key = ("d", d.dma) if d.dma is not None else ("e", d.eng)
                    if key not in last or d.pos > last[key].pos:
                        last[key] = d
                op.deps = {d: "BAR" for d in last.values() if not (d.dma is None and d.eng == op.eng)}
        for op in self.ops:
            for dep, kind in op.deps.items():
                if self._keep(op, dep, kind):
                    dep.sig = True
        cnt = {e: 0 for e in self.ENGS}
        dcnt = {}
        for e in self.ENGS:
            for op in self.q[e]:
                if op.marker:
                    continue
                if op.dma is not None:
                    dcnt[op.dma] = dcnt.get(op.dma, 0) + 16
                    op.val = dcnt[op.dma]
                elif op.sig:
                    cnt[e] += 1
                    op.val = cnt[e]
        import contextlib
        with contextlib.ExitStack() as st:
            esem = {e: st.enter_context(nc.semaphore("sem_" + e)) for e in self.ENGS}
            dsem = {k: st.enter_context(nc.semaphore("dsem_" + str(k))) for k in dcnt}
            block = st.enter_context(nc.Block())

            def run(ename, eng):
                waited = {}
                for op in self.q[ename]:
                    need = {}
                    for dep, kind in op.deps.items():
                        if not self._keep(op, dep, kind):
                            continue
                        sem = dsem[dep.dma] if dep.dma is not None else esem[dep.eng]
                        if need.get(sem.name, (None, 0))[1] < dep.val:
                            need[sem.name] = (sem, dep.val)
                    for sname, (sem, val) in need.items():
                        if waited.get(sname, 0) >= val:
                            continue
                        eng.wait_ge(sem, val)
                        waited[sname] = val
                    ins = None
                    for fn in op.fns:
                        ins = fn(eng)
                    if ins is None:
                        continue
                    if op.dma is not None:
                        ins.then_inc(dsem[op.dma], 16)
                    elif op.sig:
                        ins.then_inc(esem[ename], 1)
                if ename == "sp":
                    for k in sorted(self.out_keys, key=str):
                        eng.wait_ge(dsem[k], dcnt[k])

            block.tensor(lambda e: run("pe", e))
            block.scalar(lambda e: run("act", e))
            block.vector(lambda e: run("dve", e))
            block.gpsimd(lambda e: run("pool", e))
            block.sync(lambda e: run("sp", e))


def _t5_bucket_np(rel):
    half = N_BUCKETS // 2
    max_exact = half // 2
    ret = np.where(rel > 0, half, 0)
    n = np.abs(rel)
    nf = np.maximum(n, 1).astype(np.float32)
    large = max_exact + (np.log(nf / np.float32(max_exact)) / np.float32(math.log(MAX_DISTANCE / max_exact))
                         * np.float32(half - max_exact)).astype(np.int32)
    large = np.minimum(large, half - 1)
    return ret + np.where(n < max_exact, n, large)


def _consts():
    rel = np.arange(256) - 191
    bk = _t5_bucket_np(rel)
    onehot = np.zeros((32, 256), np.float32)
    onehot[bk, np.arange(256)] = 1.0
    identf = np.eye(128, dtype=np.float32)
    j2 = np.zeros((128, 128), np.float32)
    for p in range(64):
        j2[p, 63 - p] = 1.0
        j2[64 + p, 127 - p] = 1.0
    invc = np.zeros((128, 4, 16), np.float32)
    for g, w in enumerate(POOL_W):
        for pos in range(16):
            invc[:, g, pos] = 1.0 / min(pos + 1, w)
    return {"c_onehot": onehot, "c_ident": identf, "c_j2": j2, "c_invc": invc.reshape(128, 64)}


def build_program(stage=3):
    nc = bass.Bass("TRN2", target_bir_lowering=False)
    S = Sched()

    def din(name, shape):
        return nc.dram_tensor(name, list(shape), F32, kind="ExternalInput").ap()

    def dout(name, shape):
        return nc.dram_tensor(name, list(shape), F32, kind="ExternalOutput").ap()

    x_p = din("x_p", (S_LEN, D)); x_s = din("x_s", (NS, D))
    ck = din("ck", (4, 128, 128)); cv = din("cv", (4, 128, 128)); spool = din("spool", (4, 15, 512))
    p_p = din("p_p", (S_LEN, PLE)); p_s = din("p_s", (NS, PLE))
    table = din("table", (32, 8))
    g_mix_pre = din("g_mix_pre", (1, D)); g_mix_post = din("g_mix_post", (1, D))
    g_ffn_pre = din("g_ffn_pre", (1, D)); g_ffn_post = din("g_ffn_post", (1, D))
    w_in = din("w_in", (D, 1280)); sinks = din("sinks", (1, 8))
    w_pool = din("w_pool", (4, 128, 128)); pool_scale = din("pool_scale", (4, 128))
    w_out = din("w_out", (D, D)); w_gate = din("w_gate", (D, DFF)); w_up = din("w_up", (D, DFF))
    w_down = din("w_down", (DFF, D)); w_ple = din("w_ple", (PLE, D)); w_pg = din("w_pg", (D, D))
    c_onehot = din("c_onehot", (32, 256)); c_ident = din("c_ident", (128, 128))
    c_j2 = din("c_j2", (128, 128)); c_invc = din("c_invc", (128, 64))

    y_p = dout("y_p", (S_LEN, D)); y_s = dout("y_s", (NS, D))
    nk_p = dout("nk_p", (128, 128)); nv_p = dout("nv_p", (128, 128)); npool_p = dout("npool_p", (15, 512))
    nk_s = dout("nk_s", (NS, 128)); nv_s = dout("nv_s", (NS, 128)); npool_s = dout("npool_s", (4, 15, 512))
    rb_dram = nc.dram_tensor("rb_scratch", [8, 256], F32).ap()
    sc_g = nc.dram_tensor("sc_gate", [11, 128, 2048], BF16).ap()
    sc_u = nc.dram_tensor("sc_up", [11, 128, 2048], BF16).ap()
    sc_d = nc.dram_tensor("sc_down", [NFC, 128, 1024], BF16).ap()

    sb = nc.alloc_sbuf_tensor
    wA = sb("wA", [128, 8, 1408], BF16)
    wo = sb("wo", [128, 8, 1024], BF16)
    wpl = sb("wpl", [128, 4, 128], BF16)
    wpg = sb("wpg", [128, 8, 1024], BF16)
    wpe = sb("wpe", [128, 2, 1024], BF16)
    gb = {i: sb("gb%d" % i, [128, 1024], F32) for i in (1, 3)}
    gcol = {0: sb("gcol0", [128, 8], F32), 2: sb("gcol2", [128, 8], F32)}
    xs = [sb("xs%d" % i, [128, 1024], F32) for i in range(7)]
    hT = sb("hT", [128, 8, 576], BF16)
    actT = sb("actT", [128, NFC, 576], BF16)
    wd = sb("wd", [128, NFC, 1024], BF16)
    ring = [(sb("rg%d" % i, [128, 8, 256], BF16), sb("ru%d" % i, [128, 8, 256], BF16)) for i in range(2)]
    kT = sb("kT", [128, 128 + 512], BF16)
    VdA = sb("VdA", [128, 5, 256], BF16)
    VdB = sb("VdB", [128, 5, 256], BF16)
    bias = sb("bias", [128, 4, 192], F32)
    ident = sb("ident", [128, 128], BF16)
    identf = sb("identf", [128, 128], F32)
    sinkc = sb("sinkc", [128, 4], F32)
    pscol = sb("pscol", [128, 4], F32)
    invc = sb("invc", [128, 64], F32)
    ucarry = sb("ucarry", [128, 4, 16], F32)
    stats = sb("stats", [128, 512], F32)
    sg = [sb("sg%d" % i, [128, 512], F32) for i in range(2)]
    pbf_ = [sb("pbf%d" % i, [128, 256], BF16) for i in range(2)]
    pT_ = [sb("pT%d" % i, [128, 256], BF16) for i in range(2)]

    regions = [wd[:].rearrange("p a b -> p (a b)"), actT[:].rearrange("p a b -> p (a b)"),
               ring[0][0][:].rearrange("p a b -> p (a b)"), ring[0][1][:].rearrange("p a b -> p (a b)"),
               ring[1][0][:].rearrange("p a b -> p (a b)"), ring[1][1][:].rearrange("p a b -> p (a b)")]
    rsize = [NFC * 1024, NFC * 576, 2048, 2048, 2048, 2048]
    roff = [0] * len(regions)

    def new_pass():
        for i in range(len(roff)):
            roff[i] = 0

    def carve(nbytes, dtype):
        n16 = (nbytes + 63) // 64 * 32
        for ri in range(len(regions)):
            if roff[ri] + n16 <= rsize[ri]:
                a = roff[ri]
                roff[ri] += n16
                v = regions[ri][:, a:a + nbytes // 2]
                if dtype == F32:
                    v = v.bitcast(F32)
                return v
        raise AssertionError("carve: out of transient space")

    roff[0] = rsize[0]; roff[1] = rsize[1]
    Hk = carve(192 * 4, F32)
    rsb = carve(256 * 4, F32)
    tb32 = carve(8 * 4, F32)
    oh = carve(256 * 4, F32)
    j2 = carve(128 * 4, F32)
    gtmp = carve(128 * 4, F32)
    gtmp2 = carve(128 * 4, F32)
    stqa = carve(8 * 256 * 2, BF16)
    stqb = carve(8 * 256 * 2, BF16)
    stv = carve(8 * 128 * 2, BF16)
    new_pass()
    for _ri in (0, 1):
        roff[_ri] = rsize[_ri]
    tmpfb = [carve(4096, F32) for _ in range(2)]
    junkb = carve(2048, BF16)
    new_pass()
    for _ri in (0, 1):
        roff[_ri] = rsize[_ri]
    tmpf3_ = [carve(4096, F32) for _ in range(2)]
    x2b_ = [carve(2048, BF16) for _ in range(2)]
    x2T_ = [carve(2048, BF16) for _ in range(2)]
    new_pass()
    for _ri in (2, 3, 4, 5):
        roff[_ri] = rsize[_ri]
    hb = [carve(2048, BF16) for _ in range(2)]
    junks = [carve(2048, BF16) for _ in range(2)]
    junk = junks[0]
    qT2 = carve(12 * 4 * 64 * 2, BF16)
    uT = carve(4 * 528 * 4, F32)
    tmpf = [uT[:, 0:1024], uT[:, 1024:2048]]
    uTs = carve(4 * 4 * 32 * 4, F32)
    pa = carve(528 * 4, F32); pb_ = carve(528 * 4, F32)
    dT = carve(4 * 576 * 2, BF16)
    aT = carve(4 * 576 * 2, BF16)
    plT = carve(4 * 576 * 2, BF16)
    Sb = [carve(2 * 193 * 4, F32) for _ in range(2)]
    Pf = [carve(2 * 193 * 4, F32) for _ in range(2)]
    Pn = [carve(2 * 192 * 2, BF16) for _ in range(2)]
    PnT = [carve(512 * 2, BF16) for _ in range(2)]
    ksT = carve(4 * 144 * 2, BF16)
    ckb = carve(4 * 128 * 2, BF16)
    Vsc = carve(4 * 256 * 2, BF16)
    Vsn = carve(4 * 256 * 2, BF16)
    stg = carve(896 * 4, F32)
    spx = carve(512 * 4, F32)

    PSB = [nc.alloc_psum_tensor("PS%d" % i, [128, 1024], BF16) for i in range(8)]
    PF = {i: PSB[i][:].bitcast(F32) for i in range(8)}
    PB0 = PSB[0]
    PB5 = PSB[7]

    B = {}

    def buf(name):
        if name not in B:
            B[name] = Buf(name)
        return B[name]

    bPF = {i: buf("PS%d" % i) for i in range(8)}
    bPB0, bPB5 = bPF[0], bPF[7]
    stat_col = [0]
    NSG = 64
    stat_bufs = [Buf("st%d" % i) for i in range(NSG)]

    def new_stat_group():
        gi_ = stat_col[0] % NSG
        stat_col[0] += 1
        return gi_ * 8, stat_bufs[gi_]

    def rstd_from_ms(c, rows, bst):
        ms = stats[:, c:c + 1]; t = stats[:, c + 2:c + 3]; l = stats[:, c + 3:c + 4]; rstd = stats[:, c + 4:c + 5]
        S.add("dve", lambda e: e.tensor_scalar(out=t[0:rows, :], in0=ms[0:rows, :], scalar1=EPS, scalar2=None, op0=ALU.add),
              r=[bst], w=[bst], n=1)
        S.add("act", lambda e: e.activation(out=l[0:rows, :], in_=t[0:rows, :], func=AF.Ln), r=[bst], w=[bst], n=1)
        S.add("act", lambda e: e.activation(out=rstd[0:rows, :], in_=l[0:rows, :], func=AF.Exp, scale=-0.5), r=[bst], w=[bst], n=1)
        return rstd, bst

    S.add("pool", lambda e: e.memset(ucarry[:].rearrange("p a b -> p (a b)"), 0.0), w=[buf("ucarry")])
    S.add("pool", lambda e: e.memset(VdA[:, 0, :], 0.0), w=[buf("VdA0")])
    S.add("pool", lambda e: e.memset(kT[:, 0:128], 0.0), w=[buf("kTc")])
    S.add("sp", lambda e: e.dma_start(out=identf[:], in_=c_ident[:, :]), w=[buf("identf")], dma="identf")
    S.add("dve", lambda e: e.tensor_copy(out=ident[:], in_=identf[:]), r=[buf("identf")], w=[buf("ident")])
    S.add("sp", lambda e: e.dma_start(out=invc[:], in_=c_invc[:, :]), w=[buf("invc")], dma="invc")
    for i, gsrc in ((1, g_mix_post), (3, g_ffn_post)):
        S.add("sp", lambda e, i=i, gsrc=gsrc: e.dma_start(out=gb[i][:], in_=bass.AP(tensor=gsrc.tensor, offset=0, ap=[[0, 128], [1, 1024]])),
              w=[buf("gb%d" % i)], dma="gb%d" % i)
    for i, gsrc in ((0, g_mix_pre), (2, g_ffn_pre)):
        S.add("sp", lambda e, gsrc=gsrc: e.dma_start(out=gtmp2[0:8, 0:128], in_=bass.AP(tensor=gsrc.tensor, offset=0, ap=[[128, 8], [1, 128]])),
              w=[buf("gtmp2")], dma="gtmp2")
        S.add("pe", lambda e: e.matmul(PF[4][:, 0:8], gtmp2[0:8, 0:128], identf[0:8, 0:8], start=True, stop=True),
              r=[buf("gtmp2"), buf("identf")], w=[bPF[4]])
        S.add("dve", lambda e, i=i: e.tensor_copy(out=gcol[i][:], in_=PF[4][:, 0:8]), w=[bPF[4], buf("gcol%d" % i)])
    for pi in range(4):
        for hh in range(2):
            h = 2 * pi + hh
            S.add("sp", lambda e, pi=pi, hh=hh, h=h: e.dma_start(
                out=sinkc[hh * 64:(hh + 1) * 64, pi:pi + 1], in_=bass.AP(tensor=sinks.tensor, offset=h, ap=[[0, 64], [1, 1]])),
                w=[buf("sinkc")], dma="sinkc")
    S.add("sp", lambda e: e.dma_start(out=gtmp[0:4, 0:128], in_=pool_scale[:, :]), w=[buf("gtmp")], dma="gtmp")
    S.add("pe", lambda e: e.matmul(PF[1][:, 0:4], gtmp[0:4, 0:128], identf[0:4, 0:4], start=True, stop=True),
          r=[buf("gtmp"), buf("identf")], w=[bPF[1]])
    S.add("dve", lambda e: e.tensor_copy(out=pscol[:], in_=PF[1][:, 0:4]), w=[bPF[1], buf("pscol")])

    def wload(dst_ap, src_ap, key, bname):
        S.add("pool", lambda e: e.dma_start(out=dst_ap, in_=src_ap), w=[buf(bname)], dma=key)

    w_in_v = w_in.rearrange("(k p) c -> p k c", p=128)
    stqa3 = stqa.rearrange("p (k c) -> p k c", k=8)
    stqb3 = stqb.rearrange("p (k c) -> p k c", k=8)
    stv3 = stv.rearrange("p (k c) -> p k c", k=8)
    wload(stqa3, w_in_v[:, :, 0:256], "stqa", "stqa")
    wload(stqb3, w_in_v[:, :, 256:512], "stqb", "stqb")
    wload(wA[:, :, 512:640], w_in_v[:, :, 512:640], "wAk", "wAk")
    wload(stv3, w_in_v[:, :, 640:768], "stv", "stv")
    wload(wA[:, :, 896:1408], w_in_v[:, :, 768:1280], "wAu", "wAu")
    S.add("dve", lambda e: e.tensor_copy(out=wA[:, :, 0:512].rearrange("p k (j c) -> p k j c", j=4)[:, :, :, 0:64],
                                         in_=stqa3.rearrange("p k (j c) -> p k j c", j=4)), r=[buf("stqa")], w=[buf("wAq")], n=2048)
    S.add("pool", lambda e: e.tensor_copy(out=wA[:, :, 0:512].rearrange("p k (j c) -> p k j c", j=4)[:, :, :, 64:128],
                                          in_=stqb3.rearrange("p k (j c) -> p k j c", j=4)), r=[buf("stqb")], w=[buf("wAq")], n=2048)
    for kv in range(2):
        for dup in range(2):
            c0 = 640 + kv * 128 + dup * 64
            eng = "dve" if dup == 0 else "pool"
            S.add(eng, lambda e, kv=kv, c0=c0: e.tensor_copy(out=wA[:, :, c0:c0 + 64], in_=stv3[:, :, kv * 64:(kv + 1) * 64]),
                  r=[buf("stv")], w=[buf("wAv")], n=512)
    wload(wpl[:], w_pool.rearrange("g d e -> d g e"), "wpl", "wpl")
    wload(wo[:], w_out.rearrange("(k p) c -> p k c", p=128), "wo", "wo")
    wgv = w_gate.rearrange("(k p) c -> p k c", p=128)
    wuv = w_up.rearrange("(k p) c -> p k c", p=128)
    conv_ops = []
    for blk in range(11):
        conv_ops.append(S.add("pool", lambda e, blk=blk: e.dma_start(out=sc_g[blk].rearrange("p (k c) -> p k c", k=8),
                                                                     in_=wgv[:, :, blk * 256:(blk + 1) * 256]),
                              w=[buf("scg")], dma="cvg", n=1048576))
        conv_ops.append(S.add("pool", lambda e, blk=blk: e.dma_start(out=sc_u[blk].rearrange("p (k c) -> p k c", k=8),
                                                                     in_=wuv[:, :, blk * 256:(blk + 1) * 256]),
                              w=[buf("scu")], dma="cvu", n=1048576))
    for q4 in range(2):
        conv_ops.append(S.add("pool", lambda e, q4=q4: e.dma_start(out=sc_d[q4 * 11:(q4 + 1) * 11].rearrange("c p n -> (c p) n"),
                                                                   in_=w_down[q4 * 1408:(q4 + 1) * 1408, :]),
                              w=[buf("scd")], dma="cvd", n=5767168))

    S.add("sp", lambda e: e.dma_start(out=tb32[0:32, 0:8], in_=table[:, :]), w=[buf("tb32")], dma="tb32")
    S.add("sp", lambda e: e.dma_start(out=oh[0:32, 0:256], in_=c_onehot[:, :]), w=[buf("oh")], dma="oh")
    S.add("sp", lambda e: e.dma_start(out=j2[:], in_=c_j2[:, :]), w=[buf("j2")], dma="j2")
    S.add("pe", lambda e: e.matmul(PF[2][0:8, 0:256], tb32[0:32, 0:8], oh[0:32, 0:256], start=True, stop=True),
          r=[buf("tb32"), buf("oh")], w=[bPF[2]])
    S.add("dve", lambda e: e.tensor_copy(out=rsb[0:8, 0:256], in_=PF[2][0:8, 0:256]), w=[bPF[2], buf("rsb")])
    S.add("sp", lambda e: e.dma_start(out=rb_dram[:, :], in_=rsb[0:8, 0:256]), r=[buf("rsb")], w=[buf("rb_dram")], dma="rbw")
    Hks = [sg[0][:, 0:192], sg[0][:, 192:384], sg[1][:, 0:192], sg[1][:, 192:384]]
    for pi in range(4):
        for hh in range(2):
            h = 2 * pi + hh
            src = bass.AP(tensor=rb_dram.tensor, offset=h * 256, ap=[[1, 64], [1, 192]])
            S.add("sp", lambda e, hh=hh, src=src, pi=pi: e.dma_start(out=Hks[pi][hh * 64:(hh + 1) * 64, :], in_=src),
                  r=[buf("rb_dram")], w=[buf("Hk%d" % pi)], dma="Hk%d" % pi)
    for pi in range(4):
        pbk = 3 + (pi % 2)
        S.add("pe", lambda e, pi=pi, pbk=pbk: e.matmul(PF[pbk][:, 0:192], j2[:], Hks[pi], start=True, stop=True),
              r=[buf("j2"), buf("Hk%d" % pi)], w=[bPF[pbk]])
        S.add("dve", lambda e, pi=pi, pbk=pbk: e.tensor_copy(out=bias[:, pi, :], in_=PF[pbk][:, 0:192]),
              w=[bPF[pbk], buf("bias")])

    for ci, o in enumerate(conv_ops):
        o.prio = 3000.0 + ci
        for bn in ("bias", "wo", "wAu", "wAk", "stqa", "stqb", "stv", "wpl", "xs0", "xs1", "xs2", "xs3"):
            w_ = B[bn].w if bn in B else None
            if w_ is not None and w_ is not o:
                o.deps[w_] = "RAW"
    jsel = [0]

    def rms_stats(src_ap, rows, bsrc, eng_sq="act"):
        c, bst = new_stat_group()
        ms = stats[:, c:c + 1]
        jsel[0] ^= 1
        jk = junks[jsel[0]]; bjk = buf("junk%d" % jsel[0])
        S.add("act", lambda e: e.activation(out=jk[0:rows, :], in_=src_ap, func=AF.Square, scale=1.0 / 32.0,
                                            accum_out=ms[0:rows, :]),
              r=bsrc, w=[bjk, bst], n=1024)
        return rstd_from_ms(c, rows, bst)

    def transposes(src_tile, rows, bsrc, dst_fn, bdst, nk, psum=PB0, bps=None):
        bps = bps or bPB0
        S.begin("pe")
        for k in range(nk):
            S.add("pe", lambda e, k=k: e.transpose(psum[:, k * 128:k * 128 + rows], src_tile[0:rows, k * 128:(k + 1) * 128],
                                                   ident[0:rows, 0:rows]),
                  r=bsrc + [buf("ident")], w=[bps], n=rows)
        S.end()
        return bps

    def issue_x_loads(g, only=None):
        tl = [(i, 128) for i in range(4)] + ([(4, NS)] if g == NG - 1 else [])
        if only is not None:
            tl = [t_ for t_ in tl if t_[0] in only]
        for (i, rows) in tl:
            src = x_p[(4 * g + i) * 128:(4 * g + i + 1) * 128, :] if i < 4 else x_s[:, :]
            si = (4 * g + i) % 7
            S.add("sp", lambda e, si=si, rows=rows, src=src: e.dma_start(out=xs[si][0:rows, :], in_=src),
                  w=[buf("xs%d" % si)], dma="xs%d" % si, n=rows * 4096)

    hw = 0
    for g in range(NG):
        has_s = (g == NG - 1)
        xsl = (lambda i, g=g: (4 * g + i) % 7)
        T = GT + (NS if has_s else 0)
        halves = [(0, 320), (320, 576)] if has_s else [(0, 256), (256, 512)]
        tiles = [(i, 128) for i in range(4)] + ([(4, NS)] if has_s else [])
        gtile0 = 4 * g

        if has_s:
            S.add("pool", lambda e: e.memset(qT2[:, 8 * 256:12 * 256], 0.0), w=[buf("qT2s")])
        if g == 0:
            issue_x_loads(0)
        wgv = w_gate.rearrange("(k p) c -> p k c", p=128)
        wuv = w_up.rearrange("(k p) c -> p k c", p=128)

        def ring_load(blk):
            rg, ru = ring[blk % 2]
            rgf = rg[:].rearrange("p k c -> p (k c)")
            ruf = ru[:].rearrange("p k c -> p (k c)")
            bg, bu = buf("rg%d" % (blk % 2)), buf("ru%d" % (blk % 2))
            S.add("sp", lambda e, blk=blk, rgf=rgf: e.dma_start(out=rgf, in_=sc_g[blk, :, :]),
                  r=[buf("scg")], w=[bg], dma="rgh%d" % (blk % 2), n=524288)
            S.add("sp", lambda e, blk=blk, ruf=ruf: e.dma_start(out=ruf, in_=sc_u[blk, :, :]),
                  r=[buf("scu")], w=[bu], dma="ruh%d" % (blk % 2), n=524288)

        def wd_load(c):
            S.add("sp", lambda e, c=c: e.dma_start(out=wd[:, c, :], in_=sc_d[c, :, :]),
                  r=[buf("scd")], w=[buf("wd%d" % c)], dma="wdh%d" % (c % 4), n=262144, serial=True)

        if g > 0 and stage >= 2:
            ring_load(0)
            ring_load(1)
        for (i, rows) in tiles:
            bx = buf("xs%d" % xsl(i))
            rstd, brs = rms_stats(xs[xsl(i)][0:rows, :], rows, [bx])
            hbuf = hb[hw % 2]; bh = buf("hb%d" % (hw % 2)); hw += 1
            S.add("dve", lambda e, i=i, rows=rows, rstd=rstd, hbuf=hbuf, xi=xsl(i): e.tensor_scalar(
                out=hbuf[0:rows, :], in0=xs[xi][0:rows, :], scalar1=rstd[0:rows, :], scalar2=None, op0=ALU.mult),
                r=[bx, brs], w=[bh], n=512)
            tb = (0, 7)[i % 2]
            transposes(hbuf, rows, [bh], None, None, 8, psum=PSB[tb], bps=bPF[tb])
            c0 = i * 128
            S.add("dve", lambda e, c0=c0, rows=rows, tb=tb: e.tensor_tensor(
                out=hT[:, :, c0:c0 + rows], in0=PSB[tb][:].rearrange("p (k t) -> p k t", k=8)[:, :, 0:rows],
                in1=gcol[0][:].unsqueeze(2).to_broadcast([128, 8, rows]), op=ALU.mult),
                r=[buf("gcol0")], w=[bPF[tb], buf("hT%d" % i)], n=8 * rows)
        bhT_all = [buf("hT%d" % i) for (i, _) in tiles]

        qv = qT2.rearrange("p (c j q) -> p c j q", j=4, q=64)
        uv = uT.rearrange("p (g t) -> p g t", g=4)
        usv = uTs.rearrange("p (g b t) -> p g b t", g=4, b=4)
        ksv = ksT.rearrange("p (b t) -> p b t", b=4)
        S.add("dve", lambda e: e.tensor_copy(out=uv[:, :, 0:16], in_=ucarry[:]), r=[buf("ucarry")], w=[buf("uT")], n=64)
        def v_for_tile(i):
            if i < 4:
                S.begin("pe")
                for k in range(8):
                    S.add("pe", lambda e, i=i, k=k: e.matmul(PF[6][:, 0:256], hT[:, k, i * 128:(i + 1) * 128],
                                                             wA[:, k, 640:896], start=(k == 0), stop=(k == 7)),
                          r=[buf("wAv"), buf("hT%d" % i)], w=[bPF[6]], n=256)
                S.end()
                S.add("act", lambda e, i=i: e.activation(out=VdA[:, i + 1, :], in_=PF[6][:, 0:256], func=AF.Copy),
                      w=[bPF[6], buf("VdA%d" % (i + 1))], n=256)
                if i == 0:
                    S.add("sp", lambda e: e.dma_start(out=VdB[0:64, 0, :], in_=VdA[64:128, 0, :]),
                          r=[buf("VdA0")], w=[buf("VdBlo0")], dma="VdBlo0")
                t = i + 1
                S.add("sp", lambda e, t=t: e.dma_start(out=VdB[0:64, t, :], in_=VdA[64:128, t, :]),
                      r=[buf("VdA%d" % t)], w=[buf("VdBlo%d" % t)], dma="VdBlo%d" % t)
                S.add("sp", lambda e, t=t: e.dma_start(out=VdB[64:128, t - 1, :], in_=VdA[0:64, t, :]),
                      r=[buf("VdA%d" % t)], w=[buf("VdBhi%d" % (t - 1))], dma="VdBhi%d" % (t - 1))
            else:
                for b in range(4):
                    S.begin("pe")
                    for k in range(8):
                        S.add("pe", lambda e, b=b, k=k: e.matmul(PF[6][0:16, 0:256], hT[:, k, 512 + 16 * b:512 + 16 * b + 16],
                                                                 wA[:, k, 640:896], start=(k == 0), stop=(k == 7)),
                              r=[buf("wAv"), buf("hT4")], w=[bPF[6]], n=64)
                    S.end()
                    S.add("act", lambda e, b=b: e.activation(out=Vsn[0:16, b * 256:(b + 1) * 256], in_=PF[6][0:16, 0:256],
                                                             func=AF.Copy), w=[bPF[6], buf("Vsn")], n=256)

        half_first_idx = []
        for hi, (a, b_) in enumerate(halves):
            half_first_idx.append(len(S.ops))
            hT_need = [buf("hT%d" % t_) for t_ in range(a // 128, (b_ - 1) // 128 + 1)]
            for oc in range(9):
                pbank = PF[3 + (oc % 2)]; bp = bPF[3 + (oc % 2)]
                n = b_ - a
                S.begin("pe")
                for k in range(8):
                    S.add("pe", lambda e, oc=oc, k=k, a=a, b_=b_, n=n, pbank=pbank: e.matmul(
                        pbank[:, 0:n], wA[:, k, (oc if oc < 5 else oc + 2) * 128:((oc if oc < 5 else oc + 2) + 1) * 128],
                        hT[:, k, a:b_], start=(k == 0), stop=(k == 7)),
                        r=[buf("wAq"), buf("wAk"), buf("wAu")] + hT_need, w=[bp], n=n)
                S.end()
                npr = min(b_, GT) - a
                ncp = npr // 64
                cbase = a // 64
                if oc < 4:
                    S.add("act", lambda e, oc=oc, pbank=pbank, npr=npr, ncp=ncp, cbase=cbase: e.activation(
                        out=qv[:, cbase:cbase + ncp, oc, :], in_=pbank[:, 0:npr].rearrange("p (c q) -> p c q", q=64),
                        func=AF.Copy, scale=0.125), w=[bp, buf("qT2h%d" % hi)])
                    if has_s and hi == 1:
                        S.add("act", lambda e, oc=oc, pbank=pbank, npr=npr: e.activation(
                            out=qv[:, 8:12, oc, 0:16], in_=pbank[:, npr:npr + 64].rearrange("p (c q) -> p c q", q=16),
                            func=AF.Copy, scale=0.125), w=[bp, buf("qT2s")])
                elif oc == 4:
                    S.add("act", lambda e, pbank=pbank, npr=npr, a=a: e.activation(
                        out=kT[:, 128 + a:128 + a + npr], in_=pbank[:, 0:npr], func=AF.Copy), w=[bp, buf("kT%d" % hi)])
                    if has_s and hi == 1:
                        S.add("act", lambda e, pbank=pbank, npr=npr: e.activation(
                            out=ksv[:, :, 128:144], in_=pbank[:, npr:npr + 64].rearrange("p (b t) -> p b t", t=16),
                            func=AF.Copy), w=[bp, buf("ksT")])
                else:
                    gi = oc - 5
                    S.add("dve", lambda e, gi=gi, pbank=pbank, npr=npr, a=a: e.tensor_copy(
                        out=uv[:, gi, 16 + a:16 + a + npr], in_=pbank[:, 0:npr]), w=[bp, buf("uT")])
                    if has_s and hi == 1:
                        S.add("dve", lambda e, gi=gi, pbank=pbank, npr=npr: e.tensor_copy(
                            out=usv[:, gi, :, 16:32], in_=pbank[:, npr:npr + 64].rearrange("p (b t) -> p b t", t=16)),
                            w=[bp, buf("uTs")])
            for (i_, rows_) in tiles:
                if a <= i_ * 128 < b_:
                    v_for_tile(i_)


        out_tiles = []
        if g == NG - 1:
            out_tiles = [(3, 128, 3 * 128), (4, NS, 512)]
        for (i, rows, c0) in out_tiles:
            for (pa_, pb2, bank) in ((0, 512, 4), (512, 896, 6)):
                for k in range(8):
                    S.add("pe", lambda e, k=k, c0=c0, rows=rows, pa_=pa_, pb2=pb2, bank=bank: e.matmul(
                        PF[bank][0:rows, 0:pb2 - pa_], hT[:, k, c0:c0 + rows], wA[:, k, 512 + pa_:512 + pb2],
                        start=(k == 0), stop=(k == 7)), r=[buf("wAq"), buf("wAk"), buf("wAv"), buf("wAu"), buf("hT%d" % i)], w=[bPF[bank]])
                S.add("dve", lambda e, rows=rows, pa_=pa_, pb2=pb2, bank=bank: e.tensor_copy(
                    out=stg[0:rows, pa_:pb2], in_=PF[bank][0:rows, 0:pb2 - pa_]), w=[bPF[bank], buf("stg")])
            bs = buf("stg")
            if i == 3:
                S.add("sp", lambda e: e.dma_start(out=nk_p[:, :], in_=stg[:, 0:128]), r=[bs], dma="o_nkp", out=True)
                S.add("sp", lambda e: e.dma_start(out=nv_p[:, 0:64], in_=stg[:, 128:192]), r=[bs], dma="o_nvp", out=True)
                S.add("sp", lambda e: e.dma_start(out=nv_p[:, 64:128], in_=stg[:, 256:320]), r=[bs], dma="o_nvp", out=True)
                S.add("sp", lambda e: e.dma_start(out=npool_p[:, :], in_=stg[113:128, 384:896]), r=[bs], dma="o_npp", out=True)
            else:
                S.add("sp", lambda e: e.dma_start(out=nk_s[:, :], in_=stg[0:NS, 0:128]), r=[bs], dma="o_nks", out=True)
                S.add("sp", lambda e: e.dma_start(out=nv_s[:, 0:64], in_=stg[0:NS, 128:192]), r=[bs], dma="o_nvs", out=True)
                S.add("sp", lambda e: e.dma_start(out=nv_s[:, 64:128], in_=stg[0:NS, 256:320]), r=[bs], dma="o_nvs", out=True)
                for b in range(4):
                    S.add("sp", lambda e, b=b: e.dma_start(out=npool_s[b, :, :], in_=stg[16 * b + 1:16 * b + 16, 384:896]),
                          r=[bs], dma="o_nps", out=True)

        if has_s:
            for b in range(4):
                S.add("pool", lambda e, b=b: e.dma_start(out=ckb[:, b * 128:(b + 1) * 128], in_=ck[b, :, :]),
                      w=[buf("ckb")], dma="ckb")
                for kv in range(2):
                    for dup in range(2):
                        c0 = b * 256 + kv * 128 + dup * 64
                        S.add("pool", lambda e, b=b, kv=kv, c0=c0: e.dma_start(
                            out=Vsc[:, c0:c0 + 64], in_=cv[b, :, kv * 64:(kv + 1) * 64]), w=[buf("Vsc")], dma="Vsc")
            for b in range(4):
                S.add("pe", lambda e, b=b: e.transpose(PB5[:, b * 128:(b + 1) * 128], ckb[:, b * 128:(b + 1) * 128], ident[:]),
                      r=[buf("ckb"), buf("ident")], w=[bPB5])
            S.add("act", lambda e: e.activation(out=ksv[:, :, 0:128], in_=PB5[:, 0:512].rearrange("p (b t) -> p b t", b=4),
                                                func=AF.Copy), w=[bPB5, buf("ksT")])
            for b in range(4):
                S.add("sp", lambda e, b=b: e.dma_start(out=spx[0:15, :], in_=spool[b, :, :]), w=[buf("spx")], dma="spx")
                for gi in range(4):
                    S.add("pe", lambda e, gi=gi: e.matmul(PF[4][:, gi * 16:gi * 16 + 15], spx[0:15, gi * 128:(gi + 1) * 128],
                                                          identf[0:15, 0:15], start=True, stop=True),
                          r=[buf("spx"), buf("identf")], w=[bPF[4]])
                S.add("dve", lambda e, b=b: e.tensor_copy(
                    out=usv[:, :, b, 1:16], in_=PF[4][:, 0:64].rearrange("p (g t) -> p g t", t=16)[:, :, 0:15]),
                    w=[bPF[4], buf("uTs")])

        def pool_windows(src3, L, nb, gi, w, dst3, first16):
            pav = pa[:, 0:nb * (16 + L)].rearrange("p (b t) -> p b t", b=nb)
            pbv = pb_[:, 0:nb * (16 + L)].rearrange("p (b t) -> p b t", b=nb)
            cur = src3
            step = 1
            tmpsel = [pav, pbv]
            ti = 0
            rb_ = [buf("uT"), buf("uTs")]
            while step < w:
                nxt = tmpsel[ti]; ti ^= 1
                lo = 2 * step
                S.add("pool", lambda e, cur=cur, nxt=nxt, lo=lo, step=step, L=L: e.tensor_tensor(
                    out=nxt[:, :, lo:16 + L], in0=cur[:, :, lo:16 + L], in1=cur[:, :, lo - step:16 + L - step], op=ALU.add),
                    r=rb_ + [buf("ptmp")], w=[buf("ptmp")], n=nb * (16 + L) // 2)
                cur = nxt
                step *= 2
            tq = pbv if cur is pav else pav
            if first16:
                S.add("dve", lambda e, cur=cur, gi=gi, tq=tq: e.tensor_tensor(
                    out=tq[:, 0, 0:16], in0=cur[:, 0, 16:32], in1=invc[:, gi * 16:(gi + 1) * 16], op=ALU.mult),
                    r=[buf("ptmp"), buf("invc")], w=[buf("ptmp")], n=16)
            S.add("dve", lambda e, cur=cur, w=w, L=L: e.scalar_tensor_tensor(
                out=dst3[:, :, 0:L], in0=cur[:, :, 16:16 + L], scalar=1.0 / w, in1=src3[:, :, 16:16 + L],
                op0=ALU.mult, op1=ALU.subtract), r=rb_ + [buf("ptmp")], w=[buf("dT")], n=nb * L)
            if first16:
                S.add("dve", lambda e, tq=tq: e.tensor_tensor(
                    out=dst3[:, 0, 0:16], in0=tq[:, 0, 0:16], in1=src3[:, 0, 16:32], op=ALU.subtract),
                    r=[buf("ptmp"), buf("uT")], w=[buf("dT")], n=16)

        dv = dT.rearrange("p (g t) -> p g t", g=4)
        pool_first_idx = len(S.ops)
        for gi, w in enumerate(POOL_W):
            pool_windows(uv[:, gi:gi + 1, :], 512, 1, gi, w, dv[:, gi:gi + 1, 0:512], first16=(g == 0))
            if has_s:
                pool_windows(usv[:, gi, :, :], 16, 4, gi, w,
                             dv[:, gi, 512:576].rearrange("p (b t) -> p b t", b=4), first16=False)
        S.add("dve", lambda e: e.tensor_copy(out=ucarry[:], in_=uv[:, :, 512:528]), r=[buf("uT")], w=[buf("ucarry")], n=64)
        plv = plT.rearrange("p (g t) -> p g t", g=4)
        for gi in range(4):
            for hi, (a, b_) in enumerate(halves):
                n = b_ - a
                S.add("pe", lambda e, gi=gi, a=a, b_=b_, n=n, hi=hi: e.matmul(
                    PF[5 + hi][:, 0:n], wpl[:, gi, :], dv[:, gi, a:b_], start=True, stop=True),
                    r=[buf("wpl"), buf("dT")], w=[bPF[5 + hi]], n=n)
                S.add("act", lambda e, gi=gi, a=a, b_=b_, n=n, hi=hi: e.activation(
                    out=plv[:, gi, a:b_], in_=PF[5 + hi][:, 0:n], func=AF.Copy, scale=pscol[:, gi:gi + 1]),
                    r=[buf("pscol")], w=[bPF[5 + hi], buf("plT")], n=n)

        pool_last_idx = len(S.ops)
        av = aT.rearrange("p (j t) -> p j t", j=4)
        Sbv = [x.rearrange("p (a t) -> p a t", a=2) for x in Sb]
        Pfv = [x.rearrange("p (a t) -> p a t", a=2) for x in Pf]
        Pnv = [x.rearrange("p (a t) -> p a t", a=2) for x in Pn]
        PnTv = [x.rearrange("p (b a t) -> p b a t", b=2, a=2) for x in PnT]
        for kv in range(2):
            S.add("dve", lambda e, kv=kv: e.tensor_copy(out=Sbv[kv][:, :, 0:1],
                                                        in_=sinkc[:, 2 * kv:2 * kv + 2].rearrange("p (a o) -> p a o", o=1)),
                  r=[buf("sinkc")], w=[buf("Sb%d" % kv)], n=2)
        st_i = 0

        def s_pair(lhs_fn, rhs_ap, nkeys, bias_lo, kv, blocks, out_cols, nq, rbufs):
            nonlocal st_i
            sl = st_i % 2; st_i += 1
            sbank = 1 + sl; tbank = (0, 7)[sl]
            psS = PF[sbank][:, 0:384].rearrange("p (a t) -> p a t", a=2); bpsS = bPF[sbank]
            psT = PSB[tbank][:, 0:512].rearrange("p (b a t) -> p b a t", b=2, a=2); bpsT = bPF[tbank]
            psO = PF[tbank][:, 256:512].rearrange("p (a t) -> p a t", a=2); bpsO = bPF[tbank]
            kvp = slice(kv * 64, kv * 64 + 64)
            S.begin("pe")
            for pp in range(2):
                S.add("pe", lambda e, pp=pp: e.matmul(psS[:, pp, 0:nkeys], lhs_fn(kvp, pp), rhs_ap(kvp), start=True, stop=True),
                      r=rbufs, w=[bpsS], n=nkeys)
            S.end()
            bSb = buf("Sb%d" % kv); bPf = buf("Pf%d" % sl); bPn = buf("Pn%d" % sl); bPnT = buf("PnT%d" % sl)
            S.add("dve", lambda e: e.tensor_tensor(out=Sbv[kv][:, :, 1:1 + nkeys], in0=psS[:, :, 0:nkeys],
                                                   in1=bias[:, 2 * kv:2 * kv + 2, bias_lo:bias_lo + nkeys], op=ALU.add),
                  r=[buf("bias")], w=[bpsS, bSb], n=2 * nkeys)
            c0, bstp = new_stat_group()
            negm = stats[:, c0:c0 + 2]; rsum = stats[:, c0 + 2:c0 + 4]; rr = stats[:, c0 + 4:c0 + 6]
            bng, brsum, brr = bstp, bstp, bstp
            S.add("dve", lambda e: e.reduce_max(out=negm, in_=Sbv[kv][:, :, 0:1 + nkeys], axis=AX.X, negate=True),
                  r=[bSb], w=[bng], n=2 * nkeys)
            for pp in range(2):
                S.add("act", lambda e, pp=pp: e.activation(out=Pfv[sl][:, pp, 0:1 + nkeys], in_=Sbv[kv][:, pp, 0:1 + nkeys],
                                                           func=AF.Exp, bias=negm[:, pp:pp + 1], scale=1.0,
                                                           accum_out=rsum[:, pp:pp + 1]),
                      r=[bSb, bng], w=[bPf, brsum], n=nkeys)
            S.add("dve", lambda e: e.reciprocal(out=rr, in_=rsum), r=[brsum], w=[brr], n=2)
            S.add("pool", lambda e: e.tensor_tensor(out=Pnv[sl][:, :, 0:nkeys], in0=Pfv[sl][:, :, 1:1 + nkeys],
                                                    in1=rr.unsqueeze(2).to_broadcast([128, 2, nkeys]), op=ALU.mult),
                  r=[bPf, brr], w=[bPn], n=nkeys)
            S.begin("pe")
            off = 0
            for bi, (nk_, v_ap, vb) in enumerate(blocks):
                for pp in range(2):
                    S.add("pe", lambda e, bi=bi, pp=pp, off=off, nk_=nk_: e.transpose(
                        psT[0:nk_, bi, pp, :], Pnv[sl][:, pp, off:off + nk_], ident[:]),
                        r=[bPn, buf("ident")], w=[bpsT], n=128)
                off += nk_
            S.end()
            for bi, (nk_, v_ap, vb) in enumerate(blocks):
                if bi == 0:
                    S.add("act", lambda e, bi=bi, nk_=nk_: e.activation(out=PnTv[sl][0:nk_, bi, :, :], in_=psT[0:nk_, bi, :, :],
                                                                        func=AF.Copy), w=[bpsT, bPnT], n=256)
                else:
                    S.add("dve", lambda e, bi=bi, nk_=nk_: e.tensor_copy(out=PnTv[sl][0:nk_, bi, :, :], in_=psT[0:nk_, bi, :, :]),
                          w=[bpsT, bPnT], n=200)
            S.begin("pe")
            for pp in range(2):
                for bi, (nk_, v_ap, vb) in enumerate(blocks):
                    S.add("pe", lambda e, bi=bi, pp=pp, nk_=nk_, v_ap=v_ap: e.matmul(
                        psO[:, pp, :], v_ap, PnTv[sl][0:nk_, bi, pp, :],
                        start=(bi == 0), stop=(bi == len(blocks) - 1)), r=[bPnT] + vb, w=[bpsO], n=128)
            S.end()
            for hh in range(2):
                eng = "act" if hh == 0 else "dve"
                if eng == "act":
                    S.add("act", lambda e, hh=hh: e.activation(
                        out=av[hh * 64:(hh + 1) * 64, 2 * kv:2 * kv + 2, out_cols:out_cols + nq],
                        in_=psO[hh * 64:(hh + 1) * 64, :, hh * 64:hh * 64 + nq], func=AF.Copy), w=[bpsO, buf("aT%d" % (out_cols // 128))], n=2 * nq)
                else:
                    S.add("dve", lambda e, hh=hh: e.tensor_copy(
                        out=av[hh * 64:(hh + 1) * 64, 2 * kv:2 * kv + 2, out_cols:out_cols + nq],
                        in_=psO[hh * 64:(hh + 1) * 64, :, hh * 64:hh * 64 + nq]), w=[bpsO, buf("aT%d" % (out_cols // 128))], n=2 * nq)

        early_att_first = len(S.ops)
        early_att_last = early_att_first
        for c in range(8):
            gc = 8 * g + c
            t = c // 2
            if c * 64 == halves[0][1] or (c == 4 and halves[0][1] > 256):
                pass
            if (c + 1) * 64 <= 256:
                early_att_last = None
            elif early_att_last is None:
                early_att_last = len(S.ops)
            for kv in range(2):
                vs = slice(kv * 128, (kv + 1) * 128)
                if gc == 0:
                    kc0, nkeys, blo = 128, 64, 128
                    blocks = [(64, VdA[0:64, 1, vs], [buf("VdA1")])]
                elif gc == 1:
                    kc0, nkeys, blo = 128, 128, 64
                    blocks = [(128, VdA[:, 1, vs], [buf("VdA1")])]
                else:
                    kc0, nkeys, blo = 128 + (c - 2) * 64, 192, 0
                    if c % 2 == 0:
                        blocks = [(128, VdA[:, t, vs], [buf("VdA%d" % t)]),
                                  (64, VdA[0:64, t + 1, vs], [buf("VdA%d" % (t + 1))])]
                    else:
                        blocks = [(128, VdB[:, t, vs], [buf("VdBlo%d" % t), buf("VdBhi%d" % t)]),
                                  (64, VdB[0:64, t + 1, vs], [buf("VdBlo%d" % (t + 1))])]
                lhs_fn = (lambda kvp, pp, c=c: qT2[kvp, (c * 4 + 2 * pp) * 64:(c * 4 + 2 * pp + 2) * 64])
                rhs_fn = (lambda kvp, kc0=kc0, nkeys=nkeys: kT[kvp, kc0:kc0 + nkeys])
                hq = 0 if c * 64 < halves[0][1] else 1
                kb = set()
                for cc_ in range(max(c - 2, -2), c + 1):
                    kb.add("kTc" if cc_ < 0 else ("kT0" if cc_ * 64 < halves[0][1] else "kT1"))
                s_pair(lhs_fn, rhs_fn, nkeys, blo, kv, blocks, c * 64, 64, [buf("qT2h%d" % hq)] + [buf(x) for x in sorted(kb)])
        if has_s:
            for b in range(4):
                for kv in range(2):
                    vs = slice(b * 256 + kv * 128, b * 256 + (kv + 1) * 128)
                    blocks = [(128, Vsc[:, vs], [buf("Vsc")]), (16, Vsn[0:16, vs], [buf("Vsn")])]
                    lhs_fn = (lambda kvp, pp, b=b: qT2[kvp, ((8 + b) * 4 + 2 * pp) * 64:((8 + b) * 4 + 2 * pp + 2) * 64])
                    rhs_fn = (lambda kvp, b=b: ksT[kvp, b * 144:(b + 1) * 144])
                    s_pair(lhs_fn, rhs_fn, 144, 0, kv, blocks, 512 + 16 * b, 16, [buf("qT2s"), buf("ksT")])

        if early_att_last is not None and len(half_first_idx) > 1:
            n_e = max(early_att_last - early_att_first, 1)
            for j, o in enumerate(S.ops[early_att_first:early_att_last]):
                o.prio = half_first_idx[1] - 0.5 + 0.4 * j / n_e
        att_last_idx = len(S.ops)
        pool_ops = S.ops[pool_first_idx:pool_last_idx]
        for j, o in enumerate(pool_ops):
            o.prio = pool_last_idx + (j + 1) * (att_last_idx - pool_last_idx) * 0.4 / (len(pool_ops) + 1)
        if g < NG - 1:
            S.add("act", lambda e: e.activation(out=kT[:, 0:128], in_=kT[:, 512:640], func=AF.Copy),
                  r=[buf("kT1")], w=[buf("kTc")])
            S.add("act", lambda e: e.activation(out=VdA[:, 0, :], in_=VdA[:, 4, :], func=AF.Copy),
                  r=[buf("VdA4")], w=[buf("VdA0")])

        def post_norm_residual(i, rows, banks, gidx, tf, btf, jk, bjk):
            c, bms = new_stat_group()
            ms = stats[:, c:c + 1]; ms2 = stats[:, c + 1:c + 2]
            S.add("act", lambda e: e.activation(out=jk[0:rows, 0:512], in_=PF[banks[0]][0:rows, :], func=AF.Square,
                                                scale=1.0 / 32.0, accum_out=ms[0:rows, :]), w=[bPF[banks[0]], bjk, bms], n=512)
            S.add("act", lambda e: e.activation(out=jk[0:rows, 512:1024], in_=PF[banks[1]][0:rows, :], func=AF.Square,
                                                scale=1.0 / 32.0, accum_out=ms2[0:rows, :]), w=[bPF[banks[1]], bjk, bms], n=512)
            S.add("dve", lambda e: e.tensor_tensor(out=ms[0:rows, :], in0=ms[0:rows, :], in1=ms2[0:rows, :], op=ALU.add),
                  r=[bms], w=[bms], n=1)
            rstd, brs = rstd_from_ms(c, rows, bms)
            for hf in range(2):
                S.add("dve", lambda e, hf=hf: e.scalar_tensor_tensor(
                    out=tf[0:rows, hf * 512:(hf + 1) * 512], in0=PF[banks[hf]][0:rows, :], scalar=rstd[0:rows, :],
                    in1=gb[gidx][0:rows, hf * 512:(hf + 1) * 512], op0=ALU.mult, op1=ALU.mult),
                    r=[brs, buf("gb%d" % gidx)], w=[bPF[banks[hf]], btf], n=512)
            S.add("pool", lambda e, xi=xsl(i): e.tensor_tensor(out=xs[xi][0:rows, :], in0=xs[xi][0:rows, :], in1=tf[0:rows, :], op=ALU.add),
                  r=[btf], w=[buf("xs%d" % xsl(i))], n=1024)

        for ti, (i, rows) in enumerate(tiles):
            c0 = i * 128
            mb = (5, 6) if ti % 2 == 0 else (3, 4)
            S.begin("pe")
            for hf in range(2):
                for j in range(8):
                    src = av if j < 4 else plv
                    S.add("pe", lambda e, j=j, hf=hf, c0=c0, rows=rows, src=src, mb=mb: e.matmul(
                        PF[mb[hf]][0:rows, :], src[:, j % 4, c0:c0 + rows], wo[:, j, hf * 512:(hf + 1) * 512],
                        start=(j == 0), stop=(j == 7)), r=[buf("wo"), buf("aT%d" % i), buf("plT")], w=[bPF[mb[hf]]], n=512)
            S.end()
            post_norm_residual(i, rows, mb, 1, tmpf[ti % 2], buf("tmpf%d" % (ti % 2)), junks[ti % 2], buf("junk%d" % (ti % 2)))

        for (i, rows) in tiles:
            bx = buf("xs%d" % xsl(i))
            rstd, brs = rms_stats(xs[xsl(i)][0:rows, :], rows, [bx])
            hbuf = hb[hw % 2]; bh = buf("hb%d" % (hw % 2)); hw += 1
            S.add("dve", lambda e, i=i, rows=rows, rstd=rstd, hbuf=hbuf, xi=xsl(i): e.tensor_scalar(
                out=hbuf[0:rows, :], in0=xs[xi][0:rows, :], scalar1=rstd[0:rows, :], scalar2=None, op0=ALU.mult),
                r=[bx, brs], w=[bh], n=512)
            tb = (0, 7)[i % 2]
            transposes(hbuf, rows, [bh], None, None, 8, psum=PSB[tb], bps=bPF[tb])
            c0 = i * 128
            S.add("dve", lambda e, c0=c0, rows=rows, tb=tb: e.tensor_tensor(
                out=hT[:, :, c0:c0 + rows], in0=PSB[tb][:].rearrange("p (k t) -> p k t", k=8)[:, :, 0:rows],
                in1=gcol[2][:].unsqueeze(2).to_broadcast([128, 8, rows]), op=ALU.mult),
                r=[buf("gcol2")], w=[bPF[tb], buf("hT%d" % i)], n=8 * rows)

        if g == 0:
            wload(wpg[:], w_pg.rearrange("(k p) c -> p k c", p=128), "wpg", "wpg")
            wload(wpe[:], w_ple.rearrange("(k p) c -> p k c", p=128), "wpe", "wpe")
        if g + 1 < NG:
            issue_x_loads(g + 1, only=(0, 1, 2))
        def p2a_front(c):
            blk = c // 2
            cc = c % 2
            rg, ru = ring[blk % 2]
            for hi, (a, b_) in enumerate([(0, 512)] + ([(512, 576)] if has_s else [])):
                n = b_ - a
                gbank = (1, 2)[hi] if c % 2 == 0 else (5, 6)[hi]
                ubank = (3, 4)[hi] if c % 2 == 0 else (0, 7)[hi]
                if hi == 0 and c < 4:
                    cblocks = [(0, 384), (384, 512)]
                else:
                    cblocks = [(a, b_)]
                for (wt, wbuf, bank) in ((rg, "rg%d" % (blk % 2), gbank), (ru, "ru%d" % (blk % 2), ubank)):
                    for (ca, cb) in cblocks:
                        need = [buf("hT%d" % t_) for t_ in range(ca // 128, min((cb - 1) // 128, 4) + 1)]
                        S.begin("pe")
                        for k in range(8):
                            S.add("pe", lambda e, k=k, ca=ca, cb=cb, a=a, wt=wt, cc=cc, bank=bank: e.matmul(
                                PF[bank][:, ca - a:cb - a], wt[:, k, cc * 128:(cc + 1) * 128], hT[:, k, ca:cb],
                                start=(k == 0), stop=(k == 7)), r=[buf(wbuf)] + need, w=[bPF[bank]], n=cb - ca)
                        S.end()
                sgi = hi
                S.add("act", lambda e, n=n, gbank=gbank, sgi=sgi: e.activation(out=sg[sgi][:, 0:n], in_=PF[gbank][:, 0:n],
                                                                               func=AF.Silu), w=[bPF[gbank], buf("sg%d" % sgi)], n=n)

        def p2a_back(c):
            for hi, (a, b_) in enumerate([(0, 512)] + ([(512, 576)] if has_s else [])):
                n = b_ - a
                ubank = (3, 4)[hi] if c % 2 == 0 else (0, 7)[hi]
                sgi = hi
                S.add("dve", lambda e, c=c, a=a, b_=b_, n=n, ubank=ubank, sgi=sgi: e.tensor_tensor(
                    out=actT[:, c, a:b_], in0=PF[ubank][:, 0:n], in1=sg[sgi][:, 0:n], op=ALU.mult),
                    r=[buf("sg%d" % sgi)], w=[bPF[ubank], buf("actT%d" % c)], n=n)

        pre = 0
        if stage >= 2 and g > 0:
            for c in range(1):
                p2a_front(c)
            pre = 1
        S.barrier()

        if stage >= 2:
            if g == 0:
                ring_load(0)
                ring_load(1)
            for c in range(pre):
                p2a_back(c)
            for c in range(pre, NFC):
                blk = c // 2
                cc = c % 2
                p2a_front(c)
                p2a_back(c)
                if cc == 1:
                    if blk + 2 < NFC // 2:
                        ring_load(blk + 2)
                    wd_load(2 * blk)
                    wd_load(2 * blk + 1)

            def p_load(i, rows):
                pr = i % 2
                psrc = p_p[(gtile0 + i) * 128:(gtile0 + i + 1) * 128, :] if i < 4 else p_s[:, :]
                S.add("pool", lambda e, rows=rows, psrc=psrc, pr=pr: e.dma_start(out=pbf_[pr][0:rows, :], in_=psrc),
                      w=[buf("pbf%d" % pr)], dma="pbf%d" % pr, n=rows * 1024)

            if stage >= 3:
                p_load(0, 128)
                p_load(1, 128)
            for ti, (i, rows) in enumerate(tiles):
                c0 = i * 128
                banks = (1, 2) if ti % 2 == 0 else (5, 6)
                S.begin("pe")
                for hf in range(2):
                    for c in range(NFC):
                        S.add("pe", lambda e, c=c, hf=hf, c0=c0, rows=rows, banks=banks: e.matmul(
                            PF[banks[hf]][0:rows, :], actT[:, c, c0:c0 + rows], wd[:, c, hf * 512:(hf + 1) * 512],
                            start=(c == 0), stop=(c == NFC - 1)), r=[buf("actT%d" % c), buf("wd%d" % c)], w=[bPF[banks[hf]]], n=512)
                S.end()
                post_norm_residual(i, rows, banks, 3, tmpfb[ti % 2], buf(("rg0", "ru0")[ti % 2]), junkb, buf("rg1"))

        p2b_done = S.snapshot()

        for (i, rows) in tiles:
            bx = buf("xs%d" % xsl(i))
            if stage >= 3:
                pr = i % 2
                x2b, x2T, pbf, pT, tf = x2b_[pr], x2T_[pr], pbf_[pr], pT_[pr], tmpf3_[pr]
                bx2b, bx2T, bpbf, bpT, btf = (buf("rg1"), buf("ru1"), buf("pbf%d" % pr), buf("pT%d" % pr),
                                              buf(("rg0", "ru0")[pr]))
                tbx, tbp = ((0, 7) if pr == 0 else (7, 0))
                if i >= 2:
                    p_load(i, rows)
                S.add("act", lambda e, i=i, rows=rows, x2b=x2b, xi=xsl(i): e.activation(out=x2b[0:rows, :], in_=xs[xi][0:rows, :], func=AF.Copy),
                      r=[bx], w=[bx2b], n=1024)
                transposes(x2b, rows, [bx2b], None, None, 8, psum=PSB[tbx], bps=bPF[tbx])
                S.add("act", lambda e, rows=rows, x2T=x2T, tbx=tbx: e.activation(
                    out=x2T.rearrange("p (k t) -> p k t", k=8)[:, :, 0:rows],
                    in_=PSB[tbx][:].rearrange("p (k t) -> p k t", k=8)[:, :, 0:rows], func=AF.Copy), w=[bPF[tbx], bx2T], n=8 * rows)
                transposes(pbf, rows, [bpbf], None, None, 2, psum=PSB[tbp], bps=bPF[tbp])
                S.add("dve", lambda e, rows=rows, pT=pT, tbp=tbp: e.tensor_copy(
                    out=pT.rearrange("p (k t) -> p k t", k=2)[:, :, 0:rows],
                    in_=PSB[tbp][:, 0:256].rearrange("p (k t) -> p k t", k=2)[:, :, 0:rows]), w=[bPF[tbp], bpT], n=2 * rows)
                x2Tv = x2T.rearrange("p (k t) -> p k t", k=8)
                pTv = pT.rearrange("p (k t) -> p k t", k=2)
                gbk = (1, 2) if pr == 0 else (5, 6)
                for hf in range(2):
                    S.begin("pe")
                    for k in range(8):
                        S.add("pe", lambda e, k=k, hf=hf, rows=rows, gbk=gbk, x2Tv=x2Tv: e.matmul(
                            PF[gbk[hf]][0:rows, :], x2Tv[:, k, 0:rows], wpg[:, k, hf * 512:(hf + 1) * 512],
                            start=(k == 0), stop=(k == 7)), r=[bx2T, buf("wpg")], w=[bPF[gbk[hf]]], n=512)
                    S.end()
                    S.begin("pe")
                    for k in range(2):
                        S.add("pe", lambda e, k=k, hf=hf, rows=rows, pTv=pTv: e.matmul(
                            PF[3 + hf][0:rows, :], pTv[:, k, 0:rows], wpe[:, k, hf * 512:(hf + 1) * 512],
                            start=(k == 0), stop=(k == 1)), r=[bpT, buf("wpe")], w=[bPF[3 + hf]], n=512)
                    S.end()
                for hf in range(2):
                    S.add("act", lambda e, hf=hf, rows=rows, gbk=gbk, tf=tf: e.activation(
                        out=tf[0:rows, hf * 512:(hf + 1) * 512], in_=PF[gbk[hf]][0:rows, :], func=AF.Sigmoid),
                        w=[bPF[gbk[hf]], btf], n=512)
                    S.add("dve", lambda e, hf=hf, rows=rows, tf=tf: e.tensor_tensor(
                        out=tf[0:rows, hf * 512:(hf + 1) * 512], in0=PF[3 + hf][0:rows, :], in1=tf[0:rows, hf * 512:(hf + 1) * 512],
                        op=ALU.mult), r=[btf], w=[bPF[3 + hf], btf], n=512)
                S.add("pool", lambda e, i=i, rows=rows, tf=tf, xi=xsl(i): e.tensor_tensor(out=xs[xi][0:rows, :], in0=xs[xi][0:rows, :],
                                                                               in1=tf[0:rows, :], op=ALU.add),
                      r=[btf], w=[bx], n=1024)
            dst = y_p[(gtile0 + i) * 128:(gtile0 + i + 1) * 128, :] if i < 4 else y_s[:, :]
            S.add("sp", lambda e, i=i, rows=rows, dst=dst, xi=xsl(i): e.dma_start(out=dst, in_=xs[xi][0:rows, :]),
                  r=[bx], dma="o_xs%d" % xsl(i), out=True)

        S.barrier(prior=p2b_done)
        if g + 1 < NG:
            issue_x_loads(g + 1, only=(3, 4))

    S.emit(nc)
    return nc


_PROG = None


def kernel(x_prompt, x_sample, cache_k, cache_v, state_pool, p_prompt, p_sample, rel_bias_table,
           g_mix_pre, w_in, attn_sinks, w_pool, pool_scale, w_out, g_mix_post, g_ffn_pre,
           w_ffn_gate, w_ffn_up, w_ffn_down, g_ffn_post, w_ple, w_ple_gate):
    global _PROG
    f = lambda a: np.ascontiguousarray(np.asarray(a, dtype=np.float32))
    x_prompt, x_sample = f(x_prompt), f(x_sample)
    consts = _consts()
    shared = {
        "table": f(rel_bias_table), "g_mix_pre": f(g_mix_pre), "g_mix_post": f(g_mix_post),
        "g_ffn_pre": f(g_ffn_pre), "g_ffn_post": f(g_ffn_post), "w_in": f(w_in)[0], "sinks": f(attn_sinks),
        "w_pool": f(w_pool)[0], "pool_scale": f(pool_scale).reshape(4, 128), "w_out": f(w_out)[0],
        "w_gate": f(w_ffn_gate)[0], "w_up": f(w_ffn_up)[0], "w_down": f(w_ffn_down)[0],
        "w_ple": f(w_ple)[0], "w_pg": f(w_ple_gate)[0],
    }
    shared.update(consts)
    ck, cv, sp = f(cache_k)[0], f(cache_v)[0], f(state_pool)[0]
    pp, ps = f(p_prompt)[0], f(p_sample)[0]
    in_maps = []
    for c in range(8):
        m = dict(shared)
        m["x_p"] = x_prompt[c]
        m["x_s"] = x_sample[4 * c:4 * c + 4].reshape(NS, D)
        m["ck"] = ck[4 * c:4 * c + 4].reshape(4, 128, 128)
        m["cv"] = cv[4 * c:4 * c + 4].reshape(4, 128, 128)
        m["spool"] = sp[4 * c:4 * c + 4]
        m["p_p"] = pp[c]
        m["p_s"] = ps[4 * c:4 * c + 4].reshape(NS, PLE)
        in_maps.append({k: np.ascontiguousarray(v) for k, v in m.items()})
    if _PROG is None:
        _PROG = build_program()
    res = run_bass_kernel_spmd(_PROG, in_maps, core_ids=list(range(8)))
    R = res.results
    y_prompt = np.stack([R[c]["y_p"] for c in range(8)]).astype(np.float32)
    y_sample = np.concatenate([R[c]["y_s"].reshape(4, 16, D) for c in range(8)]).astype(np.float32)
    nkp = np.stack([R[c]["nk_p"].reshape(128, 2, 64) for c in range(8)])[None].astype(np.float32)
    nvp = np.stack([R[c]["nv_p"].reshape(128, 2, 64) for c in range(8)])[None].astype(np.float32)
    npp = np.stack([R[c]["npool_p"] for c in range(8)])[None].astype(np.float32)
    nks = np.concatenate([R[c]["nk_s"].reshape(4, 16, 2, 64) for c in range(8)])[None].astype(np.float32)
    nvs = np.concatenate([R[c]["nv_s"].reshape(4, 16, 2, 64) for c in range(8)])[None].astype(np.float32)
    nps = np.concatenate([R[c]["npool_s"] for c in range(8)])[None].astype(np.float32)
    return (y_prompt, y_sample, nkp, nvp, npp, nks, nvs, nps)
```
